# Optimizing a Trainium2 kernel written in Bass

```python
import math
import jax, jax.numpy as jnp
from jax import lax
import numpy as np


D_MODEL = 1024
BATCH = 32
SEQ = 2048
DEPTH = 2

GRID_W = 64
CTX_LEN = 256
N_EVEN = (DEPTH + 1) // 2
N_ODD = DEPTH // 2
EPS = 1e-6

HY_W = 512
HY_EMB = 33
HY_BANDS = (HY_EMB - 1) // 2
HY_ORDER = 64
HY_DECAY_TARGET = 1e-2
HY_FAST_PCT = 0.3
HY_SLOW_PCT = 1.5
HY_FILTER_GAIN = 0.05
RET_W = 512
RET_H = 4
RET_HD = RET_W // RET_H
CHUNK = 128
EVEN_IN = 4 * HY_W + 4 * RET_W
EVEN_OUT = HY_W + RET_W
N_HEADS = 8
N_KV = 2
HEAD_DIM = 128
GROUP = N_HEADS // N_KV
ATT_W = N_HEADS * HEAD_DIM
KV_W = N_KV * HEAD_DIM
ODD_IN = 2 * ATT_W + 2 * KV_W
Q_BLOCK = 128
ROPE_THETA = 10000.0

kernel_name = 'hybrid_hyena_retention_gqa_dit_prefix'

F32 = jnp.float32


def rmsnorm(x, g):
    xf = x.astype(F32)
    y = xf * lax.rsqrt(jnp.mean(xf * xf, axis=-1, keepdims=True) + EPS)
    return (y * g.astype(F32)).astype(x.dtype)


def ada_params(cond, w, b):
    m = jax.nn.silu(cond) @ w + b
    shift, scale, gate = jnp.split(m, 3, axis=-1)
    return shift[..., None, :], scale[..., None, :], gate[..., None, :]


def short_conv(u, w, b):
    up = jnp.pad(u, ((0, 0), (1, 1), (0, 0)))
    return up[:, :-2] * w[0] + up[:, 1:-1] * w[1] + up[:, 2:] * w[2] + b


def hyena_filters(L, w1, b1, freq, w2, b2, w3):
    n = jnp.arange(L, dtype=F32)
    t = n / max(L - 1, 1)
    bands = jnp.linspace(1e-4, HY_BANDS - 1, HY_BANDS, dtype=F32)
    ang = (2.0 * math.pi / L) * n[:, None] * bands[None, :]
    z = jnp.concatenate([t[:, None], jnp.cos(ang), -jnp.sin(ang)], axis=-1)
    f = freq.astype(F32)
    h = jnp.sin(f * (z @ w1.astype(F32) + b1.astype(F32)))
    h = jnp.sin(f * (h @ w2.astype(F32) + b2.astype(F32)))
    h = h @ w3.astype(F32)
    max_decay = math.log(HY_DECAY_TARGET) / HY_FAST_PCT
    min_decay = math.log(HY_DECAY_TARGET) / HY_SLOW_PCT
    deltas = jnp.abs(jnp.linspace(min_decay, max_decay, HY_W, dtype=F32))
    window = jnp.exp(-t[:, None] * deltas[None, :])
    h = h.reshape(L, 2, HY_W) * window[:, None, :]
    return h[:, 0], h[:, 1]


def long_conv_bidir(u, h_fwd, h_bwd, bias):
    L = u.shape[1]
    uf = u.astype(F32)
    k = jnp.concatenate([h_fwd, jnp.zeros_like(h_fwd[:1]), h_bwd[:0:-1]], axis=0)
    spec = jnp.fft.rfft(uf, n=2 * L, axis=1) * jnp.fft.rfft(k, n=2 * L, axis=0)[None]
    y = jnp.fft.irfft(spec, n=2 * L, axis=1)[:, :L]
    return (y + uf * bias.astype(F32)).astype(u.dtype)


def hyena_mix(z, conv_w, conv_b, filt, bias):
    z = short_conv(z, conv_w, conv_b)
    x0, x1, v = jnp.split(z, 3, axis=-1)
    h_f, h_b = hyena_filters(z.shape[1], *filt)
    return x0 * long_conv_bidir(x1 * v, h_f, h_b, bias)


def retention_chunks(q, k, v, log_g, state):
    B, H, L, d = q.shape
    nc = L // CHUNK
    idx = jnp.arange(CHUNK, dtype=F32)
    diff = idx[:, None] - idx[None, :]
    decay_in = jnp.where(diff >= 0, jnp.exp(log_g[:, None, None] * jnp.maximum(diff, 0.0)), 0.0).astype(q.dtype)
    q_decay = jnp.exp(log_g[:, None] * (idx + 1.0))[None, :, :, None].astype(q.dtype)
    k_decay = jnp.exp(log_g[:, None] * (CHUNK - 1.0 - idx))[None, :, :, None].astype(q.dtype)
    chunk_decay = jnp.exp(log_g * CHUNK)[None, :, None, None].astype(q.dtype)

    def to_chunks(a):
        return a.reshape(B, H, nc, CHUNK, d).transpose(2, 0, 1, 3, 4)

    def step(s, blk):
        qb, kb, vb = blk
        scores = jnp.einsum('bhid,bhjd->bhij', qb, kb) * decay_in
        o = jnp.einsum('bhij,bhjd->bhid', scores, vb) + jnp.einsum('bhid,bhde->bhie', qb, s) * q_decay
        s = s * chunk_decay + jnp.einsum('bhjd,bhje->bhde', kb * k_decay, vb)
        return s, o

    s, o = lax.scan(step, state, (to_chunks(q), to_chunks(k), to_chunks(v)))
    return o.transpose(1, 2, 0, 3, 4).reshape(B, H, L, d), s


def retention_bidir(q, k, v, log_f, log_b, s_f, s_b):
    o_f, s_f = retention_chunks(q, k, v, log_f, s_f)
    fl = lambda a: jnp.flip(a, axis=2)
    o_b, s_b = retention_chunks(fl(q), fl(k), fl(v), log_b, s_b)
    return o_f + fl(o_b), s_f, s_b


def head_norm(o, g):
    of = o.astype(F32)
    mu = jnp.mean(of, axis=-1, keepdims=True)
    var = jnp.mean((of - mu) ** 2, axis=-1, keepdims=True)
    y = (of - mu) * lax.rsqrt(var + EPS)
    B, H, L, d = o.shape
    y = y.transpose(0, 2, 1, 3).reshape(B, L, H * d) * g.astype(F32)
    return y.astype(o.dtype)


def even_mixer(hx, hc, in_w, out_w, conv_w, conv_b, filt, hy_bias, a_f, a_b, ret_g, ctx_out):
    cuts = [3 * HY_W, 4 * HY_W, 4 * HY_W + RET_W, 4 * HY_W + 2 * RET_W, 4 * HY_W + 3 * RET_W]
    px = jnp.split(hx @ in_w, cuts, axis=-1)
    pc = jnp.split(hc @ in_w, cuts, axis=-1)
    B = hx.shape[0]
    heads = lambda a: a.reshape(B, a.shape[1], RET_H, RET_HD).transpose(0, 2, 1, 3)
    k_scale = RET_HD ** -0.5
    log_f = -jnp.exp(a_f.astype(F32))
    log_b = -jnp.exp(a_b.astype(F32))
    zero = jnp.zeros((B, RET_H, RET_HD, RET_HD), hx.dtype)
    o_c, s_f, s_b = retention_bidir(heads(pc[2]), heads(pc[3]) * k_scale, heads(pc[4]), log_f, log_b, zero, zero)
    o_x, _, _ = retention_bidir(heads(px[2]), heads(px[3]) * k_scale, heads(px[4]), log_f, log_b, s_f, s_b)

    def merge(p, o_ret):
        y_hy = hyena_mix(p[0], conv_w, conv_b, filt, hy_bias) * jax.nn.silu(p[1])
        y_ret = head_norm(o_ret, ret_g) * jax.nn.silu(p[5])
        return jnp.concatenate([y_hy, y_ret], axis=-1) @ out_w

    yx = merge(px, o_x)
    yc = merge(pc, o_c) if ctx_out else None
    return yx, yc


def grid_angles(rows):
    r = jnp.repeat(jnp.arange(rows, dtype=F32), GRID_W)
    col = jnp.tile(jnp.arange(GRID_W, dtype=F32), rows)
    half = HEAD_DIM // 2
    inv = ROPE_THETA ** (-jnp.arange(0, half, 2, dtype=F32) / half)
    return jnp.concatenate([r[:, None] * inv, col[:, None] * inv], axis=-1)


def rope_2d(x, ang):
    cos = jnp.cos(ang)[:, None, :].astype(x.dtype)
    sin = jnp.sin(ang)[:, None, :].astype(x.dtype)
    xr = x.reshape(*x.shape[:-1], HEAD_DIM // 2, 2)
    x0, x1 = xr[..., 0], xr[..., 1]
    return jnp.stack([x0 * cos - x1 * sin, x0 * sin + x1 * cos], axis=-1).reshape(x.shape)


def gqa(q, k, v):
    B, Lq = q.shape[:2]
    qg = q.reshape(B, Lq, N_KV, GROUP, HEAD_DIM)
    s = jnp.einsum('bqkgd,bskd->bkgqs', qg, k).astype(F32) * (HEAD_DIM ** -0.5)
    p = jax.nn.softmax(s, axis=-1).astype(v.dtype)
    o = jnp.einsum('bkgqs,bskd->bqkgd', p, v)
    return o.reshape(B, Lq, ATT_W)


def blocked_gqa(q, k, v):
    B, L = q.shape[:2]
    nb = L // Q_BLOCK
    qb = q.reshape(B, nb, Q_BLOCK, N_HEADS, HEAD_DIM).swapaxes(0, 1)
    o = lax.map(lambda qi: gqa(qi, k, v), qb)
    return o.swapaxes(0, 1).reshape(B, L, ATT_W)


def attn_mixer(hx, hc, ang, in_w, out_w, q_norm, k_norm, ctx_out):
    B, L, _ = hx.shape
    Lc = hc.shape[1]
    cuts = [ATT_W, ATT_W + KV_W, ATT_W + 2 * KV_W]
    q, k, v, g = jnp.split(hx @ in_w, cuts, axis=-1)
    q = rope_2d(rmsnorm(q.reshape(B, L, N_HEADS, HEAD_DIM), q_norm), ang)
    k = rope_2d(rmsnorm(k.reshape(B, L, N_KV, HEAD_DIM), k_norm), ang)
    v = v.reshape(B, L, N_KV, HEAD_DIM)
    if ctx_out:
        qc, kc, vc, gc = jnp.split(hc @ in_w, cuts, axis=-1)
    else:
        kc, vc = jnp.split(hc @ in_w[:, ATT_W:ATT_W + 2 * KV_W], 2, axis=-1)
    kc = rmsnorm(kc.reshape(B, Lc, N_KV, HEAD_DIM), k_norm)
    vc = vc.reshape(B, Lc, N_KV, HEAD_DIM)
    o = blocked_gqa(q, jnp.concatenate([kc, k], axis=1), jnp.concatenate([vc, v], axis=1))
    yx = (jax.nn.silu(g) * o) @ out_w
    if ctx_out:
        qc = rmsnorm(qc.reshape(B, Lc, N_HEADS, HEAD_DIM), q_norm)
        yc = (jax.nn.silu(gc) * gqa(qc, kc, vc)) @ out_w
    else:
        yc = None
    return yx, yc


def setup_inputs(seed: int = 0) -> dict:
    key = jax.random.key(seed)
    ks = iter(jax.random.split(key, 32))
    nrm = lambda shape, s: jax.random.normal(next(ks), shape, F32) * s
    D = D_MODEL
    a0 = jnp.log(-jnp.log1p(-(2.0 ** (-5.0 - jnp.arange(RET_H, dtype=F32)))))
    return {
        'x': nrm((BATCH, SEQ, D), 1.0),
        'c': nrm((BATCH, D), 1.0),
        'ctx': nrm((BATCH, CTX_LEN, D), 1.0),
        'c_ctx': nrm((D,), 1.0),
        'norm_g': 1.0 + nrm((DEPTH, D), 0.02),
        'ada_w': nrm((DEPTH, D, 3 * D), 0.5 * D ** -0.5),
        'ada_b': nrm((DEPTH, 3 * D), 0.02),
        'er_in_w': nrm((N_EVEN, D, EVEN_IN), D ** -0.5),
        'er_out_w': nrm((N_EVEN, EVEN_OUT, D), EVEN_OUT ** -0.5),
        'hy_conv_w': nrm((N_EVEN, 3, 3 * HY_W), 3 ** -0.5),
        'hy_conv_b': nrm((N_EVEN, 3 * HY_W), 0.02),
        'hy_f_w1': nrm((N_EVEN, HY_EMB, HY_ORDER), HY_EMB ** -0.5),
        'hy_f_b1': nrm((N_EVEN, HY_ORDER), 0.02),
        'hy_f_freq': 1.0 + nrm((N_EVEN, HY_ORDER), 0.02),
        'hy_f_w2': nrm((N_EVEN, HY_ORDER, HY_ORDER), HY_ORDER ** -0.5),
        'hy_f_b2': nrm((N_EVEN, HY_ORDER), 0.02),
        'hy_f_w3': nrm((N_EVEN, HY_ORDER, 2 * HY_W), HY_FILTER_GAIN * HY_ORDER ** -0.5),
        'hy_bias': nrm((N_EVEN, HY_W), 0.5),
        'ret_decay_f': a0 + nrm((N_EVEN, RET_H), 0.1),
        'ret_decay_b': a0 + nrm((N_EVEN, RET_H), 0.1),
        'ret_norm_g': 1.0 + nrm((N_EVEN, RET_W), 0.02),
        'at_in_w': nrm((N_ODD, D, ODD_IN), D ** -0.5),
        'at_out_w': nrm((N_ODD, ATT_W, D), ATT_W ** -0.5),
        'at_q_norm': 1.0 + nrm((N_ODD, HEAD_DIM), 0.02),
        'at_k_norm': 1.0 + nrm((N_ODD, HEAD_DIM), 0.02),
        'final_norm_g': 1.0 + nrm((D,), 0.02),
    }


def reference(x, c, ctx, c_ctx, norm_g, ada_w, ada_b, er_in_w, er_out_w, hy_conv_w, hy_conv_b,
              hy_f_w1, hy_f_b1, hy_f_freq, hy_f_w2, hy_f_b2, hy_f_w3, hy_bias,
              ret_decay_f, ret_decay_b, ret_norm_g, at_in_w, at_out_w, at_q_norm, at_k_norm, final_norm_g):
    rows = x.shape[1] // GRID_W
    ang = grid_angles(rows)
    for i in range(DEPTH):
        ctx_out = i < DEPTH - 1
        sx, scx, gx = ada_params(c, ada_w[i], ada_b[i])
        sc, scc, gc = ada_params(c_ctx, ada_w[i], ada_b[i])
        hx = rmsnorm(x, norm_g[i]) * (1 + scx) + sx
        hc = rmsnorm(ctx, norm_g[i]) * (1 + scc) + sc
        j = i // 2
        if i % 2 == 0:
            filt = (hy_f_w1[j], hy_f_b1[j], hy_f_freq[j], hy_f_w2[j], hy_f_b2[j], hy_f_w3[j])
            yx, yc = even_mixer(hx, hc, er_in_w[j], er_out_w[j], hy_conv_w[j], hy_conv_b[j], filt, hy_bias[j],
                                ret_decay_f[j], ret_decay_b[j], ret_norm_g[j], ctx_out)
        else:
            yx, yc = attn_mixer(hx, hc, ang, at_in_w[j], at_out_w[j], at_q_norm[j], at_k_norm[j], ctx_out)
        x = x + gx * yx
        if ctx_out:
            ctx = ctx + gc * yc
    return rmsnorm(x, final_norm_g)
```

```python
import contextlib
import math
import numpy as np
import ml_dtypes
import concourse.bass as bass
import concourse.mybir as mybir
from concourse.bass_utils import run_bass_kernel_spmd

F32 = mybir.dt.float32
BF16 = mybir.dt.bfloat16
AF = mybir.ActivationFunctionType
ALU = mybir.AluOpType
AX = mybir.AxisListType

NCORES = 8
D = 1024
L = 2048
LC = 256
TOK = L + LC
NT = TOK // 128
EPS = 1e-6
NFFT = 2 * L
NFFTC = 2 * LC
PI = math.pi


class T:
    __slots__ = ("name", "w", "r")

    def __init__(self, name="t"):
        self.name = name
        self.w = None
        self.r = {}


class Sy:
    NS = 8

    def __init__(self, nc, stack):
        self.nc = nc
        self.stack = stack
        self.E = {"pe": nc.tensor, "dve": nc.vector, "act": nc.scalar, "pool": nc.gpsimd, "sp": nc.sync}
        self.sems = {}
        self.cnt = {}
        self.seen = {e: {} for e in self.E}
        for e in ("pe", "dve", "act", "pool"):
            self.sems[("c", e)] = stack.enter_context(nc.semaphore("c_" + e))
            self.cnt[e] = 0
        self.dn = {}
        for q in ("sp", "act", "pool"):
            self.dn[q] = 0
            for i in range(self.NS):
                self.sems[("d", q, i)] = stack.enter_context(nc.semaphore(f"d_{q}_{i}"))

    _uid = 0

    def sb(self, name, shape, dt, stack=None):
        Sy._uid += 1
        return (stack or self.stack).enter_context(self.nc.sbuf_tensor(f"s{Sy._uid}_{name}", list(shape), dt))

    def ps(self, name, shape, dt, stack=None):
        Sy._uid += 1
        return (stack or self.stack).enter_context(self.nc.psum_tensor(f"p{Sy._uid}_{name}", list(shape), dt))

    def _wait(self, eng, deps):
        best = {}
        for k, v in deps:
            if best.get(k, 0) < v:
                best[k] = v
        sn = self.seen[eng]
        for k, v in best.items():
            if sn.get(k, 0) >= v:
                continue
            self.E[eng].wait_ge(self.sems[k], v)
            sn[k] = v

    def _deps(self, eng_key, reads, writes, is_dma):
        deps = []
        for t in reads:
            if t.w is not None:
                if t.w[0] == eng_key and eng_key == ("c", "pe"):
                    continue
                deps.append(t.w)
        pe = eng_key == ("c", "pe")
        for t in writes:
            if t.w is not None and not (pe and t.w[0] == eng_key):
                deps.append(t.w)
            for k, v in t.r.items():
                if not (pe and k == eng_key):
                    deps.append((k, v))
        return deps

    def _reg(self, me, reads, writes):
        k, v = me
        for t in reads:
            t.r[k] = v
        for t in writes:
            t.w = me
            t.r = {}

    def op(self, eng, fn, reads=(), writes=()):
        key = ("c", eng)
        self._wait(eng, self._deps(key, reads, writes, False))
        ins = fn(self.E[eng])
        self.cnt[eng] += 1
        ins.then_inc(self.sems[key], 1)
        self._reg((key, self.cnt[eng]), reads, writes)
        return ins

    def dma(self, q, out, in_, reads=(), writes=(), **kw):
        n = self.dn[q]
        i = n % self.NS
        tgt = 16 * (n // self.NS + 1)
        key = ("d", q, i)
        deps = self._deps(key, reads, writes, True)
        if tgt > 16:
            deps.append((key, tgt - 16))
        self._wait(q, deps)
        ins = self.E[q].dma_start(out=out, in_=in_, **kw)
        ins.then_inc(self.sems[key], 16)
        self.dn[q] = n + 1
        self._reg((key, tgt), reads, writes)
        return ins

    def barrier(self):
        allk = []
        for e in ("pe", "dve", "act", "pool"):
            if self.cnt[e] > 0:
                allk.append((("c", e), self.cnt[e]))
        for q in ("sp", "act", "pool"):
            n = self.dn[q]
            for i in range(self.NS):
                c = (n - i + self.NS - 1) // self.NS if n > i else 0
                if c > 0:
                    allk.append((("d", q, i), 16 * c))
        for e in self.E:
            self._wait(e, [kv for kv in allk if kv[0] != ("c", e)])

    def finish(self, outs):
        self._wait("sp", [t.w for t in outs if t.w is not None])


class Ring:
    def __init__(self, sy, stack, name, n, shape, dt, psum=False):
        self.items = []
        for i in range(n):
            t = (sy.ps if psum else sy.sb)(f"{name}{i}", shape, dt, stack)
            self.items.append((t, T(f"{name}{i}")))
        self.i = 0

    def next(self):
        it = self.items[self.i % len(self.items)]
        self.i += 1
        return it


_CONST = {}


def _bf(a):
    return np.ascontiguousarray(a).astype(ml_dtypes.bfloat16)


def host_consts(NB):
    if NB in _CONST:
        return _CONST[NB]
    c = {}
    c["ident_bf"] = _bf(np.eye(128))
    c["ident_f"] = np.eye(128, dtype=np.float32)
    sel = np.zeros((NB + 1, NB + 1, 128), np.float32)
    for b in range(NB + 1):
        sel[b, b, :] = 1.0
    c["sel"] = sel.reshape(NB + 1, (NB + 1) * 128)
    def dft(Ls, N):
        t = np.arange(Ls, dtype=np.float64)[:, None]
        k = np.arange(N // 2, dtype=np.float64)[None, :]
        th = 2.0 * np.pi * (k + 0.5) * t / N
        return np.cos(th), -np.sin(th)
    C, S = dft(L, NFFT)
    fw = np.zeros((8, 128, 16, 512), np.float32)
    for g in range(8):
        blk = np.concatenate([C[:, 256 * g:256 * g + 256], S[:, 256 * g:256 * g + 256]], axis=1)
        fw[g] = blk.reshape(16, 128, 512).transpose(1, 0, 2)
    c["ffwd"] = _bf(fw)
    Fi = np.concatenate([C.T, S.T], axis=0) * (2.0 / NFFT)
    fi = np.zeros((4, 4, 128, 8, 512), np.float32)
    for ng in range(4):
        for fq in range(4):
            blk = Fi[fq * 1024:(fq + 1) * 1024, ng * 512:(ng + 1) * 512]
            fi[ng, fq] = blk.reshape(8, 128, 512).transpose(1, 0, 2)
    c["finv"] = _bf(fi)
    Cc, Sc = dft(LC, NFFTC)
    blk = np.concatenate([Cc, Sc], axis=1)
    c["ffwd_c"] = _bf(blk.reshape(2, 128, 512).transpose(1, 0, 2))
    Fic = np.concatenate([Cc.T, Sc.T], axis=0) * (2.0 / NFFTC)
    c["finv_c"] = _bf(Fic.reshape(4, 128, 256).transpose(1, 0, 2))
    def zfeat(Ls):
        n = np.arange(Ls, dtype=np.float32)
        t = n / np.float32(max(Ls - 1, 1))
        bands = np.linspace(1e-4, 15, 16, dtype=np.float32)
        ang = np.float32(2.0 * math.pi / Ls) * n[:, None] * bands[None, :]
        z = np.concatenate([t[:, None], np.cos(ang), -np.sin(ang)], axis=-1).astype(np.float32)
        maxd = math.log(1e-2) / 0.3
        mind = math.log(1e-2) / 1.5
        deltas = np.abs(np.linspace(mind, maxd, 512, dtype=np.float32))
        win = np.exp(-t[:, None] * deltas[None, :]).astype(np.float32)
        return z, win
    z, win = zfeat(L)
    c["zT"] = np.ascontiguousarray(z.T)
    c["win"] = np.ascontiguousarray(win.reshape(16, 128, 512).transpose(1, 0, 2))
    zc, winc = zfeat(LC)
    c["zcT"] = np.ascontiguousarray(zc.T)
    c["winc"] = np.ascontiguousarray(winc.reshape(2, 128, 512).transpose(1, 0, 2))
    rows = L // 64
    r = np.repeat(np.arange(rows, dtype=np.float32), 64)
    col = np.tile(np.arange(64, dtype=np.float32), rows)
    inv = (10000.0 ** (-np.arange(0, 64, 2, dtype=np.float32) / 64)).astype(np.float32)
    ang = np.concatenate([r[:, None] * inv, col[:, None] * inv], axis=-1)
    c["rcos"] = np.ascontiguousarray(np.cos(ang).astype(np.float32).reshape(16, 128, 64).transpose(1, 0, 2))
    c["rsin"] = np.ascontiguousarray(np.sin(ang).astype(np.float32).reshape(16, 128, 64).transpose(1, 0, 2))
    j = np.arange(128, dtype=np.float32)[:, None]
    i = np.arange(128, dtype=np.float32)[None, :]
    d = i - j
    rt = np.stack([np.maximum(d, 0), np.maximum(-d, 0), (d >= 0).astype(np.float32), (d <= 0).astype(np.float32),
                   np.broadcast_to(i + 1, (128, 128)), np.broadcast_to(128 - i, (128, 128))], axis=1)
    c["rtab"] = np.ascontiguousarray(rt.astype(np.float32))
    c["rcol"] = np.ascontiguousarray(np.stack([127 - j[:, 0], j[:, 0]], axis=1).astype(np.float32))
    _CONST[NB] = c
    return c


class Cx:
    pass


def build(NB, phases=("ada", "p1", "p2", "p3", "p4", "l1"), debug=()):
    nc = bass.Bass("TRN2", target_bir_lowering=False)
    hc = host_consts(NB)
    cx = Cx()
    cx.nc = nc
    cx.NB = NB
    NB1 = NB + 1
    dram = {}

    def din(name, shape, dt=F32):
        dram[name] = nc.dram_tensor(name, list(shape), dt, kind="ExternalInput").ap()
        return dram[name]

    def dscr(name, shape, dt):
        kind = "ExternalOutput" if name in debug else "Internal"
        dram[name] = nc.dram_tensor(name, list(shape), dt, kind=kind).ap()
        return dram[name]

    din("x", [NB, L, D]); din("ctx", [NB, LC, D]); din("cT", [128, 8, NB1])
    din("norm_g", [2, D]); din("final_norm_g", [D]); din("ada_w", [2, D, 3 * D]); din("ada_b", [2, 3 * D])
    din("er_in_w", [D, 4096]); din("er_out_w", [D, D])
    din("convw", [128, 12, 3]); din("convb", [128, 12])
    din("hy_f_w1", [33, 64]); din("hy_f_b1", [64, 1]); din("hy_f_freq", [64, 1]); din("hy_f_w2", [64, 64])
    din("hy_f_b2", [64, 1]); din("hy_f_w3", [64, 1024]); din("hy_bias", [128, 4])
    din("retdec", [8]); din("ret_norm_g", [512])
    din("at_in_w", [D, 2560]); din("at_out_w", [D, D]); din("at_q_norm", [128]); din("at_k_norm", [128])
    for k, v in hc.items():
        din("k_" + k, v.shape, BF16 if v.dtype == ml_dtypes.bfloat16 else F32)
    out = nc.dram_tensor("out", [NB, L, D], F32, kind="ExternalOutput").ap()
    dscr("u_tm", [NB, TOK, 512], BF16); dscr("u_fm", [NB, 512, TOK], BF16); dscr("w_fm", [NB, 512, TOK], BF16)
    dscr("qT", [NB, 512, TOK], BF16); dscr("kT", [NB, 512, TOK], BF16)
    dscr("k_tm", [NB, TOK, 512], BF16); dscr("v_tm", [NB, TOK, 512], BF16); dscr("g_tm", [NB, TOK, 512], BF16)
    dscr("yT", [NB, D, TOK], BF16)
    dscr("aqT", [NB, 8, 128, L], BF16); dscr("akT", [NB, 2, 128, TOK], BF16)
    dscr("av", [NB, TOK, 256], BF16); dscr("agT", [NB, D, L], BF16)
    dscr("x1", [NB, TOK, D], F32)
    dscr("dbg_mod", [2, NB1, 3 * D], F32)
    dscr("modrows", [2, NB1, 3 * D], F32)
    dscr("dbg_h", [L, 1024], F32)
    cx.dram = dram
    cx.out = out
    cx.tdram = {k: T("dram_" + k) for k in dram}
    cx.tout = T("out")

    with contextlib.ExitStack() as st:
        sy = Sy(nc, st)
        cx.sy = sy
        cx.ident_bf = sy.sb("ident_bf", [128, 128], BF16)
        cx.ident_f = sy.sb("ident_f", [128, 128], F32)
        cx.modFM = [sy.sb(f"modFM{l}", [128, 24, NB1], F32) for l in range(2)]
        cx.t_const = T("const")
        cx.t_mod = [T("mod0"), T("mod1")]
        sy.dma("sp", cx.ident_bf[:], dram["k_ident_bf"], writes=[cx.t_const])
        sy.dma("sp", cx.ident_f[:], dram["k_ident_f"], writes=[cx.t_const])
        cx.psum = []
        for i in range(8):
            cx.psum.append((sy.ps(f"pb{i}", [128, 512], F32), T(f"pb{i}")))

        if "ada" in phases:
            phase_ada(cx)
            sy.barrier()
        if "p1" in phases:
            phase1(cx)
            sy.barrier()
        if "p2" in phases:
            phase2(cx)
            sy.barrier()
        if "p3" in phases:
            phase3(cx)
            sy.barrier()
        if "p4" in phases:
            phase4(cx)
            sy.barrier()
        if "l1" in phases:
            layer1(cx)
        if "l1p" in phases:
            l1_prep(cx)
        if "l1a" in phases:
            l1_attn(cx)
        outs = [cx.tout] + [cx.tdram[k] for k in debug]
        sy.barrier()
        sy.finish(outs)
    return nc


def phase_ada(cx):
    sy, nc, NB, dram = cx.sy, cx.nc, cx.NB, cx.dram
    NB1 = NB + 1
    with contextlib.ExitStack() as ph:
        scT = sy.sb("scT", [128, 8, NB1], F32, ph)
        t_sc = T()
        sy.dma("sp", scT[:], dram["cT"], writes=[t_sc])
        sy.op("act", lambda e: e.activation(out=scT[:], in_=scT[:], func=AF.Silu), reads=[t_sc], writes=[t_sc])
        wsb = sy.sb("adaw", [128, 8, 3 * D], F32, ph)
        bb = sy.sb("adab", [NB1, 3 * D], F32, ph)
        ngb = sy.sb("ngb", [NB1, D], F32, ph)
        mTM = [sy.sb(f"modTM{l}", [NB1, 3 * D], F32, ph) for l in range(2)]
        for l in range(2):
            t_w, t_b = T(), T()
            for k in range(8):
                sy.dma("sp", wsb[:, k, :], dram["ada_w"][l, k * 128:(k + 1) * 128, :], writes=[t_w])
            sy.dma("sp", bb[:], dram["ada_b"][l].partition_broadcast(NB1), writes=[t_b])
            sy.dma("sp", ngb[:], dram["norm_g"][l].partition_broadcast(NB1), writes=[t_b])
            m = mTM[l]
            tm = cx.t_mod[l]
            for n in range(6):
                pt, tp = cx.psum[n % 2]
                for k in range(8):
                    sy.op("pe", lambda e, k=k: e.matmul(pt[0:NB1, :], scT[:, k, :], wsb[:, k, n * 512:(n + 1) * 512],
                                                        start=(k == 0), stop=(k == 7)), reads=[t_sc, t_w], writes=[tp])
                sy.op("dve", lambda e: e.tensor_tensor(out=m[:, n * 512:(n + 1) * 512], in0=pt[0:NB1, :],
                                                       in1=bb[:, n * 512:(n + 1) * 512], op=ALU.add),
                      reads=[tp, t_b], writes=[tm])
            sy.op("dve", lambda e: e.scalar_tensor_tensor(out=m[:, D:2 * D], in0=m[:, D:2 * D], scalar=1.0, in1=ngb[:],
                                                          op0=ALU.add, op1=ALU.mult), reads=[tm, t_b], writes=[tm])
            for j in range(24):
                pt, tp = cx.psum[2 + j % 2]
                sy.op("pe", lambda e: e.transpose(out=pt[:, 0:NB1], in_=m[:, j * 128:(j + 1) * 128],
                                                  identity=cx.ident_f[0:NB1, 0:NB1]), reads=[tm, cx.t_const], writes=[tp])
                sy.op("dve", lambda e: e.tensor_copy(out=cx.modFM[l][:, j, :], in_=pt[:, 0:NB1]), reads=[tp], writes=[tm])
            sy.dma("sp", dram["modrows"][l], m[:], reads=[tm], writes=[cx.tdram["modrows"]])


def load_w_bf16(cx, dst, src, ncols, t_w):
    for k in range(8):
        for c0 in range(0, ncols, 2048):
            c1 = min(ncols, c0 + 2048)
            cx.sy.dma("pool", dst[:, k, c0:c1], src[k * 128:(k + 1) * 128, c0:c1], writes=[t_w])


def interleave(gens):
    gens = list(gens)
    while gens:
        for g in list(gens):
            try:
                next(g)
            except StopIteration:
                gens.remove(g)


def norm_tile_g(cx, l, bsel, src_ap, xt_ring, scr, hxT, tok0, t_h, pbank, small, t_src=None, keep=None):
    sy = cx.sy
    xt, t_x = xt_ring.next()
    if keep is not None:
        keep.append((xt, t_x))
    sy.dma("sp", xt[:], src_ap, reads=[t_src] if t_src else [], writes=[t_x])
    ss, t_ss = small.next()
    sq, t_sq = scr.next()
    sy.op("act", lambda e: e.activation(out=sq[:], in_=xt[:], func=AF.Square, accum_out=ss[:, 0:1]),
          reads=[t_x], writes=[t_sq, t_ss])
    yield
    sy.op("act", lambda e: e.activation(out=ss[:, 1:2], in_=ss[:, 0:1], func=AF.Sqrt, scale=1.0 / D, bias=cx.eps_c[:, 0:1]),
          reads=[t_ss, cx.t_eps], writes=[t_ss])
    yield
    sy.op("dve", lambda e: e.reciprocal(out=ss[:, 2:3], in_=ss[:, 1:2]), reads=[t_ss], writes=[t_ss])
    yield
    sy.op("dve", lambda e: e.tensor_scalar(out=sq[:], in0=xt[:], scalar1=ss[:, 2:3], scalar2=None, op0=ALU.mult),
          reads=[t_x, t_ss], writes=[t_sq])
    yield
    pt, tp = pbank
    ptb = pt[:].bitcast(BF16)
    for k in range(8):
        sy.op("pe", lambda e: e.transpose(out=ptb[:, k * 128:(k + 1) * 128], in_=sq[:, k * 128:(k + 1) * 128],
                                          identity=cx.ident_bf[:]), reads=[t_sq, cx.t_const], writes=[tp])
    yield
    mf = cx.modFM[l]
    for k in range(8):
        if k % 2 == 0:
            sy.op("act", lambda e: e.activation(out=hxT[:, k, tok0:tok0 + 128], in_=ptb[:, k * 128:(k + 1) * 128],
                                                func=AF.Identity, scale=mf[:, 8 + k, bsel:bsel + 1], bias=mf[:, k, bsel:bsel + 1]),
                  reads=[tp, cx.t_mod[l]], writes=[t_h])
        else:
            sy.op("dve", lambda e: e.tensor_scalar(out=hxT[:, k, tok0:tok0 + 128], in0=ptb[:, k * 128:(k + 1) * 128],
                                                   scalar1=mf[:, 8 + k, bsel:bsel + 1], scalar2=mf[:, k, bsel:bsel + 1],
                                                   op0=ALU.mult, op1=ALU.add), reads=[tp, cx.t_mod[l]], writes=[t_h])
    yield


def norm_tile(cx, l, bsel, src_ap, xt_ring, scr, hxT, tok0, t_h, pbank, small, t_src=None):
    for _ in norm_tile_g(cx, l, bsel, src_ap, xt_ring, scr, hxT, tok0, t_h, pbank, small, t_src):
        pass


def mk_eps(cx, ph):
    cx.eps_c = cx.sy.sb("eps_c", [128, 1], F32, ph)
    cx.t_eps = T()
    cx.sy.op("dve", lambda e: e.memset(cx.eps_c[:], EPS), writes=[cx.t_eps])


def tok_src(cx, b, tt, l):
    d = cx.dram
    if l == 0:
        if tt < 2:
            return d["ctx"][b, tt * 128:(tt + 1) * 128, :], cx.NB, None
        return d["x"][b, (tt - 2) * 128:(tt - 1) * 128, :], b, None
    return d["x1"][b, tt * 128:(tt + 1) * 128, :], (cx.NB if tt < 2 else b), cx.tdram["x1"]


def phase1(cx):
    sy, nc, NB, dram = cx.sy, cx.nc, cx.NB, cx.dram
    with contextlib.ExitStack() as ph:
        mk_eps(cx, ph)
        inw = sy.sb("inw", [128, 8, 4096], BF16, ph)
        t_inw = T()
        load_w_bf16(cx, inw, dram["er_in_w"], 4096, t_inw)
        convw = sy.sb("convw", [128, 12, 3], F32, ph)
        convb = sy.sb("convb", [128, 12], F32, ph)
        t_cw = T()
        sy.dma("sp", convw[:], dram["convw"], writes=[t_cw])
        sy.dma("sp", convb[:], dram["convb"], writes=[t_cw])
        hxT = sy.sb("hxT", [128, 8, TOK], BF16, ph)
        t_h = [T() for _ in range(NT)]
        xt_ring = Ring(sy, ph, "xt", 4, [128, D], F32)
        scr = Ring(sy, ph, "scr", 4, [128, D], BF16)
        small = Ring(sy, ph, "sm", 8, [128, 4], F32)
        rowp_ring = Ring(sy, ph, "rowp", 2, [128, 2307], F32)
        for rp, t_rp in rowp_ring.items:
            for pos in (0, 257, 2306):
                sy.op("dve", lambda e: e.memset(rp[:, pos:pos + 1], 0.0), writes=[t_rp])
        acc_ring = Ring(sy, ph, "acc", 2, [128, 2305], F32)
        sg = sy.sb("sg", [128, 2305], F32, ph)
        t_sg = T()
        sy.op("dve", lambda e: e.memset(sg[:, 256:257], 0.0), writes=[t_sg])
        urow_ring = Ring(sy, ph, "urow", 1, [128, 2305], BF16)
        wrow_ring = Ring(sy, ph, "wrow", 1, [128, 2305], BF16)
        utm_ring = Ring(sy, ph, "utm", 2, [128, NT, 128], BF16)
        row_ring = Ring(sy, ph, "qkrow", 2, [128, TOK], BF16)
        st_ring = Ring(sy, ph, "stg", 4, [128, 512], BF16)
        pacc = [cx.psum[i] for i in range(7)]
        pi = [0]

        def next_bank():
            r = pacc[pi[0] % 7]
            pi[0] += 1
            return r

        tgs = [(0, 256, 1)] + [(256 + g * 512, 512, 258 + g * 512) for g in range(4)]
        for b in range(NB):
            def ntile(tt):
                src, bsel, tsrc = tok_src(cx, b, tt, 0)
                yield from norm_tile_g(cx, 0, bsel, src, xt_ring, scr, hxT, tt * 128, t_h[tt], next_bank(), small, tsrc)
            for t0 in range(0, NT, 4):
                interleave([ntile(tt) for tt in range(t0, min(NT, t0 + 4))])
            for ct in range(4):
                accs = {}
                for ft, role in ((4 + ct, "x1"), (8 + ct, "v"), (ct, "x0"), (12 + ct, "hg")):
                    if role != "hg":
                        rp, t_rp = rowp_ring.next()
                    for (tok0, n, rc0) in tgs:
                        pt, tp = next_bank()
                        hts = t_h[tok0 // 128:(tok0 + n) // 128]
                        for k in range(8):
                            sy.op("pe", lambda e: e.matmul(pt[:, 0:n], inw[:, k, ft * 128:(ft + 1) * 128], hxT[:, k, tok0:tok0 + n],
                                                           start=(k == 0), stop=(k == 7)), reads=[t_inw] + hts, writes=[tp])
                        if role == "hg":
                            sy.op("act", lambda e: e.activation(out=sg[:, rc0 - 1:rc0 - 1 + n], in_=pt[:, 0:n], func=AF.Silu),
                                  reads=[tp], writes=[t_sg])
                        else:
                            sy.op("act", lambda e: e.activation(out=rp[:, rc0:rc0 + n], in_=pt[:, 0:n], func=AF.Copy),
                                  reads=[tp], writes=[t_rp])
                    if role == "hg":
                        continue
                    ac, t_ac = acc_ring.next()
                    cw = ft
                    sy.op("dve", lambda e: e.tensor_scalar(out=ac[:], in0=rp[:, 1:2306], scalar1=convw[:, cw, 1:2], scalar2=convb[:, cw:cw + 1],
                                                           op0=ALU.mult, op1=ALU.add), reads=[t_rp, t_cw], writes=[t_ac])
                    sy.op("dve", lambda e: e.scalar_tensor_tensor(out=ac[:], in0=rp[:, 0:2305], scalar=convw[:, cw, 0:1], in1=ac[:],
                                                                  op0=ALU.mult, op1=ALU.add), reads=[t_rp, t_cw, t_ac], writes=[t_ac])
                    sy.op("dve", lambda e: e.scalar_tensor_tensor(out=ac[:], in0=rp[:, 2:2307], scalar=convw[:, cw, 2:3], in1=ac[:],
                                                                  op0=ALU.mult, op1=ALU.add), reads=[t_rp, t_cw, t_ac], writes=[t_ac])
                    accs[role] = (ac, t_ac)
                    if role == "v":
                        ur, t_ur = urow_ring.next()
                        a1, t1 = accs["x1"]
                        sy.op("pool", lambda e: e.tensor_tensor(out=ur[:], in0=a1[:], in1=ac[:], op=ALU.mult),
                              reads=[t1, t_ac], writes=[t_ur])
                        for (d0, n, s0) in ((0, 256, 0), (256, 2048, 257)):
                            sy.dma("pool", dram["u_fm"][b, ct * 128:(ct + 1) * 128, d0:d0 + n], ur[:, s0:s0 + n],
                                   reads=[t_ur], writes=[cx.tdram["u_fm"]])
                        um, t_um = utm_ring.next()
                        pt, tp = cx.psum[7]
                        ptb = pt[:].bitcast(BF16)
                        for t0 in range(0, NT, 8):
                            nn = min(8, NT - t0)
                            for ti in range(nn):
                                tt = t0 + ti
                                c0 = tt * 128 if tt < 2 else 257 + (tt - 2) * 128
                                sy.op("pe", lambda e: e.transpose(out=ptb[:, ti * 128:(ti + 1) * 128], in_=ur[:, c0:c0 + 128],
                                                                  identity=cx.ident_bf[:]), reads=[t_ur, cx.t_const], writes=[tp])
                            sy.op("act", lambda e: e.activation(out=um[:, t0:t0 + nn, :], in_=ptb[:, 0:nn * 128].rearrange("p (a c) -> p a c", c=128),
                                                                func=AF.Copy), reads=[tp], writes=[t_um])
                        sy.dma("pool", dram["u_tm"][b].rearrange("(t p) c -> p t c", p=128)[:, :, ct * 128:(ct + 1) * 128], um[:],
                               reads=[t_um], writes=[cx.tdram["u_tm"]])
                wr, t_wr = wrow_ring.next()
                a0, t0_ = accs["x0"]
                sy.op("pool", lambda e: e.tensor_tensor(out=wr[:], in0=a0[:], in1=sg[:], op=ALU.mult), reads=[t0_, t_sg], writes=[t_wr])
                for (d0, n, s0) in ((0, 256, 0), (256, 2048, 257)):
                    sy.dma("pool", dram["w_fm"][b, ct * 128:(ct + 1) * 128, d0:d0 + n], wr[:, s0:s0 + n],
                           reads=[t_wr], writes=[cx.tdram["w_fm"]])
            for ft in range(8):
                row, t_row = row_ring.next()
                for (tok0, n, _) in tgs:
                    pt, tp = next_bank()
                    hts = t_h[tok0 // 128:(tok0 + n) // 128]
                    for k in range(8):
                        sy.op("pe", lambda e: e.matmul(pt[:, 0:n], inw[:, k, 2048 + ft * 128:2048 + (ft + 1) * 128], hxT[:, k, tok0:tok0 + n],
                                                       start=(k == 0), stop=(k == 7)), reads=[t_inw] + hts, writes=[tp])
                    if (tok0 // 256) % 2 == 0:
                        sy.op("act", lambda e: e.activation(out=row[:, tok0:tok0 + n], in_=pt[:, 0:n], func=AF.Copy), reads=[tp], writes=[t_row])
                    else:
                        sy.op("dve", lambda e: e.tensor_copy(out=row[:, tok0:tok0 + n], in_=pt[:, 0:n]), reads=[tp], writes=[t_row])
                dst = dram["qT"] if ft < 4 else dram["kT"]
                tdst = cx.tdram["qT"] if ft < 4 else cx.tdram["kT"]
                h = ft % 4
                sy.dma("pool", dst[b, h * 128:(h + 1) * 128, :], row[:], reads=[t_row], writes=[tdst])
            for tt in range(NT):
                for g, nm in enumerate(("k_tm", "v_tm", "g_tm")):
                    pt, tp = next_bank()
                    c0 = 2560 + g * 512
                    for k in range(8):
                        sy.op("pe", lambda e: e.matmul(pt[:], hxT[:, k, tt * 128:(tt + 1) * 128], inw[:, k, c0:c0 + 512],
                                                       start=(k == 0), stop=(k == 7)), reads=[t_inw, t_h[tt]], writes=[tp])
                    sg_, t_st = st_ring.next()
                    if nm == "g_tm":
                        sy.op("act", lambda e: e.activation(out=sg_[:], in_=pt[:], func=AF.Silu), reads=[tp], writes=[t_st])
                    elif nm == "k_tm":
                        sy.op("dve", lambda e: e.tensor_copy(out=sg_[:], in_=pt[:]), reads=[tp], writes=[t_st])
                    else:
                        sy.op("act", lambda e: e.activation(out=sg_[:], in_=pt[:], func=AF.Copy), reads=[tp], writes=[t_st])
                    sy.dma("pool", dram[nm][b, tt * 128:(tt + 1) * 128, :], sg_[:], reads=[t_st], writes=[cx.tdram[nm]])


def phase2(cx):
    sy, nc, NB, dram = cx.sy, cx.nc, cx.NB, cx.dram
    NC_ = NT
    order_f = list(range(NC_))
    order_b = [1, 0] + list(range(NC_ - 1, 1, -1))
    with contextlib.ExitStack() as ph:
        mk_eps(cx, ph)
        rtab = sy.sb("rtab", [128, 6, 128], F32, ph)
        rcol = sy.sb("rcol", [128, 2], F32, ph)
        dec = sy.sb("dec", [128, 8], F32, ph)
        lg = sy.sb("lg", [128, 8], F32, ph)
        rng = sy.sb("rng", [128, 512], F32, ph)
        t_c = T()
        sy.dma("sp", rtab[:], dram["k_rtab"], writes=[t_c])
        sy.dma("sp", rcol[:], dram["k_rcol"], writes=[t_c])
        sy.dma("sp", dec[:], dram["retdec"].partition_broadcast(128), writes=[t_c])
        sy.dma("sp", rng[:], dram["ret_norm_g"].partition_broadcast(128), writes=[t_c])
        sy.op("act", lambda e: e.activation(out=lg[:], in_=dec[:], func=AF.Exp), reads=[t_c], writes=[t_c])
        sy.op("dve", lambda e: e.tensor_scalar(out=lg[:], in0=lg[:], scalar1=-1.0, scalar2=None, op0=ALU.mult), reads=[t_c], writes=[t_c])
        maskT = sy.sb("maskT", [128, 4, 128], F32, ph)
        qd = sy.sb("qd", [128, 8, 128], F32, ph)
        kd = sy.sb("kd", [128, 8], F32, ph)
        cd = sy.sb("cd", [128, 8], F32, ph)
        e1 = sy.sb("e1", [128, 128], F32, ph)
        e2 = sy.sb("e2", [128, 128], F32, ph)
        ks = 128.0 ** -0.5
        for h in range(4):
            sy.op("act", lambda e: e.activation(out=e1[:], in_=rtab[:, 0, :], func=AF.Exp, scale=lg[:, h:h + 1]), reads=[t_c], writes=[t_c])
            sy.op("act", lambda e: e.activation(out=e2[:], in_=rtab[:, 1, :], func=AF.Exp, scale=lg[:, 4 + h:5 + h]), reads=[t_c], writes=[t_c])
            sy.op("dve", lambda e: e.scalar_tensor_tensor(out=e1[:], in0=e1[:], scalar=ks, in1=rtab[:, 2, :], op0=ALU.mult, op1=ALU.mult), reads=[t_c], writes=[t_c])
            sy.op("dve", lambda e: e.scalar_tensor_tensor(out=e2[:], in0=e2[:], scalar=ks, in1=rtab[:, 3, :], op0=ALU.mult, op1=ALU.mult), reads=[t_c], writes=[t_c])
            sy.op("dve", lambda e: e.tensor_tensor(out=maskT[:, h, :], in0=e1[:], in1=e2[:], op=ALU.add), reads=[t_c], writes=[t_c])
            sy.op("act", lambda e: e.activation(out=qd[:, h, :], in_=rtab[:, 4, :], func=AF.Exp, scale=lg[:, h:h + 1]), reads=[t_c], writes=[t_c])
            sy.op("act", lambda e: e.activation(out=qd[:, 4 + h, :], in_=rtab[:, 5, :], func=AF.Exp, scale=lg[:, 4 + h:5 + h]), reads=[t_c], writes=[t_c])
            sy.op("act", lambda e: e.activation(out=kd[:, h:h + 1], in_=rcol[:, 0:1], func=AF.Exp, scale=lg[:, h:h + 1]), reads=[t_c], writes=[t_c])
            sy.op("act", lambda e: e.activation(out=kd[:, 4 + h:5 + h], in_=rcol[:, 1:2], func=AF.Exp, scale=lg[:, 4 + h:5 + h]), reads=[t_c], writes=[t_c])
        sy.op("act", lambda e: e.activation(out=cd[:], in_=lg[:], func=AF.Exp, scale=128.0), reads=[t_c], writes=[t_c])
        sy.op("dve", lambda e: e.tensor_scalar(out=kd[:], in0=kd[:], scalar1=ks, scalar2=None, op0=ALU.mult), reads=[t_c], writes=[t_c])

        qT_r = Ring(sy, ph, "qTh", 2, [128, TOK], BF16)
        kT_r = Ring(sy, ph, "kTh", 2, [128, TOK], BF16)
        k_r = Ring(sy, ph, "kh", 2, [128, NT, 128], BF16)
        v_r = Ring(sy, ph, "vh", 2, [128, NT, 128], BF16)
        g_r = Ring(sy, ph, "gh", 2, [128, NT, 128], BF16)
        qf_r = Ring(sy, ph, "qf", 2, [128, NT, 128], BF16)
        qb_r = Ring(sy, ph, "qb", 2, [128, NT, 128], BF16)
        kf_r = Ring(sy, ph, "kf", 2, [128, NT, 128], BF16)
        kb_r = Ring(sy, ph, "kb", 2, [128, NT, 128], BF16)
        gg_r = Ring(sy, ph, "gg", 2, [128, NT, 128], F32)
        S32_r = [Ring(sy, ph, f"S32_{i}", 2, [128, NT, 128], F32) for i in range(2)]
        Sbf_r = [Ring(sy, ph, f"Sbf_{i}", 2, [128, NT, 128], BF16) for i in range(2)]
        PT_r = Ring(sy, ph, "PT", 2, [128, 4, 128], BF16)
        oc_r = Ring(sy, ph, "oc", 2, [128, 4, 128], F32)
        sq_r = Ring(sy, ph, "sq", 2, [128, 4, 128], F32)
        y_r = Ring(sy, ph, "yy", 2, [128, 4, 128], BF16)
        st_r = Ring(sy, ph, "hst", 4, [128, 16], F32)
        yT_r = Ring(sy, ph, "yTh", 2, [128, TOK], BF16)
        pST = [cx.psum[0], cx.psum[1]]
        pO = [cx.psum[2], cx.psum[3]]
        pTr = [cx.psum[4], cx.psum[5]]
        pD = [cx.psum[6], cx.psum[7]]
        cnt = {"st": 0, "o": 0, "d": 0, "tr": 0}

        def nb(lst, key):
            r = lst[cnt[key] % len(lst)]
            cnt[key] += 1
            return r

        for b in range(NB):
            for h in range(4):
                qT, t_q = qT_r.next(); kT, t_k = kT_r.next(); kh, t_kh = k_r.next(); vh, t_vh = v_r.next(); gh, t_gh = g_r.next()
                hs = slice(h * 128, (h + 1) * 128)
                sy.dma("sp", qT[:], dram["qT"][b, hs, :], reads=[cx.tdram["qT"]], writes=[t_q])
                sy.dma("sp", kT[:], dram["kT"][b, hs, :], reads=[cx.tdram["kT"]], writes=[t_k])
                for (dst, t_d, nm) in ((kh, t_kh, "k_tm"), (vh, t_vh, "v_tm"), (gh, t_gh, "g_tm")):
                    sy.dma("sp", dst[:], dram[nm][b].rearrange("(t p) c -> p t c", p=128)[:, :, hs], reads=[cx.tdram[nm]], writes=[t_d])
                qf, t_qf = qf_r.next(); qb, t_qb = qb_r.next(); kf, t_kf = kf_r.next(); kb, t_kb = kb_r.next(); gg, t_gg = gg_r.next()
                q3 = qT[:].rearrange("p (t c) -> p t c", c=128)
                sy.op("dve", lambda e: e.tensor_scalar(out=kf[:], in0=kh[:], scalar1=kd[:, h:h + 1], scalar2=None, op0=ALU.mult),
                      reads=[t_kh, t_c], writes=[t_kf])
                sy.op("dve", lambda e: e.tensor_scalar(out=kb[:], in0=kh[:], scalar1=kd[:, 4 + h:5 + h], scalar2=None, op0=ALU.mult),
                      reads=[t_kh, t_c], writes=[t_kb])
                sy.op("pool", lambda e: e.tensor_tensor(out=qf[:], in0=q3, in1=qd[:, h, :].unsqueeze(1).to_broadcast([128, NT, 128]), op=ALU.mult),
                      reads=[t_q, t_c], writes=[t_qf])
                sy.op("pool", lambda e: e.tensor_tensor(out=qb[:], in0=q3, in1=qd[:, 4 + h, :].unsqueeze(1).to_broadcast([128, NT, 128]), op=ALU.mult),
                      reads=[t_q, t_c], writes=[t_qb])
                sy.op("pool", lambda e: e.tensor_tensor(out=gg[:], in0=gh[:], in1=rng[:, hs].unsqueeze(1).to_broadcast([128, NT, 128]), op=ALU.mult),
                      reads=[t_gh, t_c], writes=[t_gg])
                S32 = [S32_r[0].next(), S32_r[1].next()]
                Sbf = [Sbf_r[0].next(), Sbf_r[1].next()]
                orders = (order_f, order_b)
                kxs = ((kf, t_kf), (kb, t_kb))
                for di in range(2):
                    c0 = orders[di][0]
                    sy.op("pool", lambda e: e.memset(S32[di][0][:, c0, :], 0.0), writes=[S32[di][1]])
                for idx in range(NC_ - 1):
                    pt, tp = nb(pD, "d")
                    for di in range(2):
                        c = orders[di][idx]
                        sy.op("pe", lambda e: e.matmul(pt[:, di * 128:(di + 1) * 128], kxs[di][0][:, c, :], vh[:, c, :], start=True, stop=True),
                              reads=[kxs[di][1], t_vh], writes=[tp])
                    for di in range(2):
                        c = orders[di][idx]
                        cn = orders[di][idx + 1]
                        sy.op("dve", lambda e: e.scalar_tensor_tensor(out=S32[di][0][:, cn, :], in0=S32[di][0][:, c, :], scalar=cd[:, 4 * di + h:4 * di + h + 1],
                                                                      in1=pt[:, di * 128:(di + 1) * 128], op0=ALU.mult, op1=ALU.add),
                              reads=[tp, S32[di][1], t_c], writes=[S32[di][1]])
                for di in range(2):
                    sy.op("act", lambda e: e.activation(out=Sbf[di][0][:], in_=S32[di][0][:], func=AF.Copy), reads=[S32[di][1]], writes=[Sbf[di][1]])
                yT, t_yT = yT_r.next()

                def group(g0):
                    n = min(4, NC_ - g0)
                    pst, tpst = nb(pST, "st")
                    for ci in range(n):
                        c = g0 + ci
                        sy.op("pe", lambda e: e.matmul(pst[:, ci * 128:(ci + 1) * 128], kT[:, c * 128:(c + 1) * 128], qT[:, c * 128:(c + 1) * 128],
                                                       start=True, stop=True), reads=[t_k, t_q], writes=[tpst])
                    yield
                    PT, t_PT = PT_r.next()
                    sy.op("dve", lambda e: e.tensor_tensor(out=PT[:, 0:n, :], in0=pst[:, 0:n * 128].rearrange("p (a c) -> p a c", c=128),
                                                           in1=maskT[:, h, :].unsqueeze(1).to_broadcast([128, n, 128]), op=ALU.mult),
                          reads=[tpst, t_c], writes=[t_PT])
                    yield
                    po, tpo = nb(pO, "o")
                    for ci in range(n):
                        c = g0 + ci
                        osl = po[:, ci * 128:(ci + 1) * 128]
                        sy.op("pe", lambda e: e.matmul(osl, PT[:, ci, :], vh[:, c, :], start=True, stop=False), reads=[t_PT, t_vh], writes=[tpo])
                        sy.op("pe", lambda e: e.matmul(osl, qf[:, c, :], Sbf[0][0][:, c, :], start=False, stop=False), reads=[t_qf, Sbf[0][1]], writes=[tpo])
                        sy.op("pe", lambda e: e.matmul(osl, qb[:, c, :], Sbf[1][0][:, c, :], start=False, stop=True), reads=[t_qb, Sbf[1][1]], writes=[tpo])
                    yield
                    o3 = po[:, 0:n * 128].rearrange("p (a c) -> p a c", c=128)
                    stt, t_st = st_r.next()
                    oc, t_oc = oc_r.next(); sq, t_sq = sq_r.next(); yy, t_yy = y_r.next()
                    sy.op("dve", lambda e: e.tensor_reduce(out=stt[:, 0:n], in_=o3, axis=AX.X, op=ALU.add), reads=[tpo], writes=[t_st])
                    yield
                    sy.op("dve", lambda e: e.tensor_scalar(out=stt[:, 4:4 + n], in0=stt[:, 0:n], scalar1=-1.0 / 128, scalar2=None, op0=ALU.mult),
                          reads=[t_st], writes=[t_st])
                    yield
                    sy.op("dve", lambda e: e.tensor_tensor(out=oc[:, 0:n, :], in0=o3, in1=stt[:, 4:4 + n].unsqueeze(2).to_broadcast([128, n, 128]), op=ALU.add),
                          reads=[tpo, t_st], writes=[t_oc])
                    yield
                    sy.op("pool", lambda e: e.tensor_tensor(out=sq[:, 0:n, :], in0=oc[:, 0:n, :], in1=oc[:, 0:n, :], op=ALU.mult), reads=[t_oc], writes=[t_sq])
                    yield
                    sy.op("dve", lambda e: e.tensor_reduce(out=stt[:, 8:8 + n], in_=sq[:, 0:n, :], axis=AX.X, op=ALU.add), reads=[t_sq], writes=[t_st])
                    yield
                    sy.op("act", lambda e: e.activation(out=stt[:, 8:8 + n], in_=stt[:, 8:8 + n], func=AF.Sqrt, scale=1.0 / 128, bias=cx.eps_c[:, 0:1]),
                          reads=[t_st, cx.t_eps], writes=[t_st])
                    yield
                    sy.op("dve", lambda e: e.reciprocal(out=stt[:, 12:12 + n], in_=stt[:, 8:8 + n]), reads=[t_st], writes=[t_st])
                    yield
                    sy.op("dve", lambda e: e.tensor_tensor(out=oc[:, 0:n, :], in0=oc[:, 0:n, :], in1=stt[:, 12:12 + n].unsqueeze(2).to_broadcast([128, n, 128]), op=ALU.mult),
                          reads=[t_oc, t_st], writes=[t_oc])
                    yield
                    sy.op("pool", lambda e: e.tensor_tensor(out=yy[:, 0:n, :], in0=oc[:, 0:n, :], in1=gg[:, g0:g0 + n, :], op=ALU.mult),
                          reads=[t_oc, t_gg], writes=[t_yy])
                    yield
                    ptr, tptr = nb(pTr, "tr")
                    ptb = ptr[:].bitcast(BF16)
                    for ci in range(n):
                        sy.op("pe", lambda e: e.transpose(out=ptb[:, ci * 128:(ci + 1) * 128], in_=yy[:, ci, :], identity=cx.ident_bf[:]),
                              reads=[t_yy, cx.t_const], writes=[tptr])
                    yield
                    sy.op("act", lambda e: e.activation(out=yT[:, g0 * 128:(g0 + n) * 128], in_=ptb[:, 0:n * 128], func=AF.Copy),
                          reads=[tptr], writes=[t_yT])
                    yield

                interleave([group(0), group(4)])
                interleave([group(8), group(12)])
                interleave([group(16)])
                sy.dma("pool", dram["yT"][b, 512 + h * 128:512 + (h + 1) * 128, :], yT[:], reads=[t_yT], writes=[cx.tdram["yT"]])


def phase3(cx):
    sy, nc, NB, dram = cx.sy, cx.nc, cx.NB, cx.dram
    MAGIC = 12582912.0
    with contextlib.ExitStack() as ph:
        Kx = sy.sb("Kx", [128, 32, 512], BF16, ph)
        Kc = sy.sb("Kc", [128, 4, 512], BF16, ph)
        t_K = T()
        Fc = sy.sb("Fc", [128, 2, 512], BF16, ph)
        Fic = sy.sb("Fic", [128, 4, 256], BF16, ph)
        hbias = sy.sb("hbias", [128, 4], F32, ph)
        t_fc = T()
        sy.dma("sp", Fc[:], dram["k_ffwd_c"], writes=[t_fc])
        sy.dma("sp", Fic[:], dram["k_finv_c"], writes=[t_fc])
        sy.dma("sp", hbias[:], dram["hy_bias"], writes=[t_fc])
        F_r = Ring(sy, ph, "Fb", 2, [128, 16, 512], BF16)
        with contextlib.ExitStack() as fs:
            w1 = sy.sb("w1", [33, 64], F32, fs); w2 = sy.sb("w2", [64, 64], F32, fs); w3 = sy.sb("w3", [64, 1024], F32, fs)
            fb = sy.sb("fb", [64, 6], F32, fs)
            t_w = T()
            sy.dma("sp", w1[:], dram["hy_f_w1"], writes=[t_w]); sy.dma("sp", w2[:], dram["hy_f_w2"], writes=[t_w])
            sy.dma("sp", w3[:], dram["hy_f_w3"], writes=[t_w])
            sy.dma("sp", fb[:, 0:1], dram["hy_f_b1"], writes=[t_w]); sy.dma("sp", fb[:, 1:2], dram["hy_f_freq"], writes=[t_w])
            sy.dma("sp", fb[:, 2:3], dram["hy_f_b2"], writes=[t_w])
            sy.op("dve", lambda e: e.tensor_tensor(out=fb[:, 3:4], in0=fb[:, 0:1], in1=fb[:, 1:2], op=ALU.mult), reads=[t_w], writes=[t_w])
            sy.op("dve", lambda e: e.tensor_tensor(out=fb[:, 4:5], in0=fb[:, 2:3], in1=fb[:, 1:2], op=ALU.mult), reads=[t_w], writes=[t_w])
            zT = sy.sb("zT", [33, L], F32, fs)
            h1 = sy.sb("h1", [64, L], F32, fs)
            h2 = sy.sb("h2", [64, L], F32, fs)
            hfb = sy.sb("hfb", [128, 16, 1024], BF16, fs)
            tA = sy.sb("tA", [64, 512], F32, fs)
            tB = sy.sb("tB", [64, 512], F32, fs)
            win_r = Ring(sy, fs, "win", 2, [128, 512], F32)
            tmpA_r = Ring(sy, fs, "tmpA", 2, [128, 512], F32)
            for (Ls, zname, wname, Kdst, nct) in ((L, "k_zT", "k_win", Kx, 16), (LC, "k_zcT", "k_winc", Kc, 2)):
                t_z, t_h1, t_h2, t_hfb, t_t = T(), T(), T(), T(), T()
                sy.dma("sp", zT[:, 0:Ls], dram[zname], writes=[t_z])

                def sin_layer(wm, kdim, src, t_src, dst, t_dst, bcol):
                    for n0 in range(0, Ls, 512):
                        n = min(512, Ls - n0)
                        pt, tp = cx.psum[(n0 // 512) % 2]
                        sy.op("pe", lambda e: e.matmul(pt[0:64, 0:n], wm[0:kdim, :], src[0:kdim, n0:n0 + n], start=True, stop=True),
                              reads=[t_w, t_src], writes=[tp])
                        sy.op("dve", lambda e: e.tensor_scalar(out=tA[:, 0:n], in0=pt[0:64, 0:n], scalar1=fb[:, 1:2], scalar2=fb[:, bcol:bcol + 1],
                                                               op0=ALU.mult, op1=ALU.add), reads=[tp, t_w], writes=[t_t])
                        sy.op("dve", lambda e: e.tensor_scalar(out=tB[:, 0:n], in0=tA[:, 0:n], scalar1=1.0 / (2 * PI), scalar2=MAGIC,
                                                               op0=ALU.mult, op1=ALU.add), reads=[t_t], writes=[t_t])
                        sy.op("dve", lambda e: e.tensor_scalar(out=tB[:, 0:n], in0=tB[:, 0:n], scalar1=-MAGIC, scalar2=None, op0=ALU.add),
                              reads=[t_t], writes=[t_t])
                        sy.op("dve", lambda e: e.scalar_tensor_tensor(out=tA[:, 0:n], in0=tB[:, 0:n], scalar=-2 * PI, in1=tA[:, 0:n],
                                                                      op0=ALU.mult, op1=ALU.add), reads=[t_t], writes=[t_t])
                        sy.op("dve", lambda e: e.tensor_scalar(out=tA[:, 0:n], in0=tA[:, 0:n], scalar1=-PI, scalar2=PI, op0=ALU.max, op1=ALU.min),
                              reads=[t_t], writes=[t_t])
                        sy.op("act", lambda e: e.activation(out=dst[:, n0:n0 + n], in_=tA[:, 0:n], func=AF.Sin), reads=[t_t], writes=[t_dst])

                sin_layer(w1, 33, zT, t_z, h1, t_h1, 3)
                sin_layer(w2, 64, h1, t_h1, h2, t_h2, 4)
                for c in range(nct):
                    wn, t_wn = win_r.next()
                    sy.dma("sp", wn[:], dram[wname][:, c, :], writes=[t_wn])
                    for half in range(2):
                        pt, tp = cx.psum[2 + half]
                        sy.op("pe", lambda e: e.matmul(pt[:], h2[:, c * 128:(c + 1) * 128], w3[:, half * 512:(half + 1) * 512], start=True, stop=True),
                              reads=[t_h2, t_w], writes=[tp])
                        sy.op("dve", lambda e: e.tensor_tensor(out=hfb[:, c, half * 512:(half + 1) * 512], in0=pt[:], in1=wn[:], op=ALU.mult),
                              reads=[tp, t_wn], writes=[t_hfb])
                sy.op("dve", lambda e: e.memset(hfb[0:1, 0, 512:1024], 0.0), reads=[t_hfb], writes=[t_hfb])
                if "dbg_h" in cx.tdram and Ls == L:
                    pass
                ngrp = 8 if Ls == L else 1
                for g in range(ngrp):
                    if Ls == L:
                        Fb, t_F = F_r.next()
                        sy.dma("sp", Fb[:], dram["k_ffwd"][g], writes=[t_F])
                        Fsrc = Fb
                    else:
                        Fsrc, t_F = Fc, t_fc
                    for j in range(4):
                        pA, tpA = cx.psum[4 + (j % 2) * 2]
                        pB, tpB = cx.psum[5 + (j % 2) * 2]
                        for (pp, tpp, c0) in ((pA, tpA, 0), (pB, tpB, 512)):
                            for c in range(nct):
                                sy.op("pe", lambda e: e.matmul(pp[:], Fsrc[:, c, j * 128:(j + 1) * 128], hfb[:, c, c0:c0 + 512],
                                                               start=(c == 0), stop=(c == nct - 1)), reads=[t_F, t_hfb], writes=[tpp])
                        tm, t_tm = tmpA_r.next()
                        sy.op("act", lambda e: e.activation(out=tm[:], in_=pA[:], func=AF.Copy), reads=[tpA], writes=[t_tm])
                        if Ls == L:
                            ti = (2 * g + j) if j < 2 else (16 + 2 * g + j - 2)
                        else:
                            ti = j
                        sy.op("dve", lambda e: e.tensor_tensor(out=Kdst[:, ti, :], in0=tm[:], in1=pB[:], op=(ALU.add if j < 2 else ALU.subtract)),
                              reads=[t_tm, tpB], writes=[t_K])
            sy.barrier()
        utm = sy.sb("utm3", [128, NT, 512], BF16, ph)
        t_utm = T()
        Y = sy.sb("Y", [128, 32, 512], BF16, ph)
        Yc = sy.sb("Yc", [128, 4, 512], BF16, ph)
        t_Y, t_Yc = T(), T()
        Fi_r = Ring(sy, ph, "Fib", 2, [128, 8, 512], BF16)
        us_r = Ring(sy, ph, "us", 4, [128, 512], F32)
        tt_r = Ring(sy, ph, "ttm", 4, [128, 512], F32)
        uw_r = Ring(sy, ph, "uw", 4, [128, 2, 512], BF16)
        yb_r = Ring(sy, ph, "yb", 2, [128, 512], BF16)
        tmp_r = Ring(sy, ph, "tmp3", 2, [128, 512], F32)

        def spec_mul(pre, tpre, pim, tpim, Kt, ire, iim, Yt, t_Yt):
            ure, t_ure = us_r.next(); uim, t_uim = us_r.next()
            sy.op("act", lambda e: e.activation(out=ure[:], in_=pre[:], func=AF.Copy), reads=[tpre], writes=[t_ure])
            sy.op("act", lambda e: e.activation(out=uim[:], in_=pim[:], func=AF.Copy), reads=[tpim], writes=[t_uim])
            t1, t_t1 = tt_r.next(); t2, t_t2 = tt_r.next(); t3, t_t3 = tt_r.next(); t4, t_t4 = tt_r.next()
            sy.op("dve", lambda e: e.tensor_tensor(out=t1[:], in0=ure[:], in1=Kt[:, ire, :], op=ALU.mult), reads=[t_ure, t_K], writes=[t_t1])
            sy.op("dve", lambda e: e.tensor_tensor(out=t2[:], in0=uim[:], in1=Kt[:, iim, :], op=ALU.mult), reads=[t_uim, t_K], writes=[t_t2])
            sy.op("dve", lambda e: e.tensor_tensor(out=Yt[:, ire, :], in0=t1[:], in1=t2[:], op=ALU.subtract), reads=[t_t1, t_t2], writes=[t_Yt])
            sy.op("pool", lambda e: e.tensor_tensor(out=t3[:], in0=ure[:], in1=Kt[:, iim, :], op=ALU.mult), reads=[t_ure, t_K], writes=[t_t3])
            sy.op("pool", lambda e: e.tensor_tensor(out=t4[:], in0=uim[:], in1=Kt[:, ire, :], op=ALU.mult), reads=[t_uim, t_K], writes=[t_t4])
            sy.op("pool", lambda e: e.tensor_tensor(out=Yt[:, iim, :], in0=t3[:], in1=t4[:], op=ALU.add), reads=[t_t3, t_t4], writes=[t_Yt])

        def combine(b, ct, pacc, tpacc, tok0, n):
            uw, t_uw = uw_r.next()
            sy.dma("sp", uw[:, 0, 0:n], dram["u_fm"][b, ct * 128:(ct + 1) * 128, tok0:tok0 + n], reads=[cx.tdram["u_fm"]], writes=[t_uw])
            sy.dma("sp", uw[:, 1, 0:n], dram["w_fm"][b, ct * 128:(ct + 1) * 128, tok0:tok0 + n], reads=[cx.tdram["w_fm"]], writes=[t_uw])
            tm, t_tm = tmp_r.next()
            sy.op("dve", lambda e: e.scalar_tensor_tensor(out=tm[:, 0:n], in0=uw[:, 0, 0:n], scalar=hbias[:, ct:ct + 1], in1=pacc[:, 0:n],
                                                          op0=ALU.mult, op1=ALU.add), reads=[t_uw, tpacc, t_fc], writes=[t_tm])
            yb, t_yb = yb_r.next()
            sy.op("pool", lambda e: e.tensor_tensor(out=yb[:, 0:n], in0=tm[:, 0:n], in1=uw[:, 1, 0:n], op=ALU.mult), reads=[t_tm, t_uw], writes=[t_yb])
            sy.dma("pool", dram["yT"][b, ct * 128:(ct + 1) * 128, tok0:tok0 + n], yb[:, 0:n], reads=[t_yb], writes=[cx.tdram["yT"]])

        for b in range(NB):
            sy.dma("sp", utm[:], dram["u_tm"][b].rearrange("(t p) c -> p t c", p=128), reads=[cx.tdram["u_tm"]], writes=[t_utm])
            for j in range(4):
                pt, tp = cx.psum[j]
                for c in range(2):
                    sy.op("pe", lambda e: e.matmul(pt[:], Fc[:, c, j * 128:(j + 1) * 128], utm[:, c, :], start=(c == 0), stop=(c == 1)),
                          reads=[t_fc, t_utm], writes=[tp])
            for jj in range(2):
                spec_mul(cx.psum[jj][0], cx.psum[jj][1], cx.psum[2 + jj][0], cx.psum[2 + jj][1], Kc, jj, 2 + jj, Yc, t_Yc)
            for ct in range(4):
                pt, tp = cx.psum[4 + ct]
                for f in range(4):
                    sy.op("pe", lambda e: e.matmul(pt[:, 0:256], Yc[:, f, ct * 128:(ct + 1) * 128], Fic[:, f, :], start=(f == 0), stop=(f == 3)),
                          reads=[t_Yc, t_fc], writes=[tp])
                combine(b, ct, pt, tp, 0, 256)
            for g in range(8):
                Fb, t_F = F_r.next()
                sy.dma("sp", Fb[:], dram["k_ffwd"][g], writes=[t_F])
                for j in range(4):
                    pt, tp = cx.psum[j]
                    for c in range(16):
                        sy.op("pe", lambda e: e.matmul(pt[:], Fb[:, c, j * 128:(j + 1) * 128], utm[:, 2 + c, :], start=(c == 0), stop=(c == 15)),
                              reads=[t_F, t_utm], writes=[tp])
                for jj in range(2):
                    spec_mul(cx.psum[jj][0], cx.psum[jj][1], cx.psum[2 + jj][0], cx.psum[2 + jj][1], Kx, 2 * g + jj, 16 + 2 * g + jj, Y, t_Y)
            for ng in range(4):
                for fq in range(4):
                    Fi, t_Fi = Fi_r.next()
                    sy.dma("sp", Fi[:], dram["k_finv"][ng, fq], writes=[t_Fi])
                    for ct in range(4):
                        pt, tp = cx.psum[4 + ct]
                        for f in range(8):
                            sy.op("pe", lambda e: e.matmul(pt[:], Y[:, fq * 8 + f, ct * 128:(ct + 1) * 128], Fi[:, f, :],
                                                           start=(fq == 0 and f == 0), stop=(fq == 3 and f == 7)), reads=[t_Y, t_Fi], writes=[tp])
                for ct in range(4):
                    combine(b, ct, cx.psum[4 + ct][0], cx.psum[4 + ct][1], 256 + ng * 512, 512)


def gate_rows(cx, l, b, gbc, t_g):
    cx.sy.dma("sp", gbc[:], cx.dram["modrows"][l, b, 2 * D:3 * D].partition_broadcast(128), reads=[cx.tdram["modrows"]], writes=[t_g])


def phase4(cx):
    sy, nc, NB, dram = cx.sy, cx.nc, cx.NB, cx.dram
    with contextlib.ExitStack() as ph:
        ow = sy.sb("ow", [128, 8, D], BF16, ph)
        t_ow = T()
        load_w_bf16(cx, ow, dram["er_out_w"], D, t_ow)
        gc = sy.sb("gc", [128, D], F32, ph)
        t_gc = T()
        gate_rows(cx, 0, NB, gc, t_gc)
        gx_r = Ring(sy, ph, "gx", 2, [128, D], F32)
        yTs_r = Ring(sy, ph, "yTs", 2, [128, 8, TOK], BF16)
        xt_r = Ring(sy, ph, "xt4", 3, [128, D], F32)
        tm_r = Ring(sy, ph, "tm4", 3, [128, D], F32)
        bank = [0]
        for b in range(NB):
            gx, t_gx = gx_r.next()
            gate_rows(cx, 0, b, gx, t_gx)
            yTs, t_y = yTs_r.next()
            for k in range(8):
                sy.dma("sp", yTs[:, k, :], dram["yT"][b, k * 128:(k + 1) * 128, :], reads=[cx.tdram["yT"]], writes=[t_y])
            for tt in range(NT):
                src, bsel, tsrc = tok_src(cx, b, tt, 0)
                xt, t_x = xt_r.next()
                sy.dma("sp", xt[:], src, writes=[t_x])
                tm, t_tm = tm_r.next()
                g_, t_g_ = (gc, t_gc) if tt < 2 else (gx, t_gx)
                for half in range(2):
                    pt, tp = cx.psum[bank[0] % 6]
                    bank[0] += 1
                    hs = slice(half * 512, (half + 1) * 512)
                    for k in range(8):
                        sy.op("pe", lambda e: e.matmul(pt[:], yTs[:, k, tt * 128:(tt + 1) * 128], ow[:, k, hs], start=(k == 0), stop=(k == 7)),
                              reads=[t_y, t_ow], writes=[tp])
                    sy.op("dve", lambda e: e.tensor_tensor(out=tm[:, hs], in0=pt[:], in1=g_[:, hs], op=ALU.mult), reads=[tp, t_g_], writes=[t_tm])
                sy.op("pool", lambda e: e.tensor_tensor(out=tm[:], in0=tm[:], in1=xt[:], op=ALU.add), reads=[t_tm, t_x], writes=[t_tm])
                sy.dma("pool", dram["x1"][b, tt * 128:(tt + 1) * 128, :], tm[:], reads=[t_tm], writes=[cx.tdram["x1"]])


def head_norm_rope(cx, src, t_src, nh, gbc, t_c, tabs, pos_tile, dst, t_dst, tmp):
    sy = cx.sy
    sq, t_sq, st, t_st, rc, rd, t_r = tmp
    W = nh * 128
    s3 = src[:, 0:W].rearrange("p (h d) -> p h d", d=128)
    sy.op("pool", lambda e: e.tensor_tensor(out=sq[:, 0:W], in0=src[:, 0:W], in1=src[:, 0:W], op=ALU.mult), reads=[t_src], writes=[t_sq])
    yield
    sy.op("dve", lambda e: e.tensor_reduce(out=st[:, 0:nh], in_=sq[:, 0:W].rearrange("p (h d) -> p h d", d=128), axis=AX.X, op=ALU.add),
          reads=[t_sq], writes=[t_st])
    yield
    sy.op("act", lambda e: e.activation(out=st[:, 8:8 + nh], in_=st[:, 0:nh], func=AF.Sqrt, scale=1.0 / 128, bias=cx.eps_c[:, 0:1]),
          reads=[t_st, cx.t_eps], writes=[t_st])
    yield
    sy.op("dve", lambda e: e.reciprocal(out=st[:, 16:16 + nh], in_=st[:, 8:8 + nh]), reads=[t_st], writes=[t_st])
    yield
    sy.op("dve", lambda e: e.tensor_tensor(out=s3, in0=s3, in1=st[:, 16:16 + nh].unsqueeze(2).to_broadcast([128, nh, 128]), op=ALU.mult),
          reads=[t_src, t_st], writes=[t_src])
    yield
    if pos_tile is None:
        sy.op("pool", lambda e: e.tensor_tensor(out=dst[:, 0:W].rearrange("p (h d) -> p h d", d=128), in0=s3,
                                                in1=gbc[:].unsqueeze(1).to_broadcast([128, nh, 128]), op=ALU.mult), reads=[t_src, t_c], writes=[t_dst])
        yield
        return
    sy.op("pool", lambda e: e.tensor_tensor(out=s3, in0=s3, in1=gbc[:].unsqueeze(1).to_broadcast([128, nh, 128]), op=ALU.mult),
          reads=[t_src, t_c], writes=[t_src])
    yield
    rcos, rsin = tabs
    s4 = src[:, 0:W].rearrange("p (h m two) -> p h m two", two=2, m=64)
    d4 = dst[:, 0:W].rearrange("p (h m two) -> p h m two", two=2, m=64)
    x0, x1 = s4[:, :, :, 0], s4[:, :, :, 1]
    cb = rcos[:, pos_tile, :].unsqueeze(1).to_broadcast([128, nh, 64])
    sb_ = rsin[:, pos_tile, :].unsqueeze(1).to_broadcast([128, nh, 64])
    ra = sq[:, 0:nh * 64].rearrange("p (h m) -> p h m", m=64)
    rb = sq[:, 512:512 + nh * 64].rearrange("p (h m) -> p h m", m=64)
    sy.op("dve", lambda e: e.tensor_tensor(out=ra, in0=x0, in1=cb, op=ALU.mult), reads=[t_src, t_c, t_st], writes=[t_sq])
    sy.op("pool", lambda e: e.tensor_tensor(out=rc[:, 0:nh, :], in0=x0, in1=sb_, op=ALU.mult), reads=[t_src, t_c], writes=[t_r[0]])
    yield
    sy.op("dve", lambda e: e.tensor_tensor(out=rb, in0=x1, in1=sb_, op=ALU.mult), reads=[t_src, t_c], writes=[t_sq])
    sy.op("pool", lambda e: e.tensor_tensor(out=rd[:, 0:nh, :], in0=x1, in1=cb, op=ALU.mult), reads=[t_src, t_c], writes=[t_r[1]])
    yield
    sy.op("dve", lambda e: e.tensor_tensor(out=d4[:, :, :, 0], in0=ra, in1=rb, op=ALU.subtract), reads=[t_sq], writes=[t_dst])
    sy.op("pool", lambda e: e.tensor_tensor(out=d4[:, :, :, 1], in0=rc[:, 0:nh, :], in1=rd[:, 0:nh, :], op=ALU.add),
          reads=[t_r[0], t_r[1]], writes=[t_dst])
    yield


def layer1(cx):
    l1_prep(cx)
    cx.sy.barrier()
    l1_attn(cx)


def l1_prep(cx):
    sy, nc, NB, dram = cx.sy, cx.nc, cx.NB, cx.dram
    with contextlib.ExitStack() as ph:
        mk_eps(cx, ph)
        inw = sy.sb("ainw", [128, 8, 2560], BF16, ph)
        t_inw = T()
        load_w_bf16(cx, inw, dram["at_in_w"], 2560, t_inw)
        rcos = sy.sb("rcos", [128, 16, 64], F32, ph); rsin = sy.sb("rsin", [128, 16, 64], F32, ph)
        gq = sy.sb("gq", [128, 128], F32, ph); gk = sy.sb("gk", [128, 128], F32, ph)
        t_c = T()
        sy.dma("sp", rcos[:], dram["k_rcos"], writes=[t_c]); sy.dma("sp", rsin[:], dram["k_rsin"], writes=[t_c])
        sy.dma("sp", gq[:], dram["at_q_norm"].partition_broadcast(128), writes=[t_c])
        sy.dma("sp", gk[:], dram["at_k_norm"].partition_broadcast(128), writes=[t_c])
        xt_ring = Ring(sy, ph, "xtl", 4, [128, D], F32)
        scr = Ring(sy, ph, "scrl", 4, [128, D], BF16)
        small = Ring(sy, ph, "sml", 8, [128, 4], F32)
        hxa_r = Ring(sy, ph, "hxa", 2, [128, 8, 128], BF16)
        hxq_r = Ring(sy, ph, "hxq", 2, [128, 8, 512], BF16)
        kvf_r = Ring(sy, ph, "kvf", 4, [128, 256], F32)
        qf_r = Ring(sy, ph, "qf32", 4, [128, D], F32)
        qr_r = Ring(sy, ph, "qr", 4, [128, D], BF16)
        kr_r = Ring(sy, ph, "kr", 4, [128, 256], BF16)
        sq_r = Ring(sy, ph, "sqt", 4, [128, D], F32)
        st_r = Ring(sy, ph, "lst", 8, [128, 24], F32)
        rt_r = [Ring(sy, ph, f"rt{i}", 4, [128, 8, 64], F32) for i in range(2)]
        kTst_r = Ring(sy, ph, "kTst", 1, [128, 2, TOK], BF16)
        vst_r = Ring(sy, ph, "vst", 1, [128, NT, 256], BF16)
        qTst_r = Ring(sy, ph, "qTst", 2, [128, 8, 512], BF16)
        gst_r = Ring(sy, ph, "gst", 3, [128, 512], BF16)
        pM = [cx.psum[i] for i in range(8)]
        cnt = {"m": 0}

        def nb():
            r = pM[cnt["m"] % len(pM)]
            cnt["m"] += 1
            return r

        def tmpset():
            sq, t_sq = sq_r.next()
            st, t_st = st_r.next()
            rs = [r.next() for r in rt_r]
            return (sq, t_sq, st, t_st, rs[0][0], rs[1][0], [r[1] for r in rs])

        gate_gens = []
        for b in range(NB):
            kTst, t_kTst = kTst_r.next()
            vst, t_vst = vst_r.next()

            def kv_tile(tt, hx, t_hx, c0):
                pt, tp = nb()
                for k in range(8):
                    sy.op("pe", lambda e: e.matmul(pt[:], hx[:, k, c0:c0 + 128], inw[:, k, 1024:1536], start=(k == 0), stop=(k == 7)),
                          reads=[t_hx, t_inw], writes=[tp])
                yield
                kvf, t_kvf = kvf_r.next()
                sy.op("act", lambda e: e.activation(out=kvf[:], in_=pt[:, 0:256], func=AF.Copy), reads=[tp], writes=[t_kvf])
                sy.op("act", lambda e: e.activation(out=vst[:, tt, :], in_=pt[:, 256:512], func=AF.Copy), reads=[tp], writes=[t_vst])
                yield
                kr, t_kr = kr_r.next()
                yield from head_norm_rope(cx, kvf, t_kvf, 2, gk, t_c, (rcos, rsin), (tt - 2) if tt >= 2 else None, kr, t_kr, tmpset())
                pt2, tp2 = nb()
                ptb = pt2[:].bitcast(BF16)
                for g in range(2):
                    sy.op("pe", lambda e: e.transpose(out=ptb[:, g * 128:(g + 1) * 128], in_=kr[:, g * 128:(g + 1) * 128], identity=cx.ident_bf[:]),
                          reads=[t_kr, cx.t_const], writes=[tp2])
                yield
                sy.op("act", lambda e: e.activation(out=kTst[:, :, tt * 128:(tt + 1) * 128], in_=ptb[:, 0:256].rearrange("p (g c) -> p g c", c=128),
                                                    func=AF.Copy), reads=[tp2], writes=[t_kTst])
                yield

            def ctx_tile(tt):
                src, bsel, tsrc = tok_src(cx, b, tt, 1)
                hxa, t_hxa = hxa_r.next()
                yield from norm_tile_g(cx, 1, bsel, src, xt_ring, scr, hxa, 0, t_hxa, nb(), small, tsrc)
                yield from kv_tile(tt, hxa, t_hxa, 0)

            def x_tile(qg, i, hxq, t_hxq_i, qTst, t_qTst):
                tt = 2 + qg * 4 + i
                src, bsel, tsrc = tok_src(cx, b, tt, 1)
                yield from norm_tile_g(cx, 1, bsel, src, xt_ring, scr, hxq, i * 128, t_hxq_i, nb(), small, tsrc)
                qf, t_qf = qf_r.next()
                for half in range(2):
                    pt, tp = nb()
                    for k in range(8):
                        sy.op("pe", lambda e: e.matmul(pt[:], hxq[:, k, i * 128:(i + 1) * 128], inw[:, k, half * 512:(half + 1) * 512],
                                                       start=(k == 0), stop=(k == 7)), reads=[t_hxq_i, t_inw], writes=[tp])
                    yield
                    sy.op("act", lambda e: e.activation(out=qf[:, half * 512:(half + 1) * 512], in_=pt[:], func=AF.Copy), reads=[tp], writes=[t_qf])
                yield
                qr, t_qr = qr_r.next()
                yield from head_norm_rope(cx, qf, t_qf, 8, gq, t_c, (rcos, rsin), tt - 2, qr, t_qr, tmpset())
                pt2, tp2 = nb()
                ptb = pt2[:].bitcast(BF16)
                for h in range(8):
                    sy.op("pe", lambda e: e.transpose(out=ptb[:, h * 128:(h + 1) * 128], in_=qr[:, h * 128:(h + 1) * 128], identity=cx.ident_bf[:]),
                          reads=[t_qr, cx.t_const], writes=[tp2])
                yield
                sy.op("act", lambda e: e.activation(out=qTst[:, :, i * 128:(i + 1) * 128], in_=ptb[:].rearrange("p (h c) -> p h c", c=128), func=AF.Copy),
                      reads=[tp2], writes=[t_qTst])
                yield
                yield from kv_tile(tt, hxq, t_hxq_i, i * 128)

            def gate_fm(b, qg, hxq, t_hxq):
                for ft in range(8):
                    pt, tp = nb()
                    for k in range(8):
                        sy.op("pe", lambda e: e.matmul(pt[:], inw[:, k, 1536 + ft * 128:1536 + (ft + 1) * 128], hxq[:, k, :], start=(k == 0), stop=(k == 7)),
                              reads=t_hxq + [t_inw], writes=[tp])
                    yield
                    gst, t_gst = gst_r.next()
                    sy.op("act", lambda e: e.activation(out=gst[:], in_=pt[:], func=AF.Silu), reads=[tp], writes=[t_gst])
                    sy.dma("pool", dram["agT"][b, ft * 128:(ft + 1) * 128, qg * 512:(qg + 1) * 512], gst[:], reads=[t_gst], writes=[cx.tdram["agT"]])
                    yield
                    yield
                    yield

            interleave([ctx_tile(0), ctx_tile(1)] + gate_gens)
            gate_gens.clear()
            for qg in range(4):
                hxq, _ = hxq_r.next()
                t_hxq = [T() for _ in range(4)]
                qTst, t_qTst = qTst_r.next()
                interleave([x_tile(qg, i, hxq, t_hxq[i], qTst, t_qTst) for i in range(4)] + gate_gens)
                gate_gens.clear()
                sy.dma("pool", dram["aqT"][b].rearrange("h d t -> d h t")[:, :, qg * 512:(qg + 1) * 512], qTst[:], reads=[t_qTst], writes=[cx.tdram["aqT"]])
                gate_gens.append(gate_fm(b, qg, hxq, t_hxq))
            sy.dma("pool", dram["akT"][b].rearrange("g d t -> d g t"), kTst[:], reads=[t_kTst], writes=[cx.tdram["akT"]])
            sy.dma("pool", dram["av"][b].rearrange("(t p) c -> p t c", p=128), vst[:], reads=[t_vst], writes=[cx.tdram["av"]])
        interleave(gate_gens)


def l1_attn(cx):
    sy, nc, NB, dram = cx.sy, cx.nc, cx.NB, cx.dram
    ks = 128.0 ** -0.5
    KSPLIT = 9
    with contextlib.ExitStack() as ph:
        mk_eps(cx, ph)
        ow = sy.sb("aow", [128, 8, D], BF16, ph)
        t_ow = T()
        load_w_bf16(cx, ow, dram["at_out_w"], D, t_ow)
        fng = sy.sb("fng", [128, D], F32, ph)
        ones = sy.sb("ones", [128, 128], BF16, ph)
        t_c = T()
        sy.dma("sp", fng[:], dram["final_norm_g"].partition_broadcast(128), writes=[t_c])
        sy.op("pool", lambda e: e.memset(ones[:], 1.0), writes=[t_c])
        mhalf = sy.sb("mhalf", [128, 1], F32, ph)
        sy.op("pool", lambda e: e.memset(mhalf[:], -0.5), writes=[t_c])
        kT_r = Ring(sy, ph, "akT", 1, [128, 2, TOK], BF16)
        v_r = Ring(sy, ph, "av", 1, [128, NT, 256], BF16)
        qT_r = Ring(sy, ph, "aqT", 2, [128, 8, 512], BF16)
        sg_r = Ring(sy, ph, "asg", 2, [128, 8, 512], BF16)
        ogT_r = Ring(sy, ph, "ogT", 2, [128, 8, 512], BF16)
        PTh = [sy.sb(f"PTh{i}", [128, NT, 512], BF16, ph) for i in range(3)]
        t_PTh = [[T() for _ in range(NT)] for _ in range(3)]
        pab_r = [Ring(sy, ph, "pabA", 2, [128, 512], BF16), Ring(sy, ph, "pabB", 2, [128, 512], BF16)]
        rden_r = Ring(sy, ph, "rden", 2, [128, 512], F32)
        pacc_r = Ring(sy, ph, "pacc", 2, [128, 512], F32)
        tmo_r = Ring(sy, ph, "tmo", 2, [128, 512], F32)
        xt_r = Ring(sy, ph, "xta", 4, [128, D], F32)
        x2_r = Ring(sy, ph, "x2", 4, [128, D], F32)
        sqj_r = Ring(sy, ph, "sqj", 2, [128, D], F32)
        small = Ring(sy, ph, "sma", 8, [128, 4], F32)
        gx_r = Ring(sy, ph, "gxl", 2, [128, D], F32)
        pS = [cx.psum[0], cx.psum[1], cx.psum[2], cx.psum[3]]
        pO = [cx.psum[4], cx.psum[5]]
        pDs = [cx.psum[6]]
        pX = [cx.psum[7]]
        LOOK = 3
        KPOOL = NT
        cnt = {"s": 0, "o": 0, "x": 0, "h": 0, "d": 0}

        def nb(lst, key):
            r = lst[cnt[key] % len(lst)]
            cnt[key] += 1
            return r

        pend = []
        bg = []

        def bg_step(n=1):
            for _ in range(n):
                if not bg:
                    return
                try:
                    next(bg[0])
                except StopIteration:
                    bg.pop(0)

        def out_tile(b, qg, i, ogT, t_ogT, gx, t_gx):
            tt = 2 + qg * 4 + i
            xt, t_x = xt_r.next()
            sy.dma("sp", xt[:], dram["x1"][b, tt * 128:(tt + 1) * 128, :], reads=[cx.tdram["x1"]], writes=[t_x])
            x2, t_x2 = x2_r.next()
            for half in range(2):
                pt, tp = nb(pX, "x")
                hs = slice(half * 512, (half + 1) * 512)
                for k in range(8):
                    sy.op("pe", lambda e: e.matmul(pt[:], ogT[:, k, i * 128:(i + 1) * 128], ow[:, k, hs], start=(k == 0), stop=(k == 7)),
                          reads=[t_ogT, t_ow], writes=[tp])
                    if k % 2 == 1:
                        yield
                sy.op("dve", lambda e: e.tensor_tensor(out=x2[:, hs], in0=pt[:], in1=gx[:, hs], op=ALU.mult), reads=[tp, t_gx], writes=[t_x2])
                yield
            sy.op("pool", lambda e: e.tensor_tensor(out=x2[:], in0=x2[:], in1=xt[:], op=ALU.add), reads=[t_x2, t_x], writes=[t_x2])
            yield
            ss, t_ss = small.next()
            sqj, t_sqj = sqj_r.next()
            for _ in range(6):
                yield
            sy.op("act", lambda e: e.activation(out=sqj[:], in_=x2[:], func=AF.Square, accum_out=ss[:, 0:1]), reads=[t_x2], writes=[t_sqj, t_ss])
            yield
            sy.op("act", lambda e: e.activation(out=ss[:, 1:2], in_=ss[:, 0:1], func=AF.Ln, scale=1.0 / D, bias=cx.eps_c[:, 0:1]),
                  reads=[t_ss, cx.t_eps], writes=[t_ss])
            sy.op("act", lambda e: e.activation(out=ss[:, 2:3], in_=ss[:, 1:2], func=AF.Exp, scale=-0.5), reads=[t_ss], writes=[t_ss])
            yield
            sy.op("dve", lambda e: e.scalar_tensor_tensor(out=x2[:], in0=x2[:], scalar=ss[:, 2:3], in1=fng[:], op0=ALU.mult, op1=ALU.mult),
                  reads=[t_x2, t_ss, t_c], writes=[t_x2])
            yield
            sy.dma("pool", cx.out[b, (tt - 2) * 128:(tt - 1) * 128, :], x2[:], reads=[t_x2], writes=[cx.tout])
            yield

        def out_group(b, qg, ogT, t_ogT, gx, t_gx):
            for i0 in (0, 1, 2, 3):
                gens = [out_tile(b, qg, i, ogT, t_ogT, gx, t_gx) for i in (i0,)]
                while gens:
                    for g in list(gens):
                        try:
                            next(g)
                        except StopIteration:
                            gens.remove(g)
                        yield

        for b in range(NB):
            gx, t_gx = gx_r.next()
            gate_rows(cx, 1, b, gx, t_gx)
            kT, t_kT = kT_r.next()
            v, t_v = v_r.next()
            sy.dma("sp", kT[:], dram["akT"][b].rearrange("g d t -> d g t"), reads=[cx.tdram["akT"]], writes=[t_kT])
            sy.dma("sp", v[:], dram["av"][b].rearrange("(t p) c -> p t c", p=128), reads=[cx.tdram["av"]], writes=[t_v])
            for qg in range(4):
                qT, t_qT = qT_r.next()
                sg, t_sg = sg_r.next()
                sy.dma("sp", qT[:], dram["aqT"][b].rearrange("h d t -> d h t")[:, :, qg * 512:(qg + 1) * 512], reads=[cx.tdram["aqT"]], writes=[t_qT])
                sy.dma("sp", sg[:], dram["agT"][b].rearrange("(f p) t -> p f t", p=128)[:, :, qg * 512:(qg + 1) * 512], reads=[cx.tdram["agT"]], writes=[t_sg])
                while len(bg) > 1:
                    bg_step()
                ogT, t_ogT = ogT_r.next()
                steps = [(h, kt) for h in range(8) for kt in range(NT)]
                hbuf = {}

                def emit_S(s):
                    h, kt = steps[s]
                    if kt == 0:
                        hbuf[h] = cnt["h"] % 3
                        cnt["h"] += 1
                    hb = hbuf[h]
                    ps_, tps = nb(pS, "s")
                    sy.op("pe", lambda e: e.matmul(ps_[:], kT[:, h // 4, kt * 128:(kt + 1) * 128], qT[:, h, :], start=True, stop=True),
                          reads=[t_kT, t_qT], writes=[tps])
                    sy.op("act", lambda e: e.activation(out=PTh[hb][:, kt, :], in_=ps_[:], func=AF.Exp, scale=ks), reads=[tps], writes=[t_PTh[hb][kt]])

                head = {}
                hst = {}
                DCH = ((0, 3), (3, 6))
                P0, P1 = 6, 12

                def stage_b(hh, po, tpo, pab2, t_pab2, ogT=ogT, t_ogT=t_ogT, sg=sg, t_sg=t_sg):
                    pd, tpd = pDs[0]
                    sy.op("pe", lambda e: e.matmul(pd[:], ones[:], pab2[:], start=False, stop=True), reads=[t_pab2, t_c], writes=[tpd])
                    rden, t_rden = rden_r.next()
                    sy.op("dve", lambda e: e.reciprocal(out=rden[:], in_=pd[:]), reads=[tpd], writes=[t_rden])
                    tmo, t_tmo = tmo_r.next()
                    sy.op("dve", lambda e: e.tensor_tensor(out=tmo[:], in0=po[:], in1=rden[:], op=ALU.mult), reads=[tpo, t_rden], writes=[t_tmo])
                    sy.op("pool", lambda e: e.tensor_tensor(out=ogT[:, hh, :], in0=tmo[:], in1=sg[:, hh, :], op=ALU.mult), reads=[t_tmo, t_sg], writes=[t_ogT])

                for s0 in range(LOOK):
                    emit_S(s0)
                for s, (h, kt) in enumerate(steps):
                    if s + LOOK < len(steps):
                        emit_S(s + LOOK)
                    if kt == 0:
                        head[h] = nb(pO, "o")
                        hst[h] = {"first": True}
                    if kt == 4 and pend:
                        fn, after = pend.pop(0)
                        fn()
                        if after:
                            after()
                    po, tpo = head[h]
                    hb = hbuf[h]
                    g = h // 4
                    sy.op("pe", lambda e: e.matmul(po[:], v[:, kt, g * 128:(g + 1) * 128], PTh[hb][:, kt, :], start=(kt == 0), stop=(kt == NT - 1)),
                          reads=[t_v, t_PTh[hb][kt]], writes=[tpo])
                    pd, tpd = pDs[0]
                    st_ = hst[h]
                    for (k0, k1) in DCH:
                        if kt == k1 - 1:
                            pab, t_pab = pab_r[0].next()
                            with nc.allow_low_precision(reason="bf16 matmul operand; reduction itself runs in fp32"):
                                sy.op("dve", lambda e: e.tensor_reduce(out=pab[:], in_=PTh[hb][:, k0:k1, :].rearrange("p k q -> p q k"), axis=AX.X, op=ALU.add),
                                      reads=t_PTh[hb][k0:k1], writes=[t_pab])
                            st_.setdefault("pabs", []).append((pab, t_pab))
                    if kt == P0 + 1:
                        st_["pacc"] = pacc_r.next()
                        pacc, t_pacc = st_["pacc"]
                        sy.op("pool", lambda e: e.tensor_tensor(out=pacc[:], in0=PTh[hb][:, P0, :], in1=PTh[hb][:, P0 + 1, :], op=ALU.add),
                              reads=t_PTh[hb][P0:P0 + 2], writes=[t_pacc])
                    elif P0 + 1 < kt < P1:
                        pacc, t_pacc = st_["pacc"]
                        if kt == P1 - 1:
                            st_["pab2"] = pab_r[1].next()
                            dst, t_dst = st_["pab2"]
                        else:
                            dst, t_dst = pacc, t_pacc
                        sy.op("pool", lambda e: e.tensor_tensor(out=dst[:], in0=pacc[:], in1=PTh[hb][:, kt, :], op=ALU.add),
                              reads=[t_pacc, t_PTh[hb][kt]], writes=[t_dst])
                    if kt == 8:
                        for (pab, t_pab) in st_["pabs"]:
                            sy.op("pe", lambda e: e.matmul(pd[:], ones[:], pab[:], start=st_["first"], stop=False), reads=[t_pab, t_c], writes=[tpd])
                            st_["first"] = False
                    if kt >= P1:
                        sy.op("pe", lambda e: e.matmul(pd[:], ones[:], PTh[hb][:, kt, :], start=False, stop=False), reads=[t_PTh[hb][kt], t_c], writes=[tpd])
                    if kt == NT - 1:
                        pab2, t_pab2 = st_["pab2"]
                        after = None
                        if h == 7:
                            after = (lambda b=b, qg=qg, ogT=ogT, t_ogT=t_ogT, gx=gx, t_gx=t_gx: bg.append(out_group(b, qg, ogT, t_ogT, gx, t_gx)))
                        pend.append([(lambda hh=h, po=po, tpo=tpo, pab2=pab2, t_pab2=t_pab2, fb=stage_b: fb(hh, po, tpo, pab2, t_pab2)), after])
                    bg_step()
        while pend:
            fn, after = pend.pop(0)
            fn()
            if after:
                after()
        while bg:
            bg_step()


def make_in_maps(inp, NB, ncores):
    hc = host_consts(NB)
    f = lambda a: np.ascontiguousarray(np.asarray(a, dtype=np.float32))
    shared = {
        "norm_g": f(inp["norm_g"]), "final_norm_g": f(inp["final_norm_g"]), "ada_w": f(inp["ada_w"]), "ada_b": f(inp["ada_b"]),
        "er_in_w": f(inp["er_in_w"][0]), "er_out_w": f(inp["er_out_w"][0]),
        "convw": f(np.asarray(inp["hy_conv_w"][0]).reshape(3, 12, 128).transpose(2, 1, 0)),
        "convb": f(np.asarray(inp["hy_conv_b"][0]).reshape(12, 128).T),
        "hy_f_w1": f(inp["hy_f_w1"][0]), "hy_f_b1": f(np.asarray(inp["hy_f_b1"][0]).reshape(64, 1)),
        "hy_f_freq": f(np.asarray(inp["hy_f_freq"][0]).reshape(64, 1)), "hy_f_w2": f(inp["hy_f_w2"][0]),
        "hy_f_b2": f(np.asarray(inp["hy_f_b2"][0]).reshape(64, 1)), "hy_f_w3": f(inp["hy_f_w3"][0]),
        "hy_bias": f(np.asarray(inp["hy_bias"][0]).reshape(4, 128).T),
        "retdec": f(np.concatenate([np.asarray(inp["ret_decay_f"][0]), np.asarray(inp["ret_decay_b"][0])])),
        "ret_norm_g": f(inp["ret_norm_g"][0]),
        "at_in_w": f(inp["at_in_w"][0]), "at_out_w": f(inp["at_out_w"][0]),
        "at_q_norm": f(inp["at_q_norm"][0]), "at_k_norm": f(inp["at_k_norm"][0]),
    }
    for k, v in hc.items():
        shared["k_" + k] = v
    x = np.asarray(inp["x"], dtype=np.float32)
    ctx = np.asarray(inp["ctx"], dtype=np.float32)
    c = np.asarray(inp["c"], dtype=np.float32)
    cc = np.asarray(inp["c_ctx"], dtype=np.float32)
    maps = []
    for i in range(ncores):
        sl = slice(i * NB, (i + 1) * NB)
        cfull = np.concatenate([c[sl], cc[None, :]], axis=0)
        cT = np.ascontiguousarray(cfull.reshape(NB + 1, 8, 128).transpose(2, 1, 0))
        m = dict(shared)
        m["x"] = np.ascontiguousarray(x[sl])
        m["ctx"] = np.ascontiguousarray(ctx[sl])
        m["cT"] = cT
        maps.append(m)
    return maps


_NC_CACHE = {}


def kernel(**inputs):
    NB = inputs["x"].shape[0] // NCORES
    if NB not in _NC_CACHE:
        _NC_CACHE[NB] = build(NB)
    nc = _NC_CACHE[NB]
    maps = make_in_maps(inputs, NB, NCORES)
    res = run_bass_kernel_spmd(nc, maps, core_ids=list(range(NCORES)))
    return np.concatenate([r["out"] for r in res.results], axis=0).astype(np.float32)
```

```python
import contextlib
import math
import numpy as np
import ml_dtypes
import concourse.bass as bass
import concourse.mybir as mybir
from concourse.bass_utils import run_bass_kernel_spmd

F32 = mybir.dt.float32
BF16 = mybir.dt.bfloat16
AF = mybir.ActivationFunctionType
ALU = mybir.AluOpType
AX = mybir.AxisListType

NCORES = 8
D = 1024
L = 2048
LC = 256
TOK = L + LC
NT = TOK // 128
EPS = 1e-6
NFFT = 2 * L
NFFTC = 2 * LC
PI = math.pi


class T:
    __slots__ = ("name", "w", "r")

    def __init__(self, name="t"):
        self.name = name
        self.w = None
        self.r = {}


class Sy:
    NS = 8

    def __init__(self, nc, stack):
        self.nc = nc
        self.stack = stack
        self.E = {"pe": nc.tensor, "dve": nc.vector, "act": nc.scalar, "pool": nc.gpsimd, "sp": nc.sync}
        self.sems = {}
        self.cnt = {}
        self.seen = {e: {} for e in self.E}
        for e in ("pe", "dve", "act", "pool"):
            self.sems[("c", e)] = stack.enter_context(nc.semaphore("c_" + e))
            self.cnt[e] = 0
        self.dn = {}
        for q in ("sp", "act", "pool"):
            self.dn[q] = 0
            for i in range(self.NS):
                self.sems[("d", q, i)] = stack.enter_context(nc.semaphore(f"d_{q}_{i}"))

    _uid = 0

    def sb(self, name, shape, dt, stack=None):
        Sy._uid += 1
        return (stack or self.stack).enter_context(self.nc.sbuf_tensor(f"s{Sy._uid}_{name}", list(shape), dt))

    def ps(self, name, shape, dt, stack=None):
        Sy._uid += 1
        return (stack or self.stack).enter_context(self.nc.psum_tensor(f"p{Sy._uid}_{name}", list(shape), dt))

    def _wait(self, eng, deps):
        best = {}
        for k, v in deps:
            if best.get(k, 0) < v:
                best[k] = v
        sn = self.seen[eng]
        for k, v in best.items():
            if sn.get(k, 0) >= v:
                continue
            self.E[eng].wait_ge(self.sems[k], v)
            sn[k] = v

    def _deps(self, eng_key, reads, writes, is_dma):
        deps = []
        for t in reads:
            if t.w is not None:
                if t.w[0] == eng_key and eng_key == ("c", "pe"):
                    continue
                deps.append(t.w)
        pe = eng_key == ("c", "pe")
        for t in writes:
            if t.w is not None and not (pe and t.w[0] == eng_key):
                deps.append(t.w)
            for k, v in t.r.items():
                if not (pe and k == eng_key):
                    deps.append((k, v))
        return deps

    def _reg(self, me, reads, writes):
        k, v = me
        for t in reads:
            t.r[k] = v
        for t in writes:
            t.w = me
            t.r = {}

    def op(self, eng, fn, reads=(), writes=()):
        key = ("c", eng)
        self._wait(eng, self._deps(key, reads, writes, False))
        ins = fn(self.E[eng])
        self.cnt[eng] += 1
        ins.then_inc(self.sems[key], 1)
        self._reg((key, self.cnt[eng]), reads, writes)
        return ins

    def dma(self, q, out, in_, reads=(), writes=(), **kw):
        n = self.dn[q]
        i = n % self.NS
        tgt = 16 * (n // self.NS + 1)
        key = ("d", q, i)
        deps = self._deps(key, reads, writes, True)
        if tgt > 16:
            deps.append((key, tgt - 16))
        self._wait(q, deps)
        ins = self.E[q].dma_start(out=out, in_=in_, **kw)
        ins.then_inc(self.sems[key], 16)
        self.dn[q] = n + 1
        self._reg((key, tgt), reads, writes)
        return ins

    def barrier(self):
        allk = []
        for e in ("pe", "dve", "act", "pool"):
            if self.cnt[e] > 0:
                allk.append((("c", e), self.cnt[e]))
        for q in ("sp", "act", "pool"):
            n = self.dn[q]
            for i in range(self.NS):
                c = (n - i + self.NS - 1) // self.NS if n > i else 0
                if c > 0:
                    allk.append((("d", q, i), 16 * c))
        for e in self.E:
            self._wait(e, [kv for kv in allk if kv[0] != ("c", e)])

    def finish(self, outs):
        self._wait("sp", [t.w for t in outs if t.w is not None])


class Ring:
    def __init__(self, sy, stack, name, n, shape, dt, psum=False):
        self.items = []
        for i in range(n):
            t = (sy.ps if psum else sy.sb)(f"{name}{i}", shape, dt, stack)
            self.items.append((t, T(f"{name}{i}")))
        self.i = 0

    def next(self):
        it = self.items[self.i % len(self.items)]
        self.i += 1
        return it


_CONST = {}


def _bf(a):
    return np.ascontiguousarray(a).astype(ml_dtypes.bfloat16)


def host_consts(NB):
    if NB in _CONST:
        return _CONST[NB]
    c = {}
    c["ident_bf"] = _bf(np.eye(128))
    c["ident_f"] = np.eye(128, dtype=np.float32)
    sel = np.zeros((NB + 1, NB + 1, 128), np.float32)
    for b in range(NB + 1):
        sel[b, b, :] = 1.0
    c["sel"] = sel.reshape(NB + 1, (NB + 1) * 128)
    def dft(Ls, N):
        t = np.arange(Ls, dtype=np.float64)[:, None]
        k = np.arange(N // 2, dtype=np.float64)[None, :]
        th = 2.0 * np.pi * (k + 0.5) * t / N
        return np.cos(th), -np.sin(th)
    C, S = dft(L, NFFT)
    fw = np.zeros((8, 128, 16, 512), np.float32)
    for g in range(8):
        blk = np.concatenate([C[:, 256 * g:256 * g + 256], S[:, 256 * g:256 * g + 256]], axis=1)
        fw[g] = blk.reshape(16, 128, 512).transpose(1, 0, 2)
    c["ffwd"] = _bf(fw)
    Fi = np.concatenate([C.T, S.T], axis=0) * (2.0 / NFFT)
    fi = np.zeros((4, 4, 128, 8, 512), np.float32)
    for ng in range(4):
        for fq in range(4):
            blk = Fi[fq * 1024:(fq + 1) * 1024, ng * 512:(ng + 1) * 512]
            fi[ng, fq] = blk.reshape(8, 128, 512).transpose(1, 0, 2)
    c["finv"] = _bf(fi)
    Cc, Sc = dft(LC, NFFTC)
    blk = np.concatenate([Cc, Sc], axis=1)
    c["ffwd_c"] = _bf(blk.reshape(2, 128, 512).transpose(1, 0, 2))
    Fic = np.concatenate([Cc.T, Sc.T], axis=0) * (2.0 / NFFTC)
    c["finv_c"] = _bf(Fic.reshape(4, 128, 256).transpose(1, 0, 2))
    def zfeat(Ls):
        n = np.arange(Ls, dtype=np.float32)
        t = n / np.float32(max(Ls - 1, 1))
        bands = np.linspace(1e-4, 15, 16, dtype=np.float32)
        ang = np.float32(2.0 * math.pi / Ls) * n[:, None] * bands[None, :]
        z = np.concatenate([t[:, None], np.cos(ang), -np.sin(ang)], axis=-1).astype(np.float32)
        maxd = math.log(1e-2) / 0.3
        mind = math.log(1e-2) / 1.5
        deltas = np.abs(np.linspace(mind, maxd, 512, dtype=np.float32))
        win = np.exp(-t[:, None] * deltas[None, :]).astype(np.float32)
        return z, win
    z, win = zfeat(L)
    c["zT"] = np.ascontiguousarray(z.T)
    c["win"] = np.ascontiguousarray(win.reshape(16, 128, 512).transpose(1, 0, 2))
    zc, winc = zfeat(LC)
    c["zcT"] = np.ascontiguousarray(zc.T)
    c["winc"] = np.ascontiguousarray(winc.reshape(2, 128, 512).transpose(1, 0, 2))
    rows = L // 64
    r = np.repeat(np.arange(rows, dtype=np.float32), 64)
    col = np.tile(np.arange(64, dtype=np.float32), rows)
    inv = (10000.0 ** (-np.arange(0, 64, 2, dtype=np.float32) / 64)).astype(np.float32)
    ang = np.concatenate([r[:, None] * inv, col[:, None] * inv], axis=-1)
    c["rcos"] = np.ascontiguousarray(np.cos(ang).astype(np.float32).reshape(16, 128, 64).transpose(1, 0, 2))
    c["rsin"] = np.ascontiguousarray(np.sin(ang).astype(np.float32).reshape(16, 128, 64).transpose(1, 0, 2))
    j = np.arange(128, dtype=np.float32)[:, None]
    i = np.arange(128, dtype=np.float32)[None, :]
    d = i - j
    rt = np.stack([np.maximum(d, 0), np.maximum(-d, 0), (d >= 0).astype(np.float32), (d <= 0).astype(np.float32),
                   np.broadcast_to(i + 1, (128, 128)), np.broadcast_to(128 - i, (128, 128))], axis=1)
    c["rtab"] = np.ascontiguousarray(rt.astype(np.float32))
    c["rcol"] = np.ascontiguousarray(np.stack([127 - j[:, 0], j[:, 0]], axis=1).astype(np.float32))
    _CONST[NB] = c
    return c


class Cx:
    pass


def build(NB, phases=("ada", "p1", "p2", "p3", "p4", "l1"), debug=()):
    nc = bass.Bass("TRN2", target_bir_lowering=False)
    hc = host_consts(NB)
    cx = Cx()
    cx.nc = nc
    cx.NB = NB
    NB1 = NB + 1
    dram = {}

    def din(name, shape, dt=F32):
        dram[name] = nc.dram_tensor(name, list(shape), dt, kind="ExternalInput").ap()
        return dram[name]

    def dscr(name, shape, dt):
        kind = "ExternalOutput" if name in debug else "Internal"
        dram[name] = nc.dram_tensor(name, list(shape), dt, kind=kind).ap()
        return dram[name]

    din("x", [NB, L, D]); din("ctx", [NB, LC, D]); din("cT", [128, 8, NB1])
    din("norm_g", [2, D]); din("final_norm_g", [D]); din("ada_w", [2, D, 3 * D]); din("ada_b", [2, 3 * D])
    din("er_in_w", [D, 4096]); din("er_out_w", [D, D])
    din("convw", [128, 12, 3]); din("convb", [128, 12])
    din("hy_f_w1", [33, 64]); din("hy_f_b1", [64, 1]); din("hy_f_freq", [64, 1]); din("hy_f_w2", [64, 64])
    din("hy_f_b2", [64, 1]); din("hy_f_w3", [64, 1024]); din("hy_bias", [128, 4])
    din("retdec", [8]); din("ret_norm_g", [512])
    din("at_in_w", [D, 2560]); din("at_out_w", [D, D]); din("at_q_norm", [128]); din("at_k_norm", [128])
    for k, v in hc.items():
        din("k_" + k, v.shape, BF16 if v.dtype == ml_dtypes.bfloat16 else F32)
    out = nc.dram_tensor("out", [NB, L, D], F32, kind="ExternalOutput").ap()
    dscr("u_tm", [NB, TOK, 512], BF16); dscr("u_fm", [NB, 512, TOK], BF16); dscr("w_fm", [NB, 512, TOK], BF16)
    dscr("qT", [NB, 512, TOK], BF16); dscr("kT", [NB, 512, TOK], BF16)
    dscr("k_tm", [NB, TOK, 512], BF16); dscr("v_tm", [NB, TOK, 512], BF16); dscr("g_tm", [NB, TOK, 512], BF16)
    dscr("yT", [NB, D, TOK], BF16)
    dscr("aqT", [NB, 8, 128, L], BF16); dscr("akT", [NB, 2, 128, TOK], BF16)
    dscr("av", [NB, TOK, 256], BF16); dscr("agT", [NB, D, L], BF16)
    dscr("x1", [NB, TOK, D], F32)
    dscr("dbg_mod", [2, NB1, 3 * D], F32)
    dscr("modrows", [2, NB1, 3 * D], F32)
    dscr("dbg_h", [L, 1024], F32)
    cx.dram = dram
    cx.out = out
    cx.tdram = {k: T("dram_" + k) for k in dram}
    cx.tout = T("out")

    with contextlib.ExitStack() as st:
        sy = Sy(nc, st)
        cx.sy = sy
        cx.ident_bf = sy.sb("ident_bf", [128, 128], BF16)
        cx.ident_f = sy.sb("ident_f", [128, 128], F32)
        cx.modFM = [sy.sb(f"modFM{l}", [128, 24, NB1], F32) for l in range(2)]
        cx.t_const = T("const")
        cx.t_mod = [T("mod0"), T("mod1")]
        sy.dma("sp", cx.ident_bf[:], dram["k_ident_bf"], writes=[cx.t_const])
        sy.dma("sp", cx.ident_f[:], dram["k_ident_f"], writes=[cx.t_const])
        cx.psum = []
        for i in range(8):
            cx.psum.append((sy.ps(f"pb{i}", [128, 512], F32), T(f"pb{i}")))

        if "ada" in phases:
            phase_ada(cx)
            sy.barrier()
        if "p1" in phases:
            phase1(cx)
            sy.barrier()
        if "p2" in phases:
            phase2(cx)
            sy.barrier()
        if "p3" in phases:
            phase3(cx)
            sy.barrier()
        if "p4" in phases:
            phase4(cx)
            sy.barrier()
        if "l1" in phases:
            layer1(cx)
        if "l1p" in phases:
            l1_prep(cx)
        if "l1a" in phases:
            l1_attn(cx)
        outs = [cx.tout] + [cx.tdram[k] for k in debug]
        sy.barrier()
        sy.finish(outs)
    return nc


def phase_ada(cx):
    sy, nc, NB, dram = cx.sy, cx.nc, cx.NB, cx.dram
    NB1 = NB + 1
    with contextlib.ExitStack() as ph:
        scT = sy.sb("scT", [128, 8, NB1], F32, ph)
        t_sc = T()
        sy.dma("sp", scT[:], dram["cT"], writes=[t_sc])
        sy.op("act", lambda e: e.activation(out=scT[:], in_=scT[:], func=AF.Silu), reads=[t_sc], writes=[t_sc])
        wsb = sy.sb("adaw", [128, 8, 3 * D], F32, ph)
        bb = sy.sb("adab", [NB1, 3 * D], F32, ph)
        ngb = sy.sb("ngb", [NB1, D], F32, ph)
        mTM = [sy.sb(f"modTM{l}", [NB1, 3 * D], F32, ph) for l in range(2)]
        for l in range(2):
            t_w, t_b = T(), T()
            for k in range(8):
                sy.dma("sp", wsb[:, k, :], dram["ada_w"][l, k * 128:(k + 1) * 128, :], writes=[t_w])
            sy.dma("sp", bb[:], dram["ada_b"][l].partition_broadcast(NB1), writes=[t_b])
            sy.dma("sp", ngb[:], dram["norm_g"][l].partition_broadcast(NB1), writes=[t_b])
            m = mTM[l]
            tm = cx.t_mod[l]
            for n in range(6):
                pt, tp = cx.psum[n % 2]
                for k in range(8):
                    sy.op("pe", lambda e, k=k: e.matmul(pt[0:NB1, :], scT[:, k, :], wsb[:, k, n * 512:(n + 1) * 512],
                                                        start=(k == 0), stop=(k == 7)), reads=[t_sc, t_w], writes=[tp])
                sy.op("dve", lambda e: e.tensor_tensor(out=m[:, n * 512:(n + 1) * 512], in0=pt[0:NB1, :],
                                                       in1=bb[:, n * 512:(n + 1) * 512], op=ALU.add),
                      reads=[tp, t_b], writes=[tm])
            sy.op("dve", lambda e: e.scalar_tensor_tensor(out=m[:, D:2 * D], in0=m[:, D:2 * D], scalar=1.0, in1=ngb[:],
                                                          op0=ALU.add, op1=ALU.mult), reads=[tm, t_b], writes=[tm])
            for j in range(24):
                pt, tp = cx.psum[2 + j % 2]
                sy.op("pe", lambda e: e.transpose(out=pt[:, 0:NB1], in_=m[:, j * 128:(j + 1) * 128],
                                                  identity=cx.ident_f[0:NB1, 0:NB1]), reads=[tm, cx.t_const], writes=[tp])
                sy.op("dve", lambda e: e.tensor_copy(out=cx.modFM[l][:, j, :], in_=pt[:, 0:NB1]), reads=[tp], writes=[tm])
            sy.dma("sp", dram["modrows"][l], m[:], reads=[tm], writes=[cx.tdram["modrows"]])


def load_w_bf16(cx, dst, src, ncols, t_w):
    for k in range(8):
        for c0 in range(0, ncols, 2048):
            c1 = min(ncols, c0 + 2048)
            cx.sy.dma("pool", dst[:, k, c0:c1], src[k * 128:(k + 1) * 128, c0:c1], writes=[t_w])


def interleave(gens):
    gens = list(gens)
    while gens:
        for g in list(gens):
            try:
                next(g)
            except StopIteration:
                gens.remove(g)


def norm_tile_g(cx, l, bsel, src_ap, xt_ring, scr, hxT, tok0, t_h, pbank, small, t_src=None, keep=None):
    sy = cx.sy
    xt, t_x = xt_ring.next()
    if keep is not None:
        keep.append((xt, t_x))
    sy.dma("sp", xt[:], src_ap, reads=[t_src] if t_src else [], writes=[t_x])
    ss, t_ss = small.next()
    sq, t_sq = scr.next()
    sy.op("act", lambda e: e.activation(out=sq[:], in_=xt[:], func=AF.Square, accum_out=ss[:, 0:1]),
          reads=[t_x], writes=[t_sq, t_ss])
    yield
    sy.op("act", lambda e: e.activation(out=ss[:, 1:2], in_=ss[:, 0:1], func=AF.Sqrt, scale=1.0 / D, bias=cx.eps_c[:, 0:1]),
          reads=[t_ss, cx.t_eps], writes=[t_ss])
    yield
    sy.op("dve", lambda e: e.reciprocal(out=ss[:, 2:3], in_=ss[:, 1:2]), reads=[t_ss], writes=[t_ss])
    yield
    sy.op("dve", lambda e: e.tensor_scalar(out=sq[:], in0=xt[:], scalar1=ss[:, 2:3], scalar2=None, op0=ALU.mult),
          reads=[t_x, t_ss], writes=[t_sq])
    yield
    pt, tp = pbank
    ptb = pt[:].bitcast(BF16)
    for k in range(8):
        sy.op("pe", lambda e: e.transpose(out=ptb[:, k * 128:(k + 1) * 128], in_=sq[:, k * 128:(k + 1) * 128],
                                          identity=cx.ident_bf[:]), reads=[t_sq, cx.t_const], writes=[tp])
    yield
    mf = cx.modFM[l]
    for k in range(8):
        if k % 2 == 0:
            sy.op("act", lambda e: e.activation(out=hxT[:, k, tok0:tok0 + 128], in_=ptb[:, k * 128:(k + 1) * 128],
                                                func=AF.Identity, scale=mf[:, 8 + k, bsel:bsel + 1], bias=mf[:, k, bsel:bsel + 1]),
                  reads=[tp, cx.t_mod[l]], writes=[t_h])
        else:
            sy.op("dve", lambda e: e.tensor_scalar(out=hxT[:, k, tok0:tok0 + 128], in0=ptb[:, k * 128:(k + 1) * 128],
                                                   scalar1=mf[:, 8 + k, bsel:bsel + 1], scalar2=mf[:, k, bsel:bsel + 1],
                                                   op0=ALU.mult, op1=ALU.add), reads=[tp, cx.t_mod[l]], writes=[t_h])
    yield


def norm_tile(cx, l, bsel, src_ap, xt_ring, scr, hxT, tok0, t_h, pbank, small, t_src=None):
    for _ in norm_tile_g(cx, l, bsel, src_ap, xt_ring, scr, hxT, tok0, t_h, pbank, small, t_src):
        pass


def mk_eps(cx, ph):
    cx.eps_c = cx.sy.sb("eps_c", [128, 1], F32, ph)
    cx.t_eps = T()
    cx.sy.op("dve", lambda e: e.memset(cx.eps_c[:], EPS), writes=[cx.t_eps])


def tok_src(cx, b, tt, l):
    d = cx.dram
    if l == 0:
        if tt < 2:
            return d["ctx"][b, tt * 128:(tt + 1) * 128, :], cx.NB, None
        return d["x"][b, (tt - 2) * 128:(tt - 1) * 128, :], b, None
    return d["x1"][b, tt * 128:(tt + 1) * 128, :], (cx.NB if tt < 2 else b), cx.tdram["x1"]


def phase1(cx):
    sy, nc, NB, dram = cx.sy, cx.nc, cx.NB, cx.dram
    with contextlib.ExitStack() as ph:
        mk_eps(cx, ph)
        inw = sy.sb("inw", [128, 8, 4096], BF16, ph)
        t_inw = T()
        load_w_bf16(cx, inw, dram["er_in_w"], 4096, t_inw)
        convw = sy.sb("convw", [128, 12, 3], F32, ph)
        convb = sy.sb("convb", [128, 12], F32, ph)
        t_cw = T()
        sy.dma("sp", convw[:], dram["convw"], writes=[t_cw])
        sy.dma("sp", convb[:], dram["convb"], writes=[t_cw])
        hxT = sy.sb("hxT", [128, 8, TOK], BF16, ph)
        t_h = [T() for _ in range(NT)]
        xt_ring = Ring(sy, ph, "xt", 4, [128, D], F32)
        scr = Ring(sy, ph, "scr", 4, [128, D], BF16)
        small = Ring(sy, ph, "sm", 8, [128, 4], F32)
        rowp_ring = Ring(sy, ph, "rowp", 2, [128, 2307], F32)
        for rp, t_rp in rowp_ring.items:
            for pos in (0, 257, 2306):
                sy.op("dve", lambda e: e.memset(rp[:, pos:pos + 1], 0.0), writes=[t_rp])
        acc_ring = Ring(sy, ph, "acc", 2, [128, 2305], F32)
        sg = sy.sb("sg", [128, 2305], F32, ph)
        t_sg = T()
        sy.op("dve", lambda e: e.memset(sg[:, 256:257], 0.0), writes=[t_sg])
        urow_ring = Ring(sy, ph, "urow", 1, [128, 2305], BF16)
        wrow_ring = Ring(sy, ph, "wrow", 1, [128, 2305], BF16)
        utm_ring = Ring(sy, ph, "utm", 2, [128, NT, 128], BF16)
        row_ring = Ring(sy, ph, "qkrow", 2, [128, TOK], BF16)
        st_ring = Ring(sy, ph, "stg", 4, [128, 512], BF16)
        pacc = [cx.psum[i] for i in range(7)]
        pi = [0]

        def next_bank():
            r = pacc[pi[0] % 7]
            pi[0] += 1
            return r

        tgs = [(0, 256, 1)] + [(256 + g * 512, 512, 258 + g * 512) for g in range(4)]
        for b in range(NB):
            def ntile(tt):
                src, bsel, tsrc = tok_src(cx, b, tt, 0)
                yield from norm_tile_g(cx, 0, bsel, src, xt_ring, scr, hxT, tt * 128, t_h[tt], next_bank(), small, tsrc)
            for t0 in range(0, NT, 4):
                interleave([ntile(tt) for tt in range(t0, min(NT, t0 + 4))])
            deferred = []
            for ct in range(4):
                accs = {}
                for ft, role in ((4 + ct, "x1"), (8 + ct, "v"), (ct, "x0"), (12 + ct, "hg")):
                    if role != "hg":
                        rp, t_rp = rowp_ring.next()
                    for (tok0, n, rc0) in tgs:
                        pt, tp = next_bank()
                        hts = t_h[tok0 // 128:(tok0 + n) // 128]
                        for k in range(8):
                            sy.op("pe", lambda e: e.matmul(pt[:, 0:n], inw[:, k, ft * 128:(ft + 1) * 128], hxT[:, k, tok0:tok0 + n],
                                                           start=(k == 0), stop=(k == 7)), reads=[t_inw] + hts, writes=[tp])
                        if role == "hg":
                            sy.op("act", lambda e: e.activation(out=sg[:, rc0 - 1:rc0 - 1 + n], in_=pt[:, 0:n], func=AF.Silu),
                                  reads=[tp], writes=[t_sg])
                        else:
                            sy.op("act", lambda e: e.activation(out=rp[:, rc0:rc0 + n], in_=pt[:, 0:n], func=AF.Copy),
                                  reads=[tp], writes=[t_rp])
                    if role == "hg":
                        continue
                    ac, t_ac = acc_ring.next()
                    cw = ft
                    sy.op("dve", lambda e: e.tensor_scalar(out=ac[:], in0=rp[:, 1:2306], scalar1=convw[:, cw, 1:2], scalar2=convb[:, cw:cw + 1],
                                                           op0=ALU.mult, op1=ALU.add), reads=[t_rp, t_cw], writes=[t_ac])
                    sy.op("dve", lambda e: e.scalar_tensor_tensor(out=ac[:], in0=rp[:, 0:2305], scalar=convw[:, cw, 0:1], in1=ac[:],
                                                                  op0=ALU.mult, op1=ALU.add), reads=[t_rp, t_cw, t_ac], writes=[t_ac])
                    sy.op("dve", lambda e: e.scalar_tensor_tensor(out=ac[:], in0=rp[:, 2:2307], scalar=convw[:, cw, 2:3], in1=ac[:],
                                                                  op0=ALU.mult, op1=ALU.add), reads=[t_rp, t_cw, t_ac], writes=[t_ac])
                    accs[role] = (ac, t_ac)
                    if role == "v":
                        ur, t_ur = urow_ring.next()
                        a1, t1 = accs["x1"]
                        sy.op("pool", lambda e: e.tensor_tensor(out=ur[:], in0=a1[:], in1=ac[:], op=ALU.mult),
                              reads=[t1, t_ac], writes=[t_ur])
                        for (d0, n, s0) in ((0, 256, 0), (256, 2048, 257)):
                            sy.dma("pool", dram["u_fm"][b, ct * 128:(ct + 1) * 128, d0:d0 + n], ur[:, s0:s0 + n],
                                   reads=[t_ur], writes=[cx.tdram["u_fm"]])
                        def do_utm(ur=ur, t_ur=t_ur, ct=ct):
                            um, t_um = utm_ring.next()
                            pt, tp = cx.psum[7]
                            ptb = pt[:].bitcast(BF16)
                            for t0 in range(0, NT, 8):
                                nn = min(8, NT - t0)
                                for ti in range(nn):
                                    tt = t0 + ti
                                    c0 = tt * 128 if tt < 2 else 257 + (tt - 2) * 128
                                    sy.op("pe", lambda e: e.transpose(out=ptb[:, ti * 128:(ti + 1) * 128], in_=ur[:, c0:c0 + 128],
                                                                      identity=cx.ident_bf[:]), reads=[t_ur, cx.t_const], writes=[tp])
                                sy.op("act", lambda e: e.activation(out=um[:, t0:t0 + nn, :], in_=ptb[:, 0:nn * 128].rearrange("p (a c) -> p a c", c=128),
                                                                    func=AF.Copy), reads=[tp], writes=[t_um])
                            sy.dma("pool", dram["u_tm"][b].rearrange("(t p) c -> p t c", p=128)[:, :, ct * 128:(ct + 1) * 128], um[:],
                                   reads=[t_um], writes=[cx.tdram["u_tm"]])
                        deferred.append(do_utm)
                while deferred:
                    deferred.pop(0)()
                wr, t_wr = wrow_ring.next()
                a0, t0_ = accs["x0"]
                sy.op("pool", lambda e: e.tensor_tensor(out=wr[:], in0=a0[:], in1=sg[:], op=ALU.mult), reads=[t0_, t_sg], writes=[t_wr])
                for (d0, n, s0) in ((0, 256, 0), (256, 2048, 257)):
                    sy.dma("pool", dram["w_fm"][b, ct * 128:(ct + 1) * 128, d0:d0 + n], wr[:, s0:s0 + n],
                           reads=[t_wr], writes=[cx.tdram["w_fm"]])
            for ft in range(8):
                row, t_row = row_ring.next()
                for (tok0, n, _) in tgs:
                    pt, tp = next_bank()
                    hts = t_h[tok0 // 128:(tok0 + n) // 128]
                    for k in range(8):
                        sy.op("pe", lambda e: e.matmul(pt[:, 0:n], inw[:, k, 2048 + ft * 128:2048 + (ft + 1) * 128], hxT[:, k, tok0:tok0 + n],
                                                       start=(k == 0), stop=(k == 7)), reads=[t_inw] + hts, writes=[tp])
                    if (tok0 // 256) % 2 == 0:
                        sy.op("act", lambda e: e.activation(out=row[:, tok0:tok0 + n], in_=pt[:, 0:n], func=AF.Copy), reads=[tp], writes=[t_row])
                    else:
                        sy.op("dve", lambda e: e.tensor_copy(out=row[:, tok0:tok0 + n], in_=pt[:, 0:n]), reads=[tp], writes=[t_row])
                dst = dram["qT"] if ft < 4 else dram["kT"]
                tdst = cx.tdram["qT"] if ft < 4 else cx.tdram["kT"]
                h = ft % 4
                sy.dma("pool", dst[b, h * 128:(h + 1) * 128, :], row[:], reads=[t_row], writes=[tdst])
            for tt in range(NT):
                for g, nm in enumerate(("k_tm", "v_tm", "g_tm")):
                    pt, tp = next_bank()
                    c0 = 2560 + g * 512
                    for k in range(8):
                        sy.op("pe", lambda e: e.matmul(pt[:], hxT[:, k, tt * 128:(tt + 1) * 128], inw[:, k, c0:c0 + 512],
                                                       start=(k == 0), stop=(k == 7)), reads=[t_inw, t_h[tt]], writes=[tp])
                    sg_, t_st = st_ring.next()
                    if nm == "g_tm":
                        sy.op("act", lambda e: e.activation(out=sg_[:], in_=pt[:], func=AF.Silu), reads=[tp], writes=[t_st])
                    elif nm == "k_tm":
                        sy.op("dve", lambda e: e.tensor_copy(out=sg_[:], in_=pt[:]), reads=[tp], writes=[t_st])
                    else:
                        sy.op("act", lambda e: e.activation(out=sg_[:], in_=pt[:], func=AF.Copy), reads=[tp], writes=[t_st])
                    sy.dma("pool", dram[nm][b, tt * 128:(tt + 1) * 128, :], sg_[:], reads=[t_st], writes=[cx.tdram[nm]])


def phase2(cx):
    sy, nc, NB, dram = cx.sy, cx.nc, cx.NB, cx.dram
    NC_ = NT
    order_f = list(range(NC_))
    order_b = [1, 0] + list(range(NC_ - 1, 1, -1))
    with contextlib.ExitStack() as ph:
        mk_eps(cx, ph)
        rtab = sy.sb("rtab", [128, 6, 128], F32, ph)
        rcol = sy.sb("rcol", [128, 2], F32, ph)
        dec = sy.sb("dec", [128, 8], F32, ph)
        lg = sy.sb("lg", [128, 8], F32, ph)
        rng = sy.sb("rng", [128, 512], F32, ph)
        t_c = T()
        sy.dma("sp", rtab[:], dram["k_rtab"], writes=[t_c])
        sy.dma("sp", rcol[:], dram["k_rcol"], writes=[t_c])
        sy.dma("sp", dec[:], dram["retdec"].partition_broadcast(128), writes=[t_c])
        sy.dma("sp", rng[:], dram["ret_norm_g"].partition_broadcast(128), writes=[t_c])
        sy.op("act", lambda e: e.activation(out=lg[:], in_=dec[:], func=AF.Exp), reads=[t_c], writes=[t_c])
        sy.op("dve", lambda e: e.tensor_scalar(out=lg[:], in0=lg[:], scalar1=-1.0, scalar2=None, op0=ALU.mult), reads=[t_c], writes=[t_c])
        maskT = sy.sb("maskT", [128, 4, 128], F32, ph)
        qd = sy.sb("qd", [128, 8, 128], F32, ph)
        kd = sy.sb("kd", [128, 8], F32, ph)
        cd = sy.sb("cd", [128, 8], F32, ph)
        e1 = sy.sb("e1", [128, 128], F32, ph)
        e2 = sy.sb("e2", [128, 128], F32, ph)
        ks = 128.0 ** -0.5
        for h in range(4):
            sy.op("act", lambda e: e.activation(out=e1[:], in_=rtab[:, 0, :], func=AF.Exp, scale=lg[:, h:h + 1]), reads=[t_c], writes=[t_c])
            sy.op("act", lambda e: e.activation(out=e2[:], in_=rtab[:, 1, :], func=AF.Exp, scale=lg[:, 4 + h:5 + h]), reads=[t_c], writes=[t_c])
            sy.op("dve", lambda e: e.scalar_tensor_tensor(out=e1[:], in0=e1[:], scalar=ks, in1=rtab[:, 2, :], op0=ALU.mult, op1=ALU.mult), reads=[t_c], writes=[t_c])
            sy.op("dve", lambda e: e.scalar_tensor_tensor(out=e2[:], in0=e2[:], scalar=ks, in1=rtab[:, 3, :], op0=ALU.mult, op1=ALU.mult), reads=[t_c], writes=[t_c])
            sy.op("dve", lambda e: e.tensor_tensor(out=maskT[:, h, :], in0=e1[:], in1=e2[:], op=ALU.add), reads=[t_c], writes=[t_c])
            sy.op("act", lambda e: e.activation(out=qd[:, h, :], in_=rtab[:, 4, :], func=AF.Exp, scale=lg[:, h:h + 1]), reads=[t_c], writes=[t_c])
            sy.op("act", lambda e: e.activation(out=qd[:, 4 + h, :], in_=rtab[:, 5, :], func=AF.Exp, scale=lg[:, 4 + h:5 + h]), reads=[t_c], writes=[t_c])
            sy.op("act", lambda e: e.activation(out=kd[:, h:h + 1], in_=rcol[:, 0:1], func=AF.Exp, scale=lg[:, h:h + 1]), reads=[t_c], writes=[t_c])
            sy.op("act", lambda e: e.activation(out=kd[:, 4 + h:5 + h], in_=rcol[:, 1:2], func=AF.Exp, scale=lg[:, 4 + h:5 + h]), reads=[t_c], writes=[t_c])
        sy.op("act", lambda e: e.activation(out=cd[:], in_=lg[:], func=AF.Exp, scale=128.0), reads=[t_c], writes=[t_c])
        sy.op("dve", lambda e: e.tensor_scalar(out=kd[:], in0=kd[:], scalar1=ks, scalar2=None, op0=ALU.mult), reads=[t_c], writes=[t_c])

        qT_r = Ring(sy, ph, "qTh", 2, [128, TOK], BF16)
        kT_r = Ring(sy, ph, "kTh", 2, [128, TOK], BF16)
        k_r = Ring(sy, ph, "kh", 2, [128, NT, 128], BF16)
        v_r = Ring(sy, ph, "vh", 2, [128, NT, 128], BF16)
        g_r = Ring(sy, ph, "gh", 2, [128, NT, 128], BF16)
        qf_r = Ring(sy, ph, "qf", 2, [128, NT, 128], BF16)
        qb_r = Ring(sy, ph, "qb", 2, [128, NT, 128], BF16)
        kf_r = Ring(sy, ph, "kf", 2, [128, NT, 128], BF16)
        kb_r = Ring(sy, ph, "kb", 2, [128, NT, 128], BF16)
        gg_r = Ring(sy, ph, "gg", 2, [128, NT, 128], F32)
        S32_r = [Ring(sy, ph, f"S32_{i}", 2, [128, NT, 128], F32) for i in range(2)]
        Sbf_r = [Ring(sy, ph, f"Sbf_{i}", 2, [128, NT, 128], BF16) for i in range(2)]
        PT_r = Ring(sy, ph, "PT", 2, [128, 4, 128], BF16)
        oc_r = Ring(sy, ph, "oc", 2, [128, 4, 128], F32)
        sq_r = Ring(sy, ph, "sq", 2, [128, 4, 128], F32)
        y_r = Ring(sy, ph, "yy", 2, [128, 4, 128], BF16)
        st_r = Ring(sy, ph, "hst", 4, [128, 16], F32)
        yT_r = Ring(sy, ph, "yTh", 2, [128, TOK], BF16)
        pST = [cx.psum[0], cx.psum[1]]
        pO = [cx.psum[2], cx.psum[3]]
        pTr = [cx.psum[4], cx.psum[5]]
        pD = [cx.psum[6], cx.psum[7]]
        cnt = {"st": 0, "o": 0, "d": 0, "tr": 0}

        def nb(lst, key):
            r = lst[cnt[key] % len(lst)]
            cnt[key] += 1
            return r

        for b in range(NB):
            for h in range(4):
                qT, t_q = qT_r.next(); kT, t_k = kT_r.next(); kh, t_kh = k_r.next(); vh, t_vh = v_r.next(); gh, t_gh = g_r.next()
                hs = slice(h * 128, (h + 1) * 128)
                sy.dma("sp", qT[:], dram["qT"][b, hs, :], reads=[cx.tdram["qT"]], writes=[t_q])
                sy.dma("sp", kT[:], dram["kT"][b, hs, :], reads=[cx.tdram["kT"]], writes=[t_k])
                for (dst, t_d, nm) in ((kh, t_kh, "k_tm"), (vh, t_vh, "v_tm"), (gh, t_gh, "g_tm")):
                    sy.dma("sp", dst[:], dram[nm][b].rearrange("(t p) c -> p t c", p=128)[:, :, hs], reads=[cx.tdram[nm]], writes=[t_d])
                qf, t_qf = qf_r.next(); qb, t_qb = qb_r.next(); kf, t_kf = kf_r.next(); kb, t_kb = kb_r.next(); gg, t_gg = gg_r.next()
                q3 = qT[:].rearrange("p (t c) -> p t c", c=128)
                sy.op("dve", lambda e: e.tensor_scalar(out=kf[:], in0=kh[:], scalar1=kd[:, h:h + 1], scalar2=None, op0=ALU.mult),
                      reads=[t_kh, t_c], writes=[t_kf])
                sy.op("dve", lambda e: e.tensor_scalar(out=kb[:], in0=kh[:], scalar1=kd[:, 4 + h:5 + h], scalar2=None, op0=ALU.mult),
                      reads=[t_kh, t_c], writes=[t_kb])
                sy.op("pool", lambda e: e.tensor_tensor(out=qf[:], in0=q3, in1=qd[:, h, :].unsqueeze(1).to_broadcast([128, NT, 128]), op=ALU.mult),
                      reads=[t_q, t_c], writes=[t_qf])
                sy.op("pool", lambda e: e.tensor_tensor(out=qb[:], in0=q3, in1=qd[:, 4 + h, :].unsqueeze(1).to_broadcast([128, NT, 128]), op=ALU.mult),
                      reads=[t_q, t_c], writes=[t_qb])
                sy.op("pool", lambda e: e.tensor_tensor(out=gg[:], in0=gh[:], in1=rng[:, hs].unsqueeze(1).to_broadcast([128, NT, 128]), op=ALU.mult),
                      reads=[t_gh, t_c], writes=[t_gg])
                S32 = [S32_r[0].next(), S32_r[1].next()]
                Sbf = [Sbf_r[0].next(), Sbf_r[1].next()]
                orders = (order_f, order_b)
                kxs = ((kf, t_kf), (kb, t_kb))
                for di in range(2):
                    c0 = orders[di][0]
                    sy.op("pool", lambda e: e.memset(S32[di][0][:, c0, :], 0.0), writes=[S32[di][1]])
                for idx in range(NC_ - 1):
                    pt, tp = nb(pD, "d")
                    for di in range(2):
                        c = orders[di][idx]
                        sy.op("pe", lambda e: e.matmul(pt[:, di * 128:(di + 1) * 128], kxs[di][0][:, c, :], vh[:, c, :], start=True, stop=True),
                              reads=[kxs[di][1], t_vh], writes=[tp])
                    for di in range(2):
                        c = orders[di][idx]
                        cn = orders[di][idx + 1]
                        sy.op("dve", lambda e: e.scalar_tensor_tensor(out=S32[di][0][:, cn, :], in0=S32[di][0][:, c, :], scalar=cd[:, 4 * di + h:4 * di + h + 1],
                                                                      in1=pt[:, di * 128:(di + 1) * 128], op0=ALU.mult, op1=ALU.add),
                              reads=[tp, S32[di][1], t_c], writes=[S32[di][1]])
                for di in range(2):
                    sy.op("act", lambda e: e.activation(out=Sbf[di][0][:], in_=S32[di][0][:], func=AF.Copy), reads=[S32[di][1]], writes=[Sbf[di][1]])
                yT, t_yT = yT_r.next()

                def group(g0):
                    n = min(4, NC_ - g0)
                    pst, tpst = nb(pST, "st")
                    for ci in range(n):
                        c = g0 + ci
                        sy.op("pe", lambda e: e.matmul(pst[:, ci * 128:(ci + 1) * 128], kT[:, c * 128:(c + 1) * 128], qT[:, c * 128:(c + 1) * 128],
                                                       start=True, stop=True), reads=[t_k, t_q], writes=[tpst])
                    yield
                    PT, t_PT = PT_r.next()
                    sy.op("dve", lambda e: e.tensor_tensor(out=PT[:, 0:n, :], in0=pst[:, 0:n * 128].rearrange("p (a c) -> p a c", c=128),
                                                           in1=maskT[:, h, :].unsqueeze(1).to_broadcast([128, n, 128]), op=ALU.mult),
                          reads=[tpst, t_c], writes=[t_PT])
                    yield
                    po, tpo = nb(pO, "o")
                    for ci in range(n):
                        c = g0 + ci
                        osl = po[:, ci * 128:(ci + 1) * 128]
                        sy.op("pe", lambda e: e.matmul(osl, PT[:, ci, :], vh[:, c, :], start=True, stop=False), reads=[t_PT, t_vh], writes=[tpo])
                        sy.op("pe", lambda e: e.matmul(osl, qf[:, c, :], Sbf[0][0][:, c, :], start=False, stop=False), reads=[t_qf, Sbf[0][1]], writes=[tpo])
                        sy.op("pe", lambda e: e.matmul(osl, qb[:, c, :], Sbf[1][0][:, c, :], start=False, stop=True), reads=[t_qb, Sbf[1][1]], writes=[tpo])
                    yield
                    o3 = po[:, 0:n * 128].rearrange("p (a c) -> p a c", c=128)
                    stt, t_st = st_r.next()
                    oc, t_oc = oc_r.next(); sq, t_sq = sq_r.next(); yy, t_yy = y_r.next()
                    sy.op("dve", lambda e: e.tensor_reduce(out=stt[:, 0:n], in_=o3, axis=AX.X, op=ALU.add), reads=[tpo], writes=[t_st])
                    yield
                    sy.op("dve", lambda e: e.tensor_scalar(out=stt[:, 4:4 + n], in0=stt[:, 0:n], scalar1=-1.0 / 128, scalar2=None, op0=ALU.mult),
                          reads=[t_st], writes=[t_st])
                    yield
                    sy.op("dve", lambda e: e.tensor_tensor(out=oc[:, 0:n, :], in0=o3, in1=stt[:, 4:4 + n].unsqueeze(2).to_broadcast([128, n, 128]), op=ALU.add),
                          reads=[tpo, t_st], writes=[t_oc])
                    yield
                    sy.op("pool", lambda e: e.tensor_tensor(out=sq[:, 0:n, :], in0=oc[:, 0:n, :], in1=oc[:, 0:n, :], op=ALU.mult), reads=[t_oc], writes=[t_sq])
                    yield
                    sy.op("dve", lambda e: e.tensor_reduce(out=stt[:, 8:8 + n], in_=sq[:, 0:n, :], axis=AX.X, op=ALU.add), reads=[t_sq], writes=[t_st])
                    yield
                    sy.op("act", lambda e: e.activation(out=stt[:, 8:8 + n], in_=stt[:, 8:8 + n], func=AF.Sqrt, scale=1.0 / 128, bias=cx.eps_c[:, 0:1]),
                          reads=[t_st, cx.t_eps], writes=[t_st])
                    yield
                    sy.op("dve", lambda e: e.reciprocal(out=stt[:, 12:12 + n], in_=stt[:, 8:8 + n]), reads=[t_st], writes=[t_st])
                    yield
                    sy.op("dve", lambda e: e.tensor_tensor(out=oc[:, 0:n, :], in0=oc[:, 0:n, :], in1=stt[:, 12:12 + n].unsqueeze(2).to_broadcast([128, n, 128]), op=ALU.mult),
                          reads=[t_oc, t_st], writes=[t_oc])
                    yield
                    sy.op("pool", lambda e: e.tensor_tensor(out=yy[:, 0:n, :], in0=oc[:, 0:n, :], in1=gg[:, g0:g0 + n, :], op=ALU.mult),
                          reads=[t_oc, t_gg], writes=[t_yy])
                    yield
                    ptr, tptr = nb(pTr, "tr")
                    ptb = ptr[:].bitcast(BF16)
                    for ci in range(n):
                        sy.op("pe", lambda e: e.transpose(out=ptb[:, ci * 128:(ci + 1) * 128], in_=yy[:, ci, :], identity=cx.ident_bf[:]),
                              reads=[t_yy, cx.t_const], writes=[tptr])
                    yield
                    sy.op("act", lambda e: e.activation(out=yT[:, g0 * 128:(g0 + n) * 128], in_=ptb[:, 0:n * 128], func=AF.Copy),
                          reads=[tptr], writes=[t_yT])
                    yield

                interleave([group(0), group(4)])
                interleave([group(8), group(12)])
                interleave([group(16)])
                sy.dma("pool", dram["yT"][b, 512 + h * 128:512 + (h + 1) * 128, :], yT[:], reads=[t_yT], writes=[cx.tdram["yT"]])


def phase3(cx):
    sy, nc, NB, dram = cx.sy, cx.nc, cx.NB, cx.dram
    MAGIC = 12582912.0
    with contextlib.ExitStack() as ph:
        Kx = sy.sb("Kx", [128, 32, 512], BF16, ph)
        Kc = sy.sb("Kc", [128, 4, 512], BF16, ph)
        t_K = T()
        Fc = sy.sb("Fc", [128, 2, 512], BF16, ph)
        Fic = sy.sb("Fic", [128, 4, 256], BF16, ph)
        hbias = sy.sb("hbias", [128, 4], F32, ph)
        t_fc = T()
        sy.dma("sp", Fc[:], dram["k_ffwd_c"], writes=[t_fc])
        sy.dma("sp", Fic[:], dram["k_finv_c"], writes=[t_fc])
        sy.dma("sp", hbias[:], dram["hy_bias"], writes=[t_fc])
        F_r = Ring(sy, ph, "Fb", 2, [128, 16, 512], BF16)
        with contextlib.ExitStack() as fs:
            w1 = sy.sb("w1", [33, 64], F32, fs); w2 = sy.sb("w2", [64, 64], F32, fs); w3 = sy.sb("w3", [64, 1024], F32, fs)
            fb = sy.sb("fb", [64, 6], F32, fs)
            t_w = T()
            sy.dma("sp", w1[:], dram["hy_f_w1"], writes=[t_w]); sy.dma("sp", w2[:], dram["hy_f_w2"], writes=[t_w])
            sy.dma("sp", w3[:], dram["hy_f_w3"], writes=[t_w])
            sy.dma("sp", fb[:, 0:1], dram["hy_f_b1"], writes=[t_w]); sy.dma("sp", fb[:, 1:2], dram["hy_f_freq"], writes=[t_w])
            sy.dma("sp", fb[:, 2:3], dram["hy_f_b2"], writes=[t_w])
            sy.op("dve", lambda e: e.tensor_tensor(out=fb[:, 3:4], in0=fb[:, 0:1], in1=fb[:, 1:2], op=ALU.mult), reads=[t_w], writes=[t_w])
            sy.op("dve", lambda e: e.tensor_tensor(out=fb[:, 4:5], in0=fb[:, 2:3], in1=fb[:, 1:2], op=ALU.mult), reads=[t_w], writes=[t_w])
            zT = sy.sb("zT", [33, L], F32, fs)
            h1 = sy.sb("h1", [64, L], F32, fs)
            h2 = sy.sb("h2", [64, L], F32, fs)
            hfb = sy.sb("hfb", [128, 16, 1024], BF16, fs)
            tA = sy.sb("tA", [64, 512], F32, fs)
            tB = sy.sb("tB", [64, 512], F32, fs)
            win_r = Ring(sy, fs, "win", 2, [128, 512], F32)
            tmpA_r = Ring(sy, fs, "tmpA", 2, [128, 512], F32)
            for (Ls, zname, wname, Kdst, nct) in ((L, "k_zT", "k_win", Kx, 16), (LC, "k_zcT", "k_winc", Kc, 2)):
                t_z, t_h1, t_h2, t_hfb, t_t = T(), T(), T(), T(), T()
                sy.dma("sp", zT[:, 0:Ls], dram[zname], writes=[t_z])

                def sin_layer(wm, kdim, src, t_src, dst, t_dst, bcol):
                    for n0 in range(0, Ls, 512):
                        n = min(512, Ls - n0)
                        pt, tp = cx.psum[(n0 // 512) % 2]
                        sy.op("pe", lambda e: e.matmul(pt[0:64, 0:n], wm[0:kdim, :], src[0:kdim, n0:n0 + n], start=True, stop=True),
                              reads=[t_w, t_src], writes=[tp])
                        sy.op("dve", lambda e: e.tensor_scalar(out=tA[:, 0:n], in0=pt[0:64, 0:n], scalar1=fb[:, 1:2], scalar2=fb[:, bcol:bcol + 1],
                                                               op0=ALU.mult, op1=ALU.add), reads=[tp, t_w], writes=[t_t])
                        sy.op("dve", lambda e: e.tensor_scalar(out=tB[:, 0:n], in0=tA[:, 0:n], scalar1=1.0 / (2 * PI), scalar2=MAGIC,
                                                               op0=ALU.mult, op1=ALU.add), reads=[t_t], writes=[t_t])
                        sy.op("dve", lambda e: e.tensor_scalar(out=tB[:, 0:n], in0=tB[:, 0:n], scalar1=-MAGIC, scalar2=None, op0=ALU.add),
                              reads=[t_t], writes=[t_t])
                        sy.op("dve", lambda e: e.scalar_tensor_tensor(out=tA[:, 0:n], in0=tB[:, 0:n], scalar=-2 * PI, in1=tA[:, 0:n],
                                                                      op0=ALU.mult, op1=ALU.add), reads=[t_t], writes=[t_t])
                        sy.op("dve", lambda e: e.tensor_scalar(out=tA[:, 0:n], in0=tA[:, 0:n], scalar1=-PI, scalar2=PI, op0=ALU.max, op1=ALU.min),
                              reads=[t_t], writes=[t_t])
                        sy.op("act", lambda e: e.activation(out=dst[:, n0:n0 + n], in_=tA[:, 0:n], func=AF.Sin), reads=[t_t], writes=[t_dst])

                sin_layer(w1, 33, zT, t_z, h1, t_h1, 3)
                sin_layer(w2, 64, h1, t_h1, h2, t_h2, 4)
                for c in range(nct):
                    wn, t_wn = win_r.next()
                    sy.dma("sp", wn[:], dram[wname][:, c, :], writes=[t_wn])
                    for half in range(2):
                        pt, tp = cx.psum[2 + half]
                        sy.op("pe", lambda e: e.matmul(pt[:], h2[:, c * 128:(c + 1) * 128], w3[:, half * 512:(half + 1) * 512], start=True, stop=True),
                              reads=[t_h2, t_w], writes=[tp])
                        sy.op("dve", lambda e: e.tensor_tensor(out=hfb[:, c, half * 512:(half + 1) * 512], in0=pt[:], in1=wn[:], op=ALU.mult),
                              reads=[tp, t_wn], writes=[t_hfb])
                sy.op("dve", lambda e: e.memset(hfb[0:1, 0, 512:1024], 0.0), reads=[t_hfb], writes=[t_hfb])
                if "dbg_h" in cx.tdram and Ls == L:
                    pass
                ngrp = 8 if Ls == L else 1
                for g in range(ngrp):
                    if Ls == L:
                        Fb, t_F = F_r.next()
                        sy.dma("sp", Fb[:], dram["k_ffwd"][g], writes=[t_F])
                        Fsrc = Fb
                    else:
                        Fsrc, t_F = Fc, t_fc
                    for j in range(4):
                        pA, tpA = cx.psum[4 + (j % 2) * 2]
                        pB, tpB = cx.psum[5 + (j % 2) * 2]
                        for (pp, tpp, c0) in ((pA, tpA, 0), (pB, tpB, 512)):
                            for c in range(nct):
                                sy.op("pe", lambda e: e.matmul(pp[:], Fsrc[:, c, j * 128:(j + 1) * 128], hfb[:, c, c0:c0 + 512],
                                                               start=(c == 0), stop=(c == nct - 1)), reads=[t_F, t_hfb], writes=[tpp])
                        tm, t_tm = tmpA_r.next()
                        sy.op("act", lambda e: e.activation(out=tm[:], in_=pA[:], func=AF.Copy), reads=[tpA], writes=[t_tm])
                        if Ls == L:
                            ti = (2 * g + j) if j < 2 else (16 + 2 * g + j - 2)
                        else:
                            ti = j
                        sy.op("dve", lambda e: e.tensor_tensor(out=Kdst[:, ti, :], in0=tm[:], in1=pB[:], op=(ALU.add if j < 2 else ALU.subtract)),
                              reads=[t_tm, tpB], writes=[t_K])
            sy.barrier()
        utm = sy.sb("utm3", [128, NT, 512], BF16, ph)
        t_utm = T()
        Y = sy.sb("Y", [128, 32, 512], BF16, ph)
        Yc = sy.sb("Yc", [128, 4, 512], BF16, ph)
        t_Y, t_Yc = T(), T()
        Fi_r = Ring(sy, ph, "Fib", 2, [128, 8, 512], BF16)
        us_r = Ring(sy, ph, "us", 4, [128, 512], F32)
        tt_r = Ring(sy, ph, "ttm", 4, [128, 512], F32)
        uw_r = Ring(sy, ph, "uw", 4, [128, 2, 512], BF16)
        yb_r = Ring(sy, ph, "yb", 2, [128, 512], BF16)
        tmp_r = Ring(sy, ph, "tmp3", 2, [128, 512], F32)

        def spec_mul(pre, tpre, pim, tpim, Kt, ire, iim, Yt, t_Yt):
            ure, t_ure = us_r.next(); uim, t_uim = us_r.next()
            sy.op("act", lambda e: e.activation(out=ure[:], in_=pre[:], func=AF.Copy), reads=[tpre], writes=[t_ure])
            sy.op("act", lambda e: e.activation(out=uim[:], in_=pim[:], func=AF.Copy), reads=[tpim], writes=[t_uim])
            t1, t_t1 = tt_r.next(); t2, t_t2 = tt_r.next(); t3, t_t3 = tt_r.next(); t4, t_t4 = tt_r.next()
            sy.op("dve", lambda e: e.tensor_tensor(out=t1[:], in0=ure[:], in1=Kt[:, ire, :], op=ALU.mult), reads=[t_ure, t_K], writes=[t_t1])
            sy.op("dve", lambda e: e.tensor_tensor(out=t2[:], in0=uim[:], in1=Kt[:, iim, :], op=ALU.mult), reads=[t_uim, t_K], writes=[t_t2])
            sy.op("dve", lambda e: e.tensor_tensor(out=Yt[:, ire, :], in0=t1[:], in1=t2[:], op=ALU.subtract), reads=[t_t1, t_t2], writes=[t_Yt])
            sy.op("pool", lambda e: e.tensor_tensor(out=t3[:], in0=ure[:], in1=Kt[:, iim, :], op=ALU.mult), reads=[t_ure, t_K], writes=[t_t3])
            sy.op("pool", lambda e: e.tensor_tensor(out=t4[:], in0=uim[:], in1=Kt[:, ire, :], op=ALU.mult), reads=[t_uim, t_K], writes=[t_t4])
            sy.op("pool", lambda e: e.tensor_tensor(out=Yt[:, iim, :], in0=t3[:], in1=t4[:], op=ALU.add), reads=[t_t3, t_t4], writes=[t_Yt])

        def combine(b, ct, pacc, tpacc, tok0, n):
            uw, t_uw = uw_r.next()
            sy.dma("sp", uw[:, 0, 0:n], dram["u_fm"][b, ct * 128:(ct + 1) * 128, tok0:tok0 + n], reads=[cx.tdram["u_fm"]], writes=[t_uw])
            sy.dma("sp", uw[:, 1, 0:n], dram["w_fm"][b, ct * 128:(ct + 1) * 128, tok0:tok0 + n], reads=[cx.tdram["w_fm"]], writes=[t_uw])
            tm, t_tm = tmp_r.next()
            sy.op("dve", lambda e: e.scalar_tensor_tensor(out=tm[:, 0:n], in0=uw[:, 0, 0:n], scalar=hbias[:, ct:ct + 1], in1=pacc[:, 0:n],
                                                          op0=ALU.mult, op1=ALU.add), reads=[t_uw, tpacc, t_fc], writes=[t_tm])
            yb, t_yb = yb_r.next()
            sy.op("pool", lambda e: e.tensor_tensor(out=yb[:, 0:n], in0=tm[:, 0:n], in1=uw[:, 1, 0:n], op=ALU.mult), reads=[t_tm, t_uw], writes=[t_yb])
            sy.dma("pool", dram["yT"][b, ct * 128:(ct + 1) * 128, tok0:tok0 + n], yb[:, 0:n], reads=[t_yb], writes=[cx.tdram["yT"]])

        for b in range(NB):
            sy.dma("sp", utm[:], dram["u_tm"][b].rearrange("(t p) c -> p t c", p=128), reads=[cx.tdram["u_tm"]], writes=[t_utm])
            for j in range(4):
                pt, tp = cx.psum[j]
                for c in range(2):
                    sy.op("pe", lambda e: e.matmul(pt[:], Fc[:, c, j * 128:(j + 1) * 128], utm[:, c, :], start=(c == 0), stop=(c == 1)),
                          reads=[t_fc, t_utm], writes=[tp])
            for jj in range(2):
                spec_mul(cx.psum[jj][0], cx.psum[jj][1], cx.psum[2 + jj][0], cx.psum[2 + jj][1], Kc, jj, 2 + jj, Yc, t_Yc)
            for ct in range(4):
                pt, tp = cx.psum[4 + ct]
                for f in range(4):
                    sy.op("pe", lambda e: e.matmul(pt[:, 0:256], Yc[:, f, ct * 128:(ct + 1) * 128], Fic[:, f, :], start=(f == 0), stop=(f == 3)),
                          reads=[t_Yc, t_fc], writes=[tp])
                combine(b, ct, pt, tp, 0, 256)
            for g in range(8):
                Fb, t_F = F_r.next()
                sy.dma("sp", Fb[:], dram["k_ffwd"][g], writes=[t_F])
                for j in range(4):
                    pt, tp = cx.psum[j]
                    for c in range(16):
                        sy.op("pe", lambda e: e.matmul(pt[:], Fb[:, c, j * 128:(j + 1) * 128], utm[:, 2 + c, :], start=(c == 0), stop=(c == 15)),
                              reads=[t_F, t_utm], writes=[tp])
                for jj in range(2):
                    spec_mul(cx.psum[jj][0], cx.psum[jj][1], cx.psum[2 + jj][0], cx.psum[2 + jj][1], Kx, 2 * g + jj, 16 + 2 * g + jj, Y, t_Y)
            for ng in range(4):
                for fq in range(4):
                    Fi, t_Fi = Fi_r.next()
                    sy.dma("sp", Fi[:], dram["k_finv"][ng, fq], writes=[t_Fi])
                    for ct in range(4):
                        pt, tp = cx.psum[4 + ct]
                        for f in range(8):
                            sy.op("pe", lambda e: e.matmul(pt[:], Y[:, fq * 8 + f, ct * 128:(ct + 1) * 128], Fi[:, f, :],
                                                           start=(fq == 0 and f == 0), stop=(fq == 3 and f == 7)), reads=[t_Y, t_Fi], writes=[tp])
                for ct in range(4):
                    combine(b, ct, cx.psum[4 + ct][0], cx.psum[4 + ct][1], 256 + ng * 512, 512)


def gate_rows(cx, l, b, gbc, t_g):
    cx.sy.dma("sp", gbc[:], cx.dram["modrows"][l, b, 2 * D:3 * D].partition_broadcast(128), reads=[cx.tdram["modrows"]], writes=[t_g])


def phase4(cx):
    sy, nc, NB, dram = cx.sy, cx.nc, cx.NB, cx.dram
    with contextlib.ExitStack() as ph:
        ow = sy.sb("ow", [128, 8, D], BF16, ph)
        t_ow = T()
        load_w_bf16(cx, ow, dram["er_out_w"], D, t_ow)
        gc = sy.sb("gc", [128, D], F32, ph)
        t_gc = T()
        gate_rows(cx, 0, NB, gc, t_gc)
        gx_r = Ring(sy, ph, "gx", 2, [128, D], F32)
        yTs_r = Ring(sy, ph, "yTs", 2, [128, 8, TOK], BF16)
        xt_r = Ring(sy, ph, "xt4", 3, [128, D], F32)
        tm_r = Ring(sy, ph, "tm4", 3, [128, D], F32)
        bank = [0]
        for b in range(NB):
            gx, t_gx = gx_r.next()
            gate_rows(cx, 0, b, gx, t_gx)
            yTs, t_y = yTs_r.next()
            for k in range(8):
                sy.dma("sp", yTs[:, k, :], dram["yT"][b, k * 128:(k + 1) * 128, :], reads=[cx.tdram["yT"]], writes=[t_y])
            for tt in range(NT):
                src, bsel, tsrc = tok_src(cx, b, tt, 0)
                xt, t_x = xt_r.next()
                sy.dma("sp", xt[:], src, writes=[t_x])
                tm, t_tm = tm_r.next()
                g_, t_g_ = (gc, t_gc) if tt < 2 else (gx, t_gx)
                for half in range(2):
                    pt, tp = cx.psum[bank[0] % 6]
                    bank[0] += 1
                    hs = slice(half * 512, (half + 1) * 512)
                    for k in range(8):
                        sy.op("pe", lambda e: e.matmul(pt[:], yTs[:, k, tt * 128:(tt + 1) * 128], ow[:, k, hs], start=(k == 0), stop=(k == 7)),
                              reads=[t_y, t_ow], writes=[tp])
                    sy.op("dve", lambda e: e.tensor_tensor(out=tm[:, hs], in0=pt[:], in1=g_[:, hs], op=ALU.mult), reads=[tp, t_g_], writes=[t_tm])
                sy.op("pool", lambda e: e.tensor_tensor(out=tm[:], in0=tm[:], in1=xt[:], op=ALU.add), reads=[t_tm, t_x], writes=[t_tm])
                sy.dma("pool", dram["x1"][b, tt * 128:(tt + 1) * 128, :], tm[:], reads=[t_tm], writes=[cx.tdram["x1"]])


def head_norm_rope(cx, src, t_src, nh, gbc, t_c, tabs, pos_tile, dst, t_dst, tmp):
    sy = cx.sy
    sq, t_sq, st, t_st, rc, rd, t_r = tmp
    W = nh * 128
    s3 = src[:, 0:W].rearrange("p (h d) -> p h d", d=128)
    sy.op("pool", lambda e: e.tensor_tensor(out=sq[:, 0:W], in0=src[:, 0:W], in1=src[:, 0:W], op=ALU.mult), reads=[t_src], writes=[t_sq])
    yield
    sy.op("dve", lambda e: e.tensor_reduce(out=st[:, 0:nh], in_=sq[:, 0:W].rearrange("p (h d) -> p h d", d=128), axis=AX.X, op=ALU.add),
          reads=[t_sq], writes=[t_st])
    yield
    sy.op("act", lambda e: e.activation(out=st[:, 8:8 + nh], in_=st[:, 0:nh], func=AF.Sqrt, scale=1.0 / 128, bias=cx.eps_c[:, 0:1]),
          reads=[t_st, cx.t_eps], writes=[t_st])
    yield
    sy.op("dve", lambda e: e.reciprocal(out=st[:, 16:16 + nh], in_=st[:, 8:8 + nh]), reads=[t_st], writes=[t_st])
    yield
    sy.op("dve", lambda e: e.tensor_tensor(out=s3, in0=s3, in1=st[:, 16:16 + nh].unsqueeze(2).to_broadcast([128, nh, 128]), op=ALU.mult),
          reads=[t_src, t_st], writes=[t_src])
    yield
    if pos_tile is None:
        sy.op("pool", lambda e: e.tensor_tensor(out=dst[:, 0:W].rearrange("p (h d) -> p h d", d=128), in0=s3,
                                                in1=gbc[:].unsqueeze(1).to_broadcast([128, nh, 128]), op=ALU.mult), reads=[t_src, t_c], writes=[t_dst])
        yield
        return
    sy.op("pool", lambda e: e.tensor_tensor(out=s3, in0=s3, in1=gbc[:].unsqueeze(1).to_broadcast([128, nh, 128]), op=ALU.mult),
          reads=[t_src, t_c], writes=[t_src])
    yield
    rcos, rsin = tabs
    s4 = src[:, 0:W].rearrange("p (h m two) -> p h m two", two=2, m=64)
    d4 = dst[:, 0:W].rearrange("p (h m two) -> p h m two", two=2, m=64)
    x0, x1 = s4[:, :, :, 0], s4[:, :, :, 1]
    cb = rcos[:, pos_tile, :].unsqueeze(1).to_broadcast([128, nh, 64])
    sb_ = rsin[:, pos_tile, :].unsqueeze(1).to_broadcast([128, nh, 64])
    ra = sq[:, 0:nh * 64].rearrange("p (h m) -> p h m", m=64)
    rb = sq[:, 512:512 + nh * 64].rearrange("p (h m) -> p h m", m=64)
    sy.op("dve", lambda e: e.tensor_tensor(out=ra, in0=x0, in1=cb, op=ALU.mult), reads=[t_src, t_c, t_st], writes=[t_sq])
    sy.op("pool", lambda e: e.tensor_tensor(out=rc[:, 0:nh, :], in0=x0, in1=sb_, op=ALU.mult), reads=[t_src, t_c], writes=[t_r[0]])
    yield
    sy.op("dve", lambda e: e.tensor_tensor(out=rb, in0=x1, in1=sb_, op=ALU.mult), reads=[t_src, t_c], writes=[t_sq])
    sy.op("pool", lambda e: e.tensor_tensor(out=rd[:, 0:nh, :], in0=x1, in1=cb, op=ALU.mult), reads=[t_src, t_c], writes=[t_r[1]])
    yield
    sy.op("dve", lambda e: e.tensor_tensor(out=d4[:, :, :, 0], in0=ra, in1=rb, op=ALU.subtract), reads=[t_sq], writes=[t_dst])
    sy.op("pool", lambda e: e.tensor_tensor(out=d4[:, :, :, 1], in0=rc[:, 0:nh, :], in1=rd[:, 0:nh, :], op=ALU.add),
          reads=[t_r[0], t_r[1]], writes=[t_dst])
    yield


def layer1(cx):
    l1_prep(cx)
    cx.sy.barrier()
    l1_attn(cx)


def l1_prep(cx):
    sy, nc, NB, dram = cx.sy, cx.nc, cx.NB, cx.dram
    with contextlib.ExitStack() as ph:
        mk_eps(cx, ph)
        inw = sy.sb("ainw", [128, 8, 2560], BF16, ph)
        t_inw = T()
        load_w_bf16(cx, inw, dram["at_in_w"], 2560, t_inw)
        rcos = sy.sb("rcos", [128, 16, 64], F32, ph); rsin = sy.sb("rsin", [128, 16, 64], F32, ph)
        gq = sy.sb("gq", [128, 128], F32, ph); gk = sy.sb("gk", [128, 128], F32, ph)
        t_c = T()
        sy.dma("sp", rcos[:], dram["k_rcos"], writes=[t_c]); sy.dma("sp", rsin[:], dram["k_rsin"], writes=[t_c])
        sy.dma("sp", gq[:], dram["at_q_norm"].partition_broadcast(128), writes=[t_c])
        sy.dma("sp", gk[:], dram["at_k_norm"].partition_broadcast(128), writes=[t_c])
        xt_ring = Ring(sy, ph, "xtl", 4, [128, D], F32)
        scr = Ring(sy, ph, "scrl", 4, [128, D], BF16)
        small = Ring(sy, ph, "sml", 8, [128, 4], F32)
        hxa_r = Ring(sy, ph, "hxa", 2, [128, 8, 128], BF16)
        hxq_r = Ring(sy, ph, "hxq", 2, [128, 8, 512], BF16)
        kvf_r = Ring(sy, ph, "kvf", 4, [128, 256], F32)
        qf_r = Ring(sy, ph, "qf32", 4, [128, D], F32)
        qr_r = Ring(sy, ph, "qr", 4, [128, D], BF16)
        kr_r = Ring(sy, ph, "kr", 4, [128, 256], BF16)
        sq_r = Ring(sy, ph, "sqt", 4, [128, D], F32)
        st_r = Ring(sy, ph, "lst", 8, [128, 24], F32)
        rt_r = [Ring(sy, ph, f"rt{i}", 4, [128, 8, 64], F32) for i in range(2)]
        kTst_r = Ring(sy, ph, "kTst", 1, [128, 2, TOK], BF16)
        vst_r = Ring(sy, ph, "vst", 1, [128, NT, 256], BF16)
        qTst_r = Ring(sy, ph, "qTst", 2, [128, 8, 512], BF16)
        gst_r = Ring(sy, ph, "gst", 3, [128, 512], BF16)
        pM = [cx.psum[i] for i in range(8)]
        cnt = {"m": 0}

        def nb():
            r = pM[cnt["m"] % len(pM)]
            cnt["m"] += 1
            return r

        def tmpset():
            sq, t_sq = sq_r.next()
            st, t_st = st_r.next()
            rs = [r.next() for r in rt_r]
            return (sq, t_sq, st, t_st, rs[0][0], rs[1][0], [r[1] for r in rs])

        gate_gens = []
        for b in range(NB):
            kTst, t_kTst = kTst_r.next()
            vst, t_vst = vst_r.next()

            def kv_tile(tt, hx, t_hx, c0):
                pt, tp = nb()
                for k in range(8):
                    sy.op("pe", lambda e: e.matmul(pt[:], hx[:, k, c0:c0 + 128], inw[:, k, 1024:1536], start=(k == 0), stop=(k == 7)),
                          reads=[t_hx, t_inw], writes=[tp])
                yield
                kvf, t_kvf = kvf_r.next()
                sy.op("act", lambda e: e.activation(out=kvf[:], in_=pt[:, 0:256], func=AF.Copy), reads=[tp], writes=[t_kvf])
                sy.op("act", lambda e: e.activation(out=vst[:, tt, :], in_=pt[:, 256:512], func=AF.Copy), reads=[tp], writes=[t_vst])
                yield
                kr, t_kr = kr_r.next()
                yield from head_norm_rope(cx, kvf, t_kvf, 2, gk, t_c, (rcos, rsin), (tt - 2) if tt >= 2 else None, kr, t_kr, tmpset())
                pt2, tp2 = nb()
                ptb = pt2[:].bitcast(BF16)
                for g in range(2):
                    sy.op("pe", lambda e: e.transpose(out=ptb[:, g * 128:(g + 1) * 128], in_=kr[:, g * 128:(g + 1) * 128], identity=cx.ident_bf[:]),
                          reads=[t_kr, cx.t_const], writes=[tp2])
                yield
                sy.op("act", lambda e: e.activation(out=kTst[:, :, tt * 128:(tt + 1) * 128], in_=ptb[:, 0:256].rearrange("p (g c) -> p g c", c=128),
                                                    func=AF.Copy), reads=[tp2], writes=[t_kTst])
                yield

            def ctx_tile(tt):
                src, bsel, tsrc = tok_src(cx, b, tt, 1)
                hxa, t_hxa = hxa_r.next()
                yield from norm_tile_g(cx, 1, bsel, src, xt_ring, scr, hxa, 0, t_hxa, nb(), small, tsrc)
                yield from kv_tile(tt, hxa, t_hxa, 0)

            def x_tile(qg, i, hxq, t_hxq_i, qTst, t_qTst):
                tt = 2 + qg * 4 + i
                src, bsel, tsrc = tok_src(cx, b, tt, 1)
                yield from norm_tile_g(cx, 1, bsel, src, xt_ring, scr, hxq, i * 128, t_hxq_i, nb(), small, tsrc)
                qf, t_qf = qf_r.next()
                for half in range(2):
                    pt, tp = nb()
                    for k in range(8):
                        sy.op("pe", lambda e: e.matmul(pt[:], hxq[:, k, i * 128:(i + 1) * 128], inw[:, k, half * 512:(half + 1) * 512],
                                                       start=(k == 0), stop=(k == 7)), reads=[t_hxq_i, t_inw], writes=[tp])
                    yield
                    sy.op("act", lambda e: e.activation(out=qf[:, half * 512:(half + 1) * 512], in_=pt[:], func=AF.Copy), reads=[tp], writes=[t_qf])
                yield
                qr, t_qr = qr_r.next()
                yield from head_norm_rope(cx, qf, t_qf, 8, gq, t_c, (rcos, rsin), tt - 2, qr, t_qr, tmpset())
                pt2, tp2 = nb()
                ptb = pt2[:].bitcast(BF16)
                for h in range(8):
                    sy.op("pe", lambda e: e.transpose(out=ptb[:, h * 128:(h + 1) * 128], in_=qr[:, h * 128:(h + 1) * 128], identity=cx.ident_bf[:]),
                          reads=[t_qr, cx.t_const], writes=[tp2])
                yield
                sy.op("act", lambda e: e.activation(out=qTst[:, :, i * 128:(i + 1) * 128], in_=ptb[:].rearrange("p (h c) -> p h c", c=128), func=AF.Copy),
                      reads=[tp2], writes=[t_qTst])
                yield
                yield from kv_tile(tt, hxq, t_hxq_i, i * 128)

            def gate_fm(b, qg, hxq, t_hxq):
                for ft in range(8):
                    pt, tp = nb()
                    for k in range(8):
                        sy.op("pe", lambda e: e.matmul(pt[:], inw[:, k, 1536 + ft * 128:1536 + (ft + 1) * 128], hxq[:, k, :], start=(k == 0), stop=(k == 7)),
                              reads=t_hxq + [t_inw], writes=[tp])
                    yield
                    gst, t_gst = gst_r.next()
                    sy.op("act", lambda e: e.activation(out=gst[:], in_=pt[:], func=AF.Silu), reads=[tp], writes=[t_gst])
                    sy.dma("pool", dram["agT"][b, ft * 128:(ft + 1) * 128, qg * 512:(qg + 1) * 512], gst[:], reads=[t_gst], writes=[cx.tdram["agT"]])
                    yield
                    yield
                    yield

            interleave([ctx_tile(0), ctx_tile(1)] + gate_gens)
            gate_gens.clear()
            for qg in range(4):
                hxq, _ = hxq_r.next()
                t_hxq = [T() for _ in range(4)]
                qTst, t_qTst = qTst_r.next()
                interleave([x_tile(qg, i, hxq, t_hxq[i], qTst, t_qTst) for i in range(4)] + gate_gens)
                gate_gens.clear()
                sy.dma("pool", dram["aqT"][b].rearrange("h d t -> d h t")[:, :, qg * 512:(qg + 1) * 512], qTst[:], reads=[t_qTst], writes=[cx.tdram["aqT"]])
                gate_gens.append(gate_fm(b, qg, hxq, t_hxq))
            sy.dma("pool", dram["akT"][b].rearrange("g d t -> d g t"), kTst[:], reads=[t_kTst], writes=[cx.tdram["akT"]])
            sy.dma("pool", dram["av"][b].rearrange("(t p) c -> p t c", p=128), vst[:], reads=[t_vst], writes=[cx.tdram["av"]])
        interleave(gate_gens)


def l1_attn(cx):
    sy, nc, NB, dram = cx.sy, cx.nc, cx.NB, cx.dram
    ks = 128.0 ** -0.5
    KSPLIT = 9
    with contextlib.ExitStack() as ph:
        mk_eps(cx, ph)
        ow = sy.sb("aow", [128, 8, D], BF16, ph)
        t_ow = T()
        load_w_bf16(cx, ow, dram["at_out_w"], D, t_ow)
        fng = sy.sb("fng", [128, D], F32, ph)
        ones = sy.sb("ones", [128, 128], BF16, ph)
        t_c = T()
        sy.dma("sp", fng[:], dram["final_norm_g"].partition_broadcast(128), writes=[t_c])
        sy.op("pool", lambda e: e.memset(ones[:], 1.0), writes=[t_c])
        mhalf = sy.sb("mhalf", [128, 1], F32, ph)
        sy.op("pool", lambda e: e.memset(mhalf[:], -0.5), writes=[t_c])
        kT_r = Ring(sy, ph, "akT", 2, [128, 2, TOK], BF16)
        v_r = Ring(sy, ph, "av", 2, [128, NT, 256], BF16)
        qT_r = Ring(sy, ph, "aqT", 2, [128, 8, 512], BF16)
        sg_r = Ring(sy, ph, "asg", 2, [128, 8, 512], BF16)
        ogT_r = Ring(sy, ph, "ogT", 2, [128, 8, 512], BF16)
        PTh = [sy.sb(f"PTh{i}", [128, NT, 512], BF16, ph) for i in range(3)]
        t_PTh = [[T() for _ in range(NT)] for _ in range(3)]
        pab_r = [Ring(sy, ph, "pabA", 6, [128, 512], BF16)]
        rden_r = Ring(sy, ph, "rden", 2, [128, 512], F32)
        pacc_r = Ring(sy, ph, "pacc", 2, [128, 512], F32)
        tmo_r = Ring(sy, ph, "tmo", 2, [128, 512], F32)
        xt_r = Ring(sy, ph, "xta", 2, [128, D], F32)
        x2_r = Ring(sy, ph, "x2", 2, [128, D], F32)
        sqj_r = Ring(sy, ph, "sqj", 1, [128, D], BF16)
        small = Ring(sy, ph, "sma", 8, [128, 4], F32)
        gx_r = Ring(sy, ph, "gxl", 2, [128, D], F32)
        pS = [cx.psum[0], cx.psum[1], cx.psum[2]]
        pO = [cx.psum[3], cx.psum[4]]
        pDs = [cx.psum[5], cx.psum[6]]
        pX = [cx.psum[7]]
        LOOK = 2
        KPOOL = NT
        cnt = {"s": 0, "o": 0, "x": 0, "h": 0, "d": 0}

        def nb(lst, key):
            r = lst[cnt[key] % len(lst)]
            cnt[key] += 1
            return r

        pend = []
        bg = []

        def bg_step(n=1):
            for _ in range(n):
                if not bg:
                    return
                try:
                    next(bg[0])
                except StopIteration:
                    bg.pop(0)

        def out_tile(b, qg, i, ogT, t_ogT, gx, t_gx):
            tt = 2 + qg * 4 + i
            xt, t_x = xt_r.next()
            sy.dma("sp", xt[:], dram["x1"][b, tt * 128:(tt + 1) * 128, :], reads=[cx.tdram["x1"]], writes=[t_x])
            x2, t_x2 = x2_r.next()
            for half in range(2):
                pt, tp = nb(pX, "x")
                hs = slice(half * 512, (half + 1) * 512)
                for k in range(8):
                    sy.op("pe", lambda e: e.matmul(pt[:], ogT[:, k, i * 128:(i + 1) * 128], ow[:, k, hs], start=(k == 0), stop=(k == 7)),
                          reads=[t_ogT, t_ow], writes=[tp])
                    if k % 2 == 1:
                        yield
                sy.op("dve", lambda e: e.tensor_tensor(out=x2[:, hs], in0=pt[:], in1=gx[:, hs], op=ALU.mult), reads=[tp, t_gx], writes=[t_x2])
                yield
            sy.op("pool", lambda e: e.tensor_tensor(out=x2[:], in0=x2[:], in1=xt[:], op=ALU.add), reads=[t_x2, t_x], writes=[t_x2])
            yield
            ss, t_ss = small.next()
            sqj, t_sqj = sqj_r.next()
            for _ in range(6):
                yield
            sy.op("act", lambda e: e.activation(out=sqj[:], in_=x2[:], func=AF.Square, accum_out=ss[:, 0:1]), reads=[t_x2], writes=[t_sqj, t_ss])
            yield
            sy.op("act", lambda e: e.activation(out=ss[:, 1:2], in_=ss[:, 0:1], func=AF.Ln, scale=1.0 / D, bias=cx.eps_c[:, 0:1]),
                  reads=[t_ss, cx.t_eps], writes=[t_ss])
            sy.op("act", lambda e: e.activation(out=ss[:, 2:3], in_=ss[:, 1:2], func=AF.Exp, scale=-0.5), reads=[t_ss], writes=[t_ss])
            yield
            sy.op("dve", lambda e: e.scalar_tensor_tensor(out=x2[:], in0=x2[:], scalar=ss[:, 2:3], in1=fng[:], op0=ALU.mult, op1=ALU.mult),
                  reads=[t_x2, t_ss, t_c], writes=[t_x2])
            yield
            sy.dma("pool", cx.out[b, (tt - 2) * 128:(tt - 1) * 128, :], x2[:], reads=[t_x2], writes=[cx.tout])
            yield

        def out_group(b, qg, ogT, t_ogT, gx, t_gx):
            for i0 in (0, 1, 2, 3):
                gens = [out_tile(b, qg, i, ogT, t_ogT, gx, t_gx) for i in (i0,)]
                while gens:
                    for g in list(gens):
                        try:
                            next(g)
                        except StopIteration:
                            gens.remove(g)
                        yield

        pre_b = {}
        pre_q = {}

        def load_batch(b):
            gx, t_gx = gx_r.next()
            gate_rows(cx, 1, b, gx, t_gx)
            kT, t_kT = kT_r.next()
            v, t_v = v_r.next()
            sy.dma("sp", kT[:], dram["akT"][b].rearrange("g d t -> d g t"), reads=[cx.tdram["akT"]], writes=[t_kT])
            sy.dma("sp", v[:], dram["av"][b].rearrange("(t p) c -> p t c", p=128), reads=[cx.tdram["av"]], writes=[t_v])
            pre_b[b] = (gx, t_gx, kT, t_kT, v, t_v)

        def load_qg(b, qg):
            qT, t_qT = qT_r.next()
            sg, t_sg = sg_r.next()
            sy.dma("sp", qT[:], dram["aqT"][b].rearrange("h d t -> d h t")[:, :, qg * 512:(qg + 1) * 512], reads=[cx.tdram["aqT"]], writes=[t_qT])
            sy.dma("sp", sg[:], dram["agT"][b].rearrange("(f p) t -> p f t", p=128)[:, :, qg * 512:(qg + 1) * 512], reads=[cx.tdram["agT"]], writes=[t_sg])
            pre_q[(b, qg)] = (qT, t_qT, sg, t_sg)

        for b in range(NB):
            if b not in pre_b:
                load_batch(b)
            gx, t_gx, kT, t_kT, v, t_v = pre_b.pop(b)
            for qg in range(4):
                if (b, qg) not in pre_q:
                    load_qg(b, qg)
                qT, t_qT, sg, t_sg = pre_q.pop((b, qg))
                while len(bg) > 1:
                    bg_step()
                ogT, t_ogT = ogT_r.next()
                steps = [(h, kt) for h in range(8) for kt in range(NT)]
                hbuf = {}

                def emit_S(s):
                    h, kt = steps[s]
                    if kt == 0:
                        hbuf[h] = cnt["h"] % 3
                        cnt["h"] += 1
                    hb = hbuf[h]
                    ps_, tps = nb(pS, "s")
                    sy.op("pe", lambda e: e.matmul(ps_[:], kT[:, h // 4, kt * 128:(kt + 1) * 128], qT[:, h, :], start=True, stop=True),
                          reads=[t_kT, t_qT], writes=[tps])
                    sy.op("act", lambda e: e.activation(out=PTh[hb][:, kt, :], in_=ps_[:], func=AF.Exp, scale=ks), reads=[tps], writes=[t_PTh[hb][kt]])

                head = {}
                hst = {}
                hpab = {}
                DCH = ()
                KD = 0

                def stage_b(hh, po, tpo, pd, tpd, ogT=ogT, t_ogT=t_ogT, sg=sg, t_sg=t_sg):
                    rden, t_rden = rden_r.next()
                    sy.op("dve", lambda e: e.reciprocal(out=rden[:], in_=pd[:]), reads=[tpd], writes=[t_rden])
                    tmo, t_tmo = tmo_r.next()
                    sy.op("dve", lambda e: e.tensor_tensor(out=tmo[:], in0=po[:], in1=rden[:], op=ALU.mult), reads=[tpo, t_rden], writes=[t_tmo])
                    sy.op("pool", lambda e: e.tensor_tensor(out=ogT[:, hh, :], in0=tmo[:], in1=sg[:, hh, :], op=ALU.mult), reads=[t_tmo, t_sg], writes=[t_ogT])

                for s0 in range(LOOK):
                    emit_S(s0)
                for s, (h, kt) in enumerate(steps):
                    if s + LOOK < len(steps):
                        emit_S(s + LOOK)
                    if kt == 0:
                        head[h] = nb(pO, "o")
                        hst[h] = nb(pDs, "d")
                    if kt == 4 and pend:
                        fn, after = pend.pop(0)
                        fn()
                        if after:
                            after()
                    po, tpo = head[h]
                    pd, tpd = hst[h]
                    hb = hbuf[h]
                    g = h // 4
                    sy.op("pe", lambda e: e.matmul(po[:], v[:, kt, g * 128:(g + 1) * 128], PTh[hb][:, kt, :], start=(kt == 0), stop=(kt == NT - 1)),
                          reads=[t_v, t_PTh[hb][kt]], writes=[tpo])
                    for (k0, k1) in DCH:
                        if kt == k1 - 1:
                            pab, t_pab = pab_r[0].next()
                            with nc.allow_low_precision(reason="bf16 matmul operand; reduction itself runs in fp32"):
                                sy.op("dve", lambda e: e.tensor_reduce(out=pab[:], in_=PTh[hb][:, k0:k1, :].rearrange("p k q -> p q k"), axis=AX.X, op=ALU.add),
                                      reads=t_PTh[hb][k0:k1], writes=[t_pab])
                            hpab.setdefault(h, []).append((pab, t_pab))
                    if kt >= KD:
                        sy.op("pe", lambda e: e.matmul(pd[:], ones[:], PTh[hb][:, kt, :], start=(kt == KD), stop=(kt == NT - 1)),
                              reads=[t_PTh[hb][kt], t_c], writes=[tpd])
                    if kt == NT - 2:
                        for (pab, t_pab) in hpab.pop(h, []):
                            sy.op("pe", lambda e: e.matmul(pd[:], ones[:], pab[:], start=False, stop=False), reads=[t_pab, t_c], writes=[tpd])
                    if kt == NT - 1:
                        after = None
                        if h == 7:
                            after = (lambda b=b, qg=qg, ogT=ogT, t_ogT=t_ogT, gx=gx, t_gx=t_gx: bg.append(out_group(b, qg, ogT, t_ogT, gx, t_gx)))
                        pend.append([(lambda hh=h, po=po, tpo=tpo, pd=pd, tpd=tpd, fb=stage_b: fb(hh, po, tpo, pd, tpd)), after])
                    if s == 40:
                        if qg < 3:
                            load_qg(b, qg + 1)
                        elif b + 1 < NB:
                            load_batch(b + 1)
                            load_qg(b + 1, 0)
                    bg_step()
        while pend:
            fn, after = pend.pop(0)
            fn()
            if after:
                after()
        while bg:
            bg_step()


def make_in_maps(inp, NB, ncores):
    hc = host_consts(NB)
    f = lambda a: np.ascontiguousarray(np.asarray(a, dtype=np.float32))
    shared = {
        "norm_g": f(inp["norm_g"]), "final_norm_g": f(inp["final_norm_g"]), "ada_w": f(inp["ada_w"]), "ada_b": f(inp["ada_b"]),
        "er_in_w": f(inp["er_in_w"][0]), "er_out_w": f(inp["er_out_w"][0]),
        "convw": f(np.asarray(inp["hy_conv_w"][0]).reshape(3, 12, 128).transpose(2, 1, 0)),
        "convb": f(np.asarray(inp["hy_conv_b"][0]).reshape(12, 128).T),
        "hy_f_w1": f(inp["hy_f_w1"][0]), "hy_f_b1": f(np.asarray(inp["hy_f_b1"][0]).reshape(64, 1)),
        "hy_f_freq": f(np.asarray(inp["hy_f_freq"][0]).reshape(64, 1)), "hy_f_w2": f(inp["hy_f_w2"][0]),
        "hy_f_b2": f(np.asarray(inp["hy_f_b2"][0]).reshape(64, 1)), "hy_f_w3": f(inp["hy_f_w3"][0]),
        "hy_bias": f(np.asarray(inp["hy_bias"][0]).reshape(4, 128).T),
        "retdec": f(np.concatenate([np.asarray(inp["ret_decay_f"][0]), np.asarray(inp["ret_decay_b"][0])])),
        "ret_norm_g": f(inp["ret_norm_g"][0]),
        "at_in_w": f(inp["at_in_w"][0]), "at_out_w": f(inp["at_out_w"][0]),
        "at_q_norm": f(inp["at_q_norm"][0]), "at_k_norm": f(inp["at_k_norm"][0]),
    }
    for k, v in hc.items():
        shared["k_" + k] = v
    x = np.asarray(inp["x"], dtype=np.float32)
    ctx = np.asarray(inp["ctx"], dtype=np.float32)
    c = np.asarray(inp["c"], dtype=np.float32)
    cc = np.asarray(inp["c_ctx"], dtype=np.float32)
    maps = []
    for i in range(ncores):
        sl = slice(i * NB, (i + 1) * NB)
        cfull = np.concatenate([c[sl], cc[None, :]], axis=0)
        cT = np.ascontiguousarray(cfull.reshape(NB + 1, 8, 128).transpose(2, 1, 0))
        m = dict(shared)
        m["x"] = np.ascontiguousarray(x[sl])
        m["ctx"] = np.ascontiguousarray(ctx[sl])
        m["cT"] = cT
        maps.append(m)
    return maps


_NC_CACHE = {}


def kernel(**inputs):
    NB = inputs["x"].shape[0] // NCORES
    if NB not in _NC_CACHE:
        _NC_CACHE[NB] = build(NB)
    nc = _NC_CACHE[NB]
    maps = make_in_maps(inputs, NB, NCORES)
    res = run_bass_kernel_spmd(nc, maps, core_ids=list(range(NCORES)))
    return np.concatenate([r["out"] for r in res.results], axis=0).astype(np.float32)
```

```python
import contextlib
import math
import numpy as np
import ml_dtypes
import concourse.bass as bass
import concourse.mybir as mybir
from concourse.bass_utils import run_bass_kernel_spmd

F32 = mybir.dt.float32
BF16 = mybir.dt.bfloat16
AF = mybir.ActivationFunctionType
ALU = mybir.AluOpType
AX = mybir.AxisListType

NCORES = 8
D = 1024
L = 2048
LC = 256
TOK = L + LC
NT = TOK // 128
EPS = 1e-6
NFFT = 2 * L
NFFTC = 2 * LC
PI = math.pi


class T:
    __slots__ = ("name", "w", "r")

    def __init__(self, name="t"):
        self.name = name
        self.w = None
        self.r = {}


class Sy:
    NS = 8

    def __init__(self, nc, stack):
        self.nc = nc
        self.stack = stack
        self.E = {"pe": nc.tensor, "dve": nc.vector, "act": nc.scalar, "pool": nc.gpsimd, "sp": nc.sync}
        self.sems = {}
        self.cnt = {}
        self.seen = {e: {} for e in self.E}
        for e in ("pe", "dve", "act", "pool"):
            self.sems[("c", e)] = stack.enter_context(nc.semaphore("c_" + e))
            self.cnt[e] = 0
        self.dn = {}
        for q in ("sp", "act", "pool"):
            self.dn[q] = 0
            for i in range(self.NS):
                self.sems[("d", q, i)] = stack.enter_context(nc.semaphore(f"d_{q}_{i}"))

    _uid = 0

    def sb(self, name, shape, dt, stack=None):
        Sy._uid += 1
        return (stack or self.stack).enter_context(self.nc.sbuf_tensor(f"s{Sy._uid}_{name}", list(shape), dt))

    def ps(self, name, shape, dt, stack=None):
        Sy._uid += 1
        return (stack or self.stack).enter_context(self.nc.psum_tensor(f"p{Sy._uid}_{name}", list(shape), dt))

    def _wait(self, eng, deps):
        best = {}
        for k, v in deps:
            if best.get(k, 0) < v:
                best[k] = v
        sn = self.seen[eng]
        for k, v in best.items():
            if sn.get(k, 0) >= v:
                continue
            self.E[eng].wait_ge(self.sems[k], v)
            sn[k] = v

    def _deps(self, eng_key, reads, writes, is_dma):
        deps = []
        for t in reads:
            if t.w is not None:
                if t.w[0] == eng_key and eng_key == ("c", "pe"):
                    continue
                deps.append(t.w)
        pe = eng_key == ("c", "pe")
        for t in writes:
            if t.w is not None and not (pe and t.w[0] == eng_key):
                deps.append(t.w)
            for k, v in t.r.items():
                if not (pe and k == eng_key):
                    deps.append((k, v))
        return deps

    def _reg(self, me, reads, writes):
        k, v = me
        for t in reads:
            t.r[k] = v
        for t in writes:
            t.w = me
            t.r = {}

    def op(self, eng, fn, reads=(), writes=()):
        key = ("c", eng)
        self._wait(eng, self._deps(key, reads, writes, False))
        ins = fn(self.E[eng])
        self.cnt[eng] += 1
        ins.then_inc(self.sems[key], 1)
        self._reg((key, self.cnt[eng]), reads, writes)
        return ins

    def dma(self, q, out, in_, reads=(), writes=(), **kw):
        n = self.dn[q]
        i = n % self.NS
        tgt = 16 * (n // self.NS + 1)
        key = ("d", q, i)
        deps = self._deps(key, reads, writes, True)
        if tgt > 16:
            deps.append((key, tgt - 16))
        self._wait(q, deps)
        ins = self.E[q].dma_start(out=out, in_=in_, **kw)
        ins.then_inc(self.sems[key], 16)
        self.dn[q] = n + 1
        self._reg((key, tgt), reads, writes)
        return ins

    def barrier(self):
        allk = []
        for e in ("pe", "dve", "act", "pool"):
            if self.cnt[e] > 0:
                allk.append((("c", e), self.cnt[e]))
        for q in ("sp", "act", "pool"):
            n = self.dn[q]
            for i in range(self.NS):
                c = (n - i + self.NS - 1) // self.NS if n > i else 0
                if c > 0:
                    allk.append((("d", q, i), 16 * c))
        for e in self.E:
            self._wait(e, [kv for kv in allk if kv[0] != ("c", e)])

    def finish(self, outs):
        self._wait("sp", [t.w for t in outs if t.w is not None])


class Ring:
    def __init__(self, sy, stack, name, n, shape, dt, psum=False):
        self.items = []
        for i in range(n):
            t = (sy.ps if psum else sy.sb)(f"{name}{i}", shape, dt, stack)
            self.items.append((t, T(f"{name}{i}")))
        self.i = 0

    def next(self):
        it = self.items[self.i % len(self.items)]
        self.i += 1
        return it


_CONST = {}


def _bf(a):
    return np.ascontiguousarray(a).astype(ml_dtypes.bfloat16)


def host_consts(NB):
    if NB in _CONST:
        return _CONST[NB]
    c = {}
    c["ident_bf"] = _bf(np.eye(128))
    c["ident_f"] = np.eye(128, dtype=np.float32)
    sel = np.zeros((NB + 1, NB + 1, 128), np.float32)
    for b in range(NB + 1):
        sel[b, b, :] = 1.0
    c["sel"] = sel.reshape(NB + 1, (NB + 1) * 128)
    def dft(Ls, N):
        t = np.arange(Ls, dtype=np.float64)[:, None]
        k = np.arange(N // 2, dtype=np.float64)[None, :]
        th = 2.0 * np.pi * (k + 0.5) * t / N
        return np.cos(th), -np.sin(th)
    C, S = dft(L, NFFT)
    fw = np.zeros((8, 128, 16, 512), np.float32)
    for g in range(8):
        blk = np.concatenate([C[:, 256 * g:256 * g + 256], S[:, 256 * g:256 * g + 256]], axis=1)
        fw[g] = blk.reshape(16, 128, 512).transpose(1, 0, 2)
    c["ffwd"] = _bf(fw)
    Fi = np.concatenate([C.T, S.T], axis=0) * (2.0 / NFFT)
    fi = np.zeros((4, 4, 128, 8, 512), np.float32)
    for ng in range(4):
        for fq in range(4):
            blk = Fi[fq * 1024:(fq + 1) * 1024, ng * 512:(ng + 1) * 512]
            fi[ng, fq] = blk.reshape(8, 128, 512).transpose(1, 0, 2)
    c["finv"] = _bf(fi)
    Cc, Sc = dft(LC, NFFTC)
    blk = np.concatenate([Cc, Sc], axis=1)
    c["ffwd_c"] = _bf(blk.reshape(2, 128, 512).transpose(1, 0, 2))
    Fic = np.concatenate([Cc.T, Sc.T], axis=0) * (2.0 / NFFTC)
    c["finv_c"] = _bf(Fic.reshape(4, 128, 256).transpose(1, 0, 2))
    def zfeat(Ls):
        n = np.arange(Ls, dtype=np.float32)
        t = n / np.float32(max(Ls - 1, 1))
        bands = np.linspace(1e-4, 15, 16, dtype=np.float32)
        ang = np.float32(2.0 * math.pi / Ls) * n[:, None] * bands[None, :]
        z = np.concatenate([t[:, None], np.cos(ang), -np.sin(ang)], axis=-1).astype(np.float32)
        maxd = math.log(1e-2) / 0.3
        mind = math.log(1e-2) / 1.5
        deltas = np.abs(np.linspace(mind, maxd, 512, dtype=np.float32))
        win = np.exp(-t[:, None] * deltas[None, :]).astype(np.float32)
        return z, win
    z, win = zfeat(L)
    c["zT"] = np.ascontiguousarray(z.T)
    c["win"] = np.ascontiguousarray(win.reshape(16, 128, 512).transpose(1, 0, 2))
    zc, winc = zfeat(LC)
    c["zcT"] = np.ascontiguousarray(zc.T)
    c["winc"] = np.ascontiguousarray(winc.reshape(2, 128, 512).transpose(1, 0, 2))
    rows = L // 64
    r = np.repeat(np.arange(rows, dtype=np.float32), 64)
    col = np.tile(np.arange(64, dtype=np.float32), rows)
    inv = (10000.0 ** (-np.arange(0, 64, 2, dtype=np.float32) / 64)).astype(np.float32)
    ang = np.concatenate([r[:, None] * inv, col[:, None] * inv], axis=-1)
    c["rcos"] = np.ascontiguousarray(np.cos(ang).astype(np.float32).reshape(16, 128, 64).transpose(1, 0, 2))
    c["rsin"] = np.ascontiguousarray(np.sin(ang).astype(np.float32).reshape(16, 128, 64).transpose(1, 0, 2))
    j = np.arange(128, dtype=np.float32)[:, None]
    i = np.arange(128, dtype=np.float32)[None, :]
    d = i - j
    rt = np.stack([np.maximum(d, 0), np.maximum(-d, 0), (d >= 0).astype(np.float32), (d <= 0).astype(np.float32),
                   np.broadcast_to(i + 1, (128, 128)), np.broadcast_to(128 - i, (128, 128))], axis=1)
    c["rtab"] = np.ascontiguousarray(rt.astype(np.float32))
    c["rcol"] = np.ascontiguousarray(np.stack([127 - j[:, 0], j[:, 0]], axis=1).astype(np.float32))
    _CONST[NB] = c
    return c


class Cx:
    pass


def build(NB, phases=("ada", "p1", "p2", "p3", "p4", "l1"), debug=()):
    nc = bass.Bass("TRN2", target_bir_lowering=False)
    hc = host_consts(NB)
    cx = Cx()
    cx.nc = nc
    cx.NB = NB
    NB1 = NB + 1
    dram = {}

    def din(name, shape, dt=F32):
        dram[name] = nc.dram_tensor(name, list(shape), dt, kind="ExternalInput").ap()
        return dram[name]

    def dscr(name, shape, dt):
        kind = "ExternalOutput" if name in debug else "Internal"
        dram[name] = nc.dram_tensor(name, list(shape), dt, kind=kind).ap()
        return dram[name]

    din("x", [NB, L, D]); din("ctx", [NB, LC, D]); din("cT", [128, 8, NB1])
    din("norm_g", [2, D]); din("final_norm_g", [D]); din("ada_w", [2, D, 3 * D]); din("ada_b", [2, 3 * D])
    din("er_in_w", [D, 4096]); din("er_out_w", [D, D])
    din("convw", [128, 12, 3]); din("convb", [128, 12])
    din("hy_f_w1", [33, 64]); din("hy_f_b1", [64, 1]); din("hy_f_freq", [64, 1]); din("hy_f_w2", [64, 64])
    din("hy_f_b2", [64, 1]); din("hy_f_w3", [64, 1024]); din("hy_bias", [128, 4])
    din("retdec", [8]); din("ret_norm_g", [512])
    din("at_in_w", [D, 2560]); din("at_out_w", [D, D]); din("at_q_norm", [128]); din("at_k_norm", [128])
    for k, v in hc.items():
        din("k_" + k, v.shape, BF16 if v.dtype == ml_dtypes.bfloat16 else F32)
    out = nc.dram_tensor("out", [NB, L, D], F32, kind="ExternalOutput").ap()
    dscr("u_tm", [NB, TOK, 512], BF16); dscr("u_fm", [NB, 512, TOK], BF16); dscr("w_fm", [NB, 512, TOK], BF16)
    dscr("qT", [NB, 512, TOK], BF16); dscr("kT", [NB, 512, TOK], BF16)
    dscr("k_tm", [NB, TOK, 512], BF16); dscr("v_tm", [NB, TOK, 512], BF16); dscr("g_tm", [NB, TOK, 512], BF16)
    dscr("yT", [NB, D, TOK], BF16)
    dscr("aqT", [NB, 8, 128, L], BF16); dscr("akT", [NB, 2, 128, TOK], BF16)
    dscr("av", [NB, TOK, 256], BF16); dscr("agT", [NB, D, L], BF16)
    dscr("x1", [NB, TOK, D], F32)
    dscr("dbg_mod", [2, NB1, 3 * D], F32)
    dscr("modrows", [2, NB1, 3 * D], F32)
    dscr("dbg_h", [L, 1024], F32)
    cx.dram = dram
    cx.out = out
    cx.tdram = {k: T("dram_" + k) for k in dram}
    cx.tout = T("out")

    with contextlib.ExitStack() as st:
        sy = Sy(nc, st)
        cx.sy = sy
        cx.ident_bf = sy.sb("ident_bf", [128, 128], BF16)
        cx.ident_f = sy.sb("ident_f", [128, 128], F32)
        cx.modFM = [sy.sb(f"modFM{l}", [128, 24, NB1], F32) for l in range(2)]
        cx.t_const = T("const")
        cx.t_mod = [T("mod0"), T("mod1")]
        sy.dma("sp", cx.ident_bf[:], dram["k_ident_bf"], writes=[cx.t_const])
        sy.dma("sp", cx.ident_f[:], dram["k_ident_f"], writes=[cx.t_const])
        cx.psum = []
        for i in range(8):
            cx.psum.append((sy.ps(f"pb{i}", [128, 512], F32), T(f"pb{i}")))

        if "ada" in phases:
            phase_ada(cx)
            sy.barrier()
        if "p1" in phases:
            phase1(cx)
            sy.barrier()
        if "p2" in phases:
            phase2(cx)
            sy.barrier()
        if "p3" in phases:
            phase3(cx)
            sy.barrier()
        if "p4" in phases:
            phase4(cx)
            sy.barrier()
        if "l1" in phases:
            layer1(cx)
        if "l1p" in phases:
            l1_prep(cx)
        if "l1a" in phases:
            l1_attn(cx)
        outs = [cx.tout] + [cx.tdram[k] for k in debug]
        sy.barrier()
        sy.finish(outs)
    return nc


def phase_ada(cx):
    sy, nc, NB, dram = cx.sy, cx.nc, cx.NB, cx.dram
    NB1 = NB + 1
    with contextlib.ExitStack() as ph:
        scT = sy.sb("scT", [128, 8, NB1], F32, ph)
        t_sc = T()
        sy.dma("sp", scT[:], dram["cT"], writes=[t_sc])
        sy.op("act", lambda e: e.activation(out=scT[:], in_=scT[:], func=AF.Silu), reads=[t_sc], writes=[t_sc])
        wsb = sy.sb("adaw", [128, 8, 3 * D], F32, ph)
        bb = sy.sb("adab", [NB1, 3 * D], F32, ph)
        ngb = sy.sb("ngb", [NB1, D], F32, ph)
        mTM = [sy.sb(f"modTM{l}", [NB1, 3 * D], F32, ph) for l in range(2)]
        for l in range(2):
            t_w, t_b = T(), T()
            for k in range(8):
                sy.dma("sp", wsb[:, k, :], dram["ada_w"][l, k * 128:(k + 1) * 128, :], writes=[t_w])
            sy.dma("sp", bb[:], dram["ada_b"][l].partition_broadcast(NB1), writes=[t_b])
            sy.dma("sp", ngb[:], dram["norm_g"][l].partition_broadcast(NB1), writes=[t_b])
            m = mTM[l]
            tm = cx.t_mod[l]
            for n in range(6):
                pt, tp = cx.psum[n % 2]
                for k in range(8):
                    sy.op("pe", lambda e, k=k: e.matmul(pt[0:NB1, :], scT[:, k, :], wsb[:, k, n * 512:(n + 1) * 512],
                                                        start=(k == 0), stop=(k == 7)), reads=[t_sc, t_w], writes=[tp])
                sy.op("dve", lambda e: e.tensor_tensor(out=m[:, n * 512:(n + 1) * 512], in0=pt[0:NB1, :],
                                                       in1=bb[:, n * 512:(n + 1) * 512], op=ALU.add),
                      reads=[tp, t_b], writes=[tm])
            sy.op("dve", lambda e: e.scalar_tensor_tensor(out=m[:, D:2 * D], in0=m[:, D:2 * D], scalar=1.0, in1=ngb[:],
                                                          op0=ALU.add, op1=ALU.mult), reads=[tm, t_b], writes=[tm])
            for j in range(24):
                pt, tp = cx.psum[2 + j % 2]
                sy.op("pe", lambda e: e.transpose(out=pt[:, 0:NB1], in_=m[:, j * 128:(j + 1) * 128],
                                                  identity=cx.ident_f[0:NB1, 0:NB1]), reads=[tm, cx.t_const], writes=[tp])
                sy.op("dve", lambda e: e.tensor_copy(out=cx.modFM[l][:, j, :], in_=pt[:, 0:NB1]), reads=[tp], writes=[tm])
            sy.dma("sp", dram["modrows"][l], m[:], reads=[tm], writes=[cx.tdram["modrows"]])


def load_w_bf16(cx, dst, src, ncols, t_w):
    for k in range(8):
        for c0 in range(0, ncols, 2048):
            c1 = min(ncols, c0 + 2048)
            cx.sy.dma("pool", dst[:, k, c0:c1], src[k * 128:(k + 1) * 128, c0:c1], writes=[t_w])


def interleave(gens):
    gens = list(gens)
    while gens:
        for g in list(gens):
            try:
                next(g)
            except StopIteration:
                gens.remove(g)


def norm_tile_g(cx, l, bsel, src_ap, xt_ring, scr, hxT, tok0, t_h, pbank, small, t_src=None, keep=None):
    sy = cx.sy
    xt, t_x = xt_ring.next()
    if keep is not None:
        keep.append((xt, t_x))
    sy.dma("sp", xt[:], src_ap, reads=[t_src] if t_src else [], writes=[t_x])
    ss, t_ss = small.next()
    sq, t_sq = scr.next()
    sy.op("act", lambda e: e.activation(out=sq[:], in_=xt[:], func=AF.Square, accum_out=ss[:, 0:1]),
          reads=[t_x], writes=[t_sq, t_ss])
    yield
    sy.op("act", lambda e: e.activation(out=ss[:, 1:2], in_=ss[:, 0:1], func=AF.Sqrt, scale=1.0 / D, bias=cx.eps_c[:, 0:1]),
          reads=[t_ss, cx.t_eps], writes=[t_ss])
    yield
    sy.op("dve", lambda e: e.reciprocal(out=ss[:, 2:3], in_=ss[:, 1:2]), reads=[t_ss], writes=[t_ss])
    yield
    sy.op("dve", lambda e: e.tensor_scalar(out=sq[:], in0=xt[:], scalar1=ss[:, 2:3], scalar2=None, op0=ALU.mult),
          reads=[t_x, t_ss], writes=[t_sq])
    yield
    pt, tp = pbank
    ptb = pt[:].bitcast(BF16)
    for k in range(8):
        sy.op("pe", lambda e: e.transpose(out=ptb[:, k * 128:(k + 1) * 128], in_=sq[:, k * 128:(k + 1) * 128],
                                          identity=cx.ident_bf[:]), reads=[t_sq, cx.t_const], writes=[tp])
    mf = cx.modFM[l]
    for k in range(8):
        if k % 2 == 0:
            sy.op("act", lambda e: e.activation(out=hxT[:, k, tok0:tok0 + 128], in_=ptb[:, k * 128:(k + 1) * 128],
                                                func=AF.Identity, scale=mf[:, 8 + k, bsel:bsel + 1], bias=mf[:, k, bsel:bsel + 1]),
                  reads=[tp, cx.t_mod[l]], writes=[t_h])
        else:
            sy.op("dve", lambda e: e.tensor_scalar(out=hxT[:, k, tok0:tok0 + 128], in0=ptb[:, k * 128:(k + 1) * 128],
                                                   scalar1=mf[:, 8 + k, bsel:bsel + 1], scalar2=mf[:, k, bsel:bsel + 1],
                                                   op0=ALU.mult, op1=ALU.add), reads=[tp, cx.t_mod[l]], writes=[t_h])
    yield


def norm_tile(cx, l, bsel, src_ap, xt_ring, scr, hxT, tok0, t_h, pbank, small, t_src=None):
    for _ in norm_tile_g(cx, l, bsel, src_ap, xt_ring, scr, hxT, tok0, t_h, pbank, small, t_src):
        pass


def mk_eps(cx, ph):
    cx.eps_c = cx.sy.sb("eps_c", [128, 1], F32, ph)
    cx.t_eps = T()
    cx.sy.op("dve", lambda e: e.memset(cx.eps_c[:], EPS), writes=[cx.t_eps])


def tok_src(cx, b, tt, l):
    d = cx.dram
    if l == 0:
        if tt < 2:
            return d["ctx"][b, tt * 128:(tt + 1) * 128, :], cx.NB, None
        return d["x"][b, (tt - 2) * 128:(tt - 1) * 128, :], b, None
    return d["x1"][b, tt * 128:(tt + 1) * 128, :], (cx.NB if tt < 2 else b), cx.tdram["x1"]


def phase1(cx):
    sy, nc, NB, dram = cx.sy, cx.nc, cx.NB, cx.dram
    with contextlib.ExitStack() as ph:
        mk_eps(cx, ph)
        inw = sy.sb("inw", [128, 8, 4096], BF16, ph)
        t_inw = T()
        load_w_bf16(cx, inw, dram["er_in_w"], 4096, t_inw)
        convw = sy.sb("convw", [128, 12, 3], F32, ph)
        convb = sy.sb("convb", [128, 12], F32, ph)
        t_cw = T()
        sy.dma("sp", convw[:], dram["convw"], writes=[t_cw])
        sy.dma("sp", convb[:], dram["convb"], writes=[t_cw])
        hxT = sy.sb("hxT", [128, 8, TOK], BF16, ph)
        t_h = [T() for _ in range(NT)]
        xt_ring = Ring(sy, ph, "xt", 4, [128, D], F32)
        scr = Ring(sy, ph, "scr", 4, [128, D], BF16)
        small = Ring(sy, ph, "sm", 8, [128, 4], F32)
        rowp_ring = Ring(sy, ph, "rowp", 2, [128, 2307], F32)
        for rp, t_rp in rowp_ring.items:
            for pos in (0, 257, 2306):
                sy.op("dve", lambda e: e.memset(rp[:, pos:pos + 1], 0.0), writes=[t_rp])
        acc_ring = Ring(sy, ph, "acc", 2, [128, 2305], F32)
        sg = sy.sb("sg", [128, 2305], F32, ph)
        t_sg = T()
        sy.op("dve", lambda e: e.memset(sg[:, 256:257], 0.0), writes=[t_sg])
        urow_ring = Ring(sy, ph, "urow", 1, [128, 2305], BF16)
        wrow_ring = Ring(sy, ph, "wrow", 1, [128, 2305], BF16)
        utm_ring = Ring(sy, ph, "utm", 2, [128, NT, 128], BF16)
        row_ring = Ring(sy, ph, "qkrow", 2, [128, TOK], BF16)
        st_ring = Ring(sy, ph, "stg", 4, [128, 512], BF16)
        pacc = [cx.psum[i] for i in range(7)]
        pi = [0]

        def next_bank():
            r = pacc[pi[0] % 7]
            pi[0] += 1
            return r

        tgs = [(0, 256, 1)] + [(256 + g * 512, 512, 258 + g * 512) for g in range(4)]
        for b in range(NB):
            def ntile(tt):
                src, bsel, tsrc = tok_src(cx, b, tt, 0)
                yield from norm_tile_g(cx, 0, bsel, src, xt_ring, scr, hxT, tt * 128, t_h[tt], next_bank(), small, tsrc)
            for t0 in range(0, NT, 4):
                interleave([ntile(tt) for tt in range(t0, min(NT, t0 + 4))])
            deferred = []
            for ct in range(4):
                accs = {}
                for ft, role in ((4 + ct, "x1"), (8 + ct, "v"), (ct, "x0"), (12 + ct, "hg")):
                    if role != "hg":
                        rp, t_rp = rowp_ring.next()
                    for (tok0, n, rc0) in tgs:
                        pt, tp = next_bank()
                        hts = t_h[tok0 // 128:(tok0 + n) // 128]
                        for k in range(8):
                            sy.op("pe", lambda e: e.matmul(pt[:, 0:n], inw[:, k, ft * 128:(ft + 1) * 128], hxT[:, k, tok0:tok0 + n],
                                                           start=(k == 0), stop=(k == 7)), reads=[t_inw] + hts, writes=[tp])
                        if role == "hg":
                            sy.op("act", lambda e: e.activation(out=sg[:, rc0 - 1:rc0 - 1 + n], in_=pt[:, 0:n], func=AF.Silu),
                                  reads=[tp], writes=[t_sg])
                        else:
                            sy.op("act", lambda e: e.activation(out=rp[:, rc0:rc0 + n], in_=pt[:, 0:n], func=AF.Copy),
                                  reads=[tp], writes=[t_rp])
                    if role == "hg":
                        continue
                    ac, t_ac = acc_ring.next()
                    cw = ft
                    sy.op("dve", lambda e: e.tensor_scalar(out=ac[:], in0=rp[:, 1:2306], scalar1=convw[:, cw, 1:2], scalar2=convb[:, cw:cw + 1],
                                                           op0=ALU.mult, op1=ALU.add), reads=[t_rp, t_cw], writes=[t_ac])
                    sy.op("dve", lambda e: e.scalar_tensor_tensor(out=ac[:], in0=rp[:, 0:2305], scalar=convw[:, cw, 0:1], in1=ac[:],
                                                                  op0=ALU.mult, op1=ALU.add), reads=[t_rp, t_cw, t_ac], writes=[t_ac])
                    sy.op("dve", lambda e: e.scalar_tensor_tensor(out=ac[:], in0=rp[:, 2:2307], scalar=convw[:, cw, 2:3], in1=ac[:],
                                                                  op0=ALU.mult, op1=ALU.add), reads=[t_rp, t_cw, t_ac], writes=[t_ac])
                    accs[role] = (ac, t_ac)
                    if role == "v":
                        ur, t_ur = urow_ring.next()
                        a1, t1 = accs["x1"]
                        sy.op("pool", lambda e: e.tensor_tensor(out=ur[:], in0=a1[:], in1=ac[:], op=ALU.mult),
                              reads=[t1, t_ac], writes=[t_ur])
                        for (d0, n, s0) in ((0, 256, 0), (256, 2048, 257)):
                            sy.dma("pool", dram["u_fm"][b, ct * 128:(ct + 1) * 128, d0:d0 + n], ur[:, s0:s0 + n],
                                   reads=[t_ur], writes=[cx.tdram["u_fm"]])
                        def do_utm(ur=ur, t_ur=t_ur, ct=ct):
                            um, t_um = utm_ring.next()
                            pt, tp = cx.psum[7]
                            ptb = pt[:].bitcast(BF16)
                            for t0 in range(0, NT, 8):
                                nn = min(8, NT - t0)
                                for ti in range(nn):
                                    tt = t0 + ti
                                    c0 = tt * 128 if tt < 2 else 257 + (tt - 2) * 128
                                    sy.op("pe", lambda e: e.transpose(out=ptb[:, ti * 128:(ti + 1) * 128], in_=ur[:, c0:c0 + 128],
                                                                      identity=cx.ident_bf[:]), reads=[t_ur, cx.t_const], writes=[tp])
                                sy.op("act", lambda e: e.activation(out=um[:, t0:t0 + nn, :], in_=ptb[:, 0:nn * 128].rearrange("p (a c) -> p a c", c=128),
                                                                    func=AF.Copy), reads=[tp], writes=[t_um])
                            sy.dma("pool", dram["u_tm"][b].rearrange("(t p) c -> p t c", p=128)[:, :, ct * 128:(ct + 1) * 128], um[:],
                                   reads=[t_um], writes=[cx.tdram["u_tm"]])
                        deferred.append(do_utm)
                while deferred:
                    deferred.pop(0)()
                wr, t_wr = wrow_ring.next()
                a0, t0_ = accs["x0"]
                sy.op("pool", lambda e: e.tensor_tensor(out=wr[:], in0=a0[:], in1=sg[:], op=ALU.mult), reads=[t0_, t_sg], writes=[t_wr])
                for (d0, n, s0) in ((0, 256, 0), (256, 2048, 257)):
                    sy.dma("pool", dram["w_fm"][b, ct * 128:(ct + 1) * 128, d0:d0 + n], wr[:, s0:s0 + n],
                           reads=[t_wr], writes=[cx.tdram["w_fm"]])
            for ft in range(8):
                row, t_row = row_ring.next()
                for (tok0, n, _) in tgs:
                    pt, tp = next_bank()
                    hts = t_h[tok0 // 128:(tok0 + n) // 128]
                    for k in range(8):
                        sy.op("pe", lambda e: e.matmul(pt[:, 0:n], inw[:, k, 2048 + ft * 128:2048 + (ft + 1) * 128], hxT[:, k, tok0:tok0 + n],
                                                       start=(k == 0), stop=(k == 7)), reads=[t_inw] + hts, writes=[tp])
                    if (tok0 // 256) % 2 == 0:
                        sy.op("act", lambda e: e.activation(out=row[:, tok0:tok0 + n], in_=pt[:, 0:n], func=AF.Copy), reads=[tp], writes=[t_row])
                    else:
                        sy.op("dve", lambda e: e.tensor_copy(out=row[:, tok0:tok0 + n], in_=pt[:, 0:n]), reads=[tp], writes=[t_row])
                dst = dram["qT"] if ft < 4 else dram["kT"]
                tdst = cx.tdram["qT"] if ft < 4 else cx.tdram["kT"]
                h = ft % 4
                sy.dma("pool", dst[b, h * 128:(h + 1) * 128, :], row[:], reads=[t_row], writes=[tdst])
            for tt in range(NT):
                for g, nm in enumerate(("k_tm", "v_tm", "g_tm")):
                    pt, tp = next_bank()
                    c0 = 2560 + g * 512
                    for k in range(8):
                        sy.op("pe", lambda e: e.matmul(pt[:], hxT[:, k, tt * 128:(tt + 1) * 128], inw[:, k, c0:c0 + 512],
                                                       start=(k == 0), stop=(k == 7)), reads=[t_inw, t_h[tt]], writes=[tp])
                    sg_, t_st = st_ring.next()
                    if nm == "g_tm":
                        sy.op("act", lambda e: e.activation(out=sg_[:], in_=pt[:], func=AF.Silu), reads=[tp], writes=[t_st])
                    elif nm == "k_tm":
                        sy.op("dve", lambda e: e.tensor_copy(out=sg_[:], in_=pt[:]), reads=[tp], writes=[t_st])
                    else:
                        sy.op("act", lambda e: e.activation(out=sg_[:], in_=pt[:], func=AF.Copy), reads=[tp], writes=[t_st])
                    sy.dma("pool", dram[nm][b, tt * 128:(tt + 1) * 128, :], sg_[:], reads=[t_st], writes=[cx.tdram[nm]])


def phase2(cx):
    sy, nc, NB, dram = cx.sy, cx.nc, cx.NB, cx.dram
    NC_ = NT
    order_f = list(range(NC_))
    order_b = [1, 0] + list(range(NC_ - 1, 1, -1))
    with contextlib.ExitStack() as ph:
        mk_eps(cx, ph)
        rtab = sy.sb("rtab", [128, 6, 128], F32, ph)
        rcol = sy.sb("rcol", [128, 2], F32, ph)
        dec = sy.sb("dec", [128, 8], F32, ph)
        lg = sy.sb("lg", [128, 8], F32, ph)
        rng = sy.sb("rng", [128, 512], F32, ph)
        t_c = T()
        sy.dma("sp", rtab[:], dram["k_rtab"], writes=[t_c])
        sy.dma("sp", rcol[:], dram["k_rcol"], writes=[t_c])
        sy.dma("sp", dec[:], dram["retdec"].partition_broadcast(128), writes=[t_c])
        sy.dma("sp", rng[:], dram["ret_norm_g"].partition_broadcast(128), writes=[t_c])
        sy.op("act", lambda e: e.activation(out=lg[:], in_=dec[:], func=AF.Exp), reads=[t_c], writes=[t_c])
        sy.op("dve", lambda e: e.tensor_scalar(out=lg[:], in0=lg[:], scalar1=-1.0, scalar2=None, op0=ALU.mult), reads=[t_c], writes=[t_c])
        maskT = sy.sb("maskT", [128, 4, 128], F32, ph)
        qd = sy.sb("qd", [128, 8, 128], F32, ph)
        kd = sy.sb("kd", [128, 8], F32, ph)
        cd = sy.sb("cd", [128, 8], F32, ph)
        e1 = sy.sb("e1", [128, 128], F32, ph)
        e2 = sy.sb("e2", [128, 128], F32, ph)
        ks = 128.0 ** -0.5
        for h in range(4):
            sy.op("act", lambda e: e.activation(out=e1[:], in_=rtab[:, 0, :], func=AF.Exp, scale=lg[:, h:h + 1]), reads=[t_c], writes=[t_c])
            sy.op("act", lambda e: e.activation(out=e2[:], in_=rtab[:, 1, :], func=AF.Exp, scale=lg[:, 4 + h:5 + h]), reads=[t_c], writes=[t_c])
            sy.op("dve", lambda e: e.scalar_tensor_tensor(out=e1[:], in0=e1[:], scalar=ks, in1=rtab[:, 2, :], op0=ALU.mult, op1=ALU.mult), reads=[t_c], writes=[t_c])
            sy.op("dve", lambda e: e.scalar_tensor_tensor(out=e2[:], in0=e2[:], scalar=ks, in1=rtab[:, 3, :], op0=ALU.mult, op1=ALU.mult), reads=[t_c], writes=[t_c])
            sy.op("dve", lambda e: e.tensor_tensor(out=maskT[:, h, :], in0=e1[:], in1=e2[:], op=ALU.add), reads=[t_c], writes=[t_c])
            sy.op("act", lambda e: e.activation(out=qd[:, h, :], in_=rtab[:, 4, :], func=AF.Exp, scale=lg[:, h:h + 1]), reads=[t_c], writes=[t_c])
            sy.op("act", lambda e: e.activation(out=qd[:, 4 + h, :], in_=rtab[:, 5, :], func=AF.Exp, scale=lg[:, 4 + h:5 + h]), reads=[t_c], writes=[t_c])
            sy.op("act", lambda e: e.activation(out=kd[:, h:h + 1], in_=rcol[:, 0:1], func=AF.Exp, scale=lg[:, h:h + 1]), reads=[t_c], writes=[t_c])
            sy.op("act", lambda e: e.activation(out=kd[:, 4 + h:5 + h], in_=rcol[:, 1:2], func=AF.Exp, scale=lg[:, 4 + h:5 + h]), reads=[t_c], writes=[t_c])
        sy.op("act", lambda e: e.activation(out=cd[:], in_=lg[:], func=AF.Exp, scale=128.0), reads=[t_c], writes=[t_c])
        sy.op("dve", lambda e: e.tensor_scalar(out=kd[:], in0=kd[:], scalar1=ks, scalar2=None, op0=ALU.mult), reads=[t_c], writes=[t_c])

        qT_r = Ring(sy, ph, "qTh", 2, [128, TOK], BF16)
        kT_r = Ring(sy, ph, "kTh", 2, [128, TOK], BF16)
        k_r = Ring(sy, ph, "kh", 2, [128, NT, 128], BF16)
        v_r = Ring(sy, ph, "vh", 2, [128, NT, 128], BF16)
        g_r = Ring(sy, ph, "gh", 2, [128, NT, 128], BF16)
        qf_r = Ring(sy, ph, "qf", 2, [128, NT, 128], BF16)
        qb_r = Ring(sy, ph, "qb", 2, [128, NT, 128], BF16)
        kf_r = Ring(sy, ph, "kf", 2, [128, NT, 128], BF16)
        kb_r = Ring(sy, ph, "kb", 2, [128, NT, 128], BF16)
        gg_r = Ring(sy, ph, "gg", 2, [128, NT, 128], F32)
        S32_r = [Ring(sy, ph, f"S32_{i}", 2, [128, NT, 128], F32) for i in range(2)]
        Sbf_r = [Ring(sy, ph, f"Sbf_{i}", 2, [128, NT, 128], BF16) for i in range(2)]
        PT_r = Ring(sy, ph, "PT", 2, [128, 4, 128], BF16)
        oc_r = Ring(sy, ph, "oc", 2, [128, 4, 128], F32)
        sq_r = Ring(sy, ph, "sq", 2, [128, 4, 128], F32)
        y_r = Ring(sy, ph, "yy", 2, [128, 4, 128], BF16)
        st_r = Ring(sy, ph, "hst", 4, [128, 16], F32)
        yT_r = Ring(sy, ph, "yTh", 2, [128, TOK], BF16)
        pST = [cx.psum[0], cx.psum[1]]
        pO = [cx.psum[2], cx.psum[3]]
        pTr = [cx.psum[4], cx.psum[5]]
        pD = [cx.psum[6], cx.psum[7]]
        cnt = {"st": 0, "o": 0, "d": 0, "tr": 0}

        def nb(lst, key):
            r = lst[cnt[key] % len(lst)]
            cnt[key] += 1
            return r

        def head_front(b, h, C):
            qT, t_q = qT_r.next(); kT, t_k = kT_r.next(); kh, t_kh = k_r.next(); vh, t_vh = v_r.next(); gh, t_gh = g_r.next()
            hs = slice(h * 128, (h + 1) * 128)
            sy.dma("sp", qT[:], dram["qT"][b, hs, :], reads=[cx.tdram["qT"]], writes=[t_q])
            sy.dma("sp", kT[:], dram["kT"][b, hs, :], reads=[cx.tdram["kT"]], writes=[t_k])
            for (dst, t_d, nm) in ((kh, t_kh, "k_tm"), (vh, t_vh, "v_tm"), (gh, t_gh, "g_tm")):
                sy.dma("sp", dst[:], dram[nm][b].rearrange("(t p) c -> p t c", p=128)[:, :, hs], reads=[cx.tdram[nm]], writes=[t_d])
            qf, t_qf = qf_r.next(); qb, t_qb = qb_r.next(); kf, t_kf = kf_r.next(); kb, t_kb = kb_r.next(); gg, t_gg = gg_r.next()
            q3 = qT[:].rearrange("p (t c) -> p t c", c=128)
            sy.op("dve", lambda e: e.tensor_scalar(out=kf[:], in0=kh[:], scalar1=kd[:, h:h + 1], scalar2=None, op0=ALU.mult),
                  reads=[t_kh, t_c], writes=[t_kf])
            sy.op("dve", lambda e: e.tensor_scalar(out=kb[:], in0=kh[:], scalar1=kd[:, 4 + h:5 + h], scalar2=None, op0=ALU.mult),
                  reads=[t_kh, t_c], writes=[t_kb])
            yield
            sy.op("pool", lambda e: e.tensor_tensor(out=qf[:], in0=q3, in1=qd[:, h, :].unsqueeze(1).to_broadcast([128, NT, 128]), op=ALU.mult),
                  reads=[t_q, t_c], writes=[t_qf])
            yield
            sy.op("pool", lambda e: e.tensor_tensor(out=qb[:], in0=q3, in1=qd[:, 4 + h, :].unsqueeze(1).to_broadcast([128, NT, 128]), op=ALU.mult),
                  reads=[t_q, t_c], writes=[t_qb])
            yield
            sy.op("pool", lambda e: e.tensor_tensor(out=gg[:], in0=gh[:], in1=rng[:, hs].unsqueeze(1).to_broadcast([128, NT, 128]), op=ALU.mult),
                  reads=[t_gh, t_c], writes=[t_gg])
            yield
            S32 = [S32_r[0].next(), S32_r[1].next()]
            Sbf = [Sbf_r[0].next(), Sbf_r[1].next()]
            orders = (order_f, order_b)
            kxs = ((kf, t_kf), (kb, t_kb))
            for di in range(2):
                c0 = orders[di][0]
                sy.op("pool", lambda e: e.memset(S32[di][0][:, c0, :], 0.0), writes=[S32[di][1]])
            for idx in range(NC_ - 1):
                pt, tp = nb(pD, "d")
                for di in range(2):
                    c = orders[di][idx]
                    sy.op("pe", lambda e: e.matmul(pt[:, di * 128:(di + 1) * 128], kxs[di][0][:, c, :], vh[:, c, :], start=True, stop=True),
                          reads=[kxs[di][1], t_vh], writes=[tp])
                for di in range(2):
                    c = orders[di][idx]
                    cn = orders[di][idx + 1]
                    sy.op("dve", lambda e: e.scalar_tensor_tensor(out=S32[di][0][:, cn, :], in0=S32[di][0][:, c, :], scalar=cd[:, 4 * di + h:4 * di + h + 1],
                                                                  in1=pt[:, di * 128:(di + 1) * 128], op0=ALU.mult, op1=ALU.add),
                          reads=[tp, S32[di][1], t_c], writes=[S32[di][1]])
                yield
            for di in range(2):
                sy.op("act", lambda e: e.activation(out=Sbf[di][0][:], in_=S32[di][0][:], func=AF.Copy), reads=[S32[di][1]], writes=[Sbf[di][1]])
            C.update(b=b, h=h, qT=qT, t_q=t_q, kT=kT, t_k=t_k, vh=vh, t_vh=t_vh, qf=qf, t_qf=t_qf, qb=qb, t_qb=t_qb, gg=gg, t_gg=t_gg, Sbf=Sbf)
            yield

        def head_back(C):
            b, h = C["b"], C["h"]
            qT, t_q, kT, t_k, vh, t_vh = C["qT"], C["t_q"], C["kT"], C["t_k"], C["vh"], C["t_vh"]
            qf, t_qf, qb, t_qb, gg, t_gg, Sbf = C["qf"], C["t_qf"], C["qb"], C["t_qb"], C["gg"], C["t_gg"], C["Sbf"]
            yT, t_yT = yT_r.next()

            def group(g0):
                n = min(4, NC_ - g0)
                pst, tpst = nb(pST, "st")
                for ci in range(n):
                    c = g0 + ci
                    sy.op("pe", lambda e: e.matmul(pst[:, ci * 128:(ci + 1) * 128], kT[:, c * 128:(c + 1) * 128], qT[:, c * 128:(c + 1) * 128],
                                                   start=True, stop=True), reads=[t_k, t_q], writes=[tpst])
                yield
                PT, t_PT = PT_r.next()
                sy.op("dve", lambda e: e.tensor_tensor(out=PT[:, 0:n, :], in0=pst[:, 0:n * 128].rearrange("p (a c) -> p a c", c=128),
                                                       in1=maskT[:, h, :].unsqueeze(1).to_broadcast([128, n, 128]), op=ALU.mult),
                      reads=[tpst, t_c], writes=[t_PT])
                yield
                po, tpo = nb(pO, "o")
                for ci in range(n):
                    c = g0 + ci
                    osl = po[:, ci * 128:(ci + 1) * 128]
                    sy.op("pe", lambda e: e.matmul(osl, PT[:, ci, :], vh[:, c, :], start=True, stop=False), reads=[t_PT, t_vh], writes=[tpo])
                    sy.op("pe", lambda e: e.matmul(osl, qf[:, c, :], Sbf[0][0][:, c, :], start=False, stop=False), reads=[t_qf, Sbf[0][1]], writes=[tpo])
                    sy.op("pe", lambda e: e.matmul(osl, qb[:, c, :], Sbf[1][0][:, c, :], start=False, stop=True), reads=[t_qb, Sbf[1][1]], writes=[tpo])
                yield
                o3 = po[:, 0:n * 128].rearrange("p (a c) -> p a c", c=128)
                stt, t_st = st_r.next()
                oc, t_oc = oc_r.next(); sq, t_sq = sq_r.next(); yy, t_yy = y_r.next()
                sy.op("dve", lambda e: e.tensor_reduce(out=stt[:, 0:n], in_=o3, axis=AX.X, op=ALU.add), reads=[tpo], writes=[t_st])
                yield
                sy.op("dve", lambda e: e.tensor_scalar(out=stt[:, 4:4 + n], in0=stt[:, 0:n], scalar1=-1.0 / 128, scalar2=None, op0=ALU.mult),
                      reads=[t_st], writes=[t_st])
                yield
                sy.op("dve", lambda e: e.tensor_tensor(out=oc[:, 0:n, :], in0=o3, in1=stt[:, 4:4 + n].unsqueeze(2).to_broadcast([128, n, 128]), op=ALU.add),
                      reads=[tpo, t_st], writes=[t_oc])
                yield
                sy.op("pool", lambda e: e.tensor_tensor(out=sq[:, 0:n, :], in0=oc[:, 0:n, :], in1=oc[:, 0:n, :], op=ALU.mult), reads=[t_oc], writes=[t_sq])
                yield
                sy.op("dve", lambda e: e.tensor_reduce(out=stt[:, 8:8 + n], in_=sq[:, 0:n, :], axis=AX.X, op=ALU.add), reads=[t_sq], writes=[t_st])
                yield
                sy.op("act", lambda e: e.activation(out=stt[:, 8:8 + n], in_=stt[:, 8:8 + n], func=AF.Sqrt, scale=1.0 / 128, bias=cx.eps_c[:, 0:1]),
                      reads=[t_st, cx.t_eps], writes=[t_st])
                yield
                sy.op("dve", lambda e: e.reciprocal(out=stt[:, 12:12 + n], in_=stt[:, 8:8 + n]), reads=[t_st], writes=[t_st])
                yield
                sy.op("dve", lambda e: e.tensor_tensor(out=oc[:, 0:n, :], in0=oc[:, 0:n, :], in1=stt[:, 12:12 + n].unsqueeze(2).to_broadcast([128, n, 128]), op=ALU.mult),
                      reads=[t_oc, t_st], writes=[t_oc])
                yield
                sy.op("pool", lambda e: e.tensor_tensor(out=yy[:, 0:n, :], in0=oc[:, 0:n, :], in1=gg[:, g0:g0 + n, :], op=ALU.mult),
                      reads=[t_oc, t_gg], writes=[t_yy])
                yield
                ptr, tptr = nb(pTr, "tr")
                ptb = ptr[:].bitcast(BF16)
                for ci in range(n):
                    sy.op("pe", lambda e: e.transpose(out=ptb[:, ci * 128:(ci + 1) * 128], in_=yy[:, ci, :], identity=cx.ident_bf[:]),
                          reads=[t_yy, cx.t_const], writes=[tptr])
                yield
                sy.op("act", lambda e: e.activation(out=yT[:, g0 * 128:(g0 + n) * 128], in_=ptb[:, 0:n * 128], func=AF.Copy),
                      reads=[tptr], writes=[t_yT])
                yield

            for pair in ((0, 4), (8, 12), (16,)):
                gens = [group(g0) for g0 in pair]
                while gens:
                    for g in list(gens):
                        try:
                            next(g)
                        except StopIteration:
                            gens.remove(g)
                    yield
            sy.dma("pool", dram["yT"][b, 512 + h * 128:512 + (h + 1) * 128, :], yT[:], reads=[t_yT], writes=[cx.tdram["yT"]])
            yield

        hl = [(b, h) for b in range(NB) for h in range(4)]
        Cs = [dict() for _ in hl]
        interleave([head_front(hl[0][0], hl[0][1], Cs[0])])
        for i in range(len(hl)):
            gens = [head_back(Cs[i])]
            if i + 1 < len(hl):
                gens.append(head_front(hl[i + 1][0], hl[i + 1][1], Cs[i + 1]))
            interleave(gens)


def phase3(cx):
    sy, nc, NB, dram = cx.sy, cx.nc, cx.NB, cx.dram
    MAGIC = 12582912.0
    with contextlib.ExitStack() as ph:
        Kx = sy.sb("Kx", [128, 32, 512], BF16, ph)
        Kc = sy.sb("Kc", [128, 4, 512], BF16, ph)
        t_K = T()
        Fc = sy.sb("Fc", [128, 2, 512], BF16, ph)
        Fic = sy.sb("Fic", [128, 4, 256], BF16, ph)
        hbias = sy.sb("hbias", [128, 4], F32, ph)
        t_fc = T()
        sy.dma("sp", Fc[:], dram["k_ffwd_c"], writes=[t_fc])
        sy.dma("sp", Fic[:], dram["k_finv_c"], writes=[t_fc])
        sy.dma("sp", hbias[:], dram["hy_bias"], writes=[t_fc])
        F_r = Ring(sy, ph, "Fb", 2, [128, 16, 512], BF16)
        with contextlib.ExitStack() as fs:
            w1 = sy.sb("w1", [33, 64], F32, fs); w2 = sy.sb("w2", [64, 64], F32, fs); w3 = sy.sb("w3", [64, 1024], F32, fs)
            fb = sy.sb("fb", [64, 6], F32, fs)
            t_w = T()
            sy.dma("sp", w1[:], dram["hy_f_w1"], writes=[t_w]); sy.dma("sp", w2[:], dram["hy_f_w2"], writes=[t_w])
            sy.dma("sp", w3[:], dram["hy_f_w3"], writes=[t_w])
            sy.dma("sp", fb[:, 0:1], dram["hy_f_b1"], writes=[t_w]); sy.dma("sp", fb[:, 1:2], dram["hy_f_freq"], writes=[t_w])
            sy.dma("sp", fb[:, 2:3], dram["hy_f_b2"], writes=[t_w])
            sy.op("dve", lambda e: e.tensor_tensor(out=fb[:, 3:4], in0=fb[:, 0:1], in1=fb[:, 1:2], op=ALU.mult), reads=[t_w], writes=[t_w])
            sy.op("dve", lambda e: e.tensor_tensor(out=fb[:, 4:5], in0=fb[:, 2:3], in1=fb[:, 1:2], op=ALU.mult), reads=[t_w], writes=[t_w])
            zT = sy.sb("zT", [33, L], F32, fs)
            h1 = sy.sb("h1", [64, L], F32, fs)
            h2 = sy.sb("h2", [64, L], F32, fs)
            hfb = sy.sb("hfb", [128, 16, 1024], BF16, fs)
            tA = sy.sb("tA", [64, 512], F32, fs)
            tB = sy.sb("tB", [64, 512], F32, fs)
            win_r = Ring(sy, fs, "win", 2, [128, 512], F32)
            tmpA_r = Ring(sy, fs, "tmpA", 2, [128, 512], F32)
            for (Ls, zname, wname, Kdst, nct) in ((L, "k_zT", "k_win", Kx, 16), (LC, "k_zcT", "k_winc", Kc, 2)):
                t_z, t_h1, t_h2, t_hfb, t_t = T(), T(), T(), T(), T()
                sy.dma("sp", zT[:, 0:Ls], dram[zname], writes=[t_z])

                def sin_layer(wm, kdim, src, t_src, dst, t_dst, bcol):
                    for n0 in range(0, Ls, 512):
                        n = min(512, Ls - n0)
                        pt, tp = cx.psum[(n0 // 512) % 2]
                        sy.op("pe", lambda e: e.matmul(pt[0:64, 0:n], wm[0:kdim, :], src[0:kdim, n0:n0 + n], start=True, stop=True),
                              reads=[t_w, t_src], writes=[tp])
                        sy.op("dve", lambda e: e.tensor_scalar(out=tA[:, 0:n], in0=pt[0:64, 0:n], scalar1=fb[:, 1:2], scalar2=fb[:, bcol:bcol + 1],
                                                               op0=ALU.mult, op1=ALU.add), reads=[tp, t_w], writes=[t_t])
                        sy.op("dve", lambda e: e.tensor_scalar(out=tB[:, 0:n], in0=tA[:, 0:n], scalar1=1.0 / (2 * PI), scalar2=MAGIC,
                                                               op0=ALU.mult, op1=ALU.add), reads=[t_t], writes=[t_t])
                        sy.op("dve", lambda e: e.tensor_scalar(out=tB[:, 0:n], in0=tB[:, 0:n], scalar1=-MAGIC, scalar2=None, op0=ALU.add),
                              reads=[t_t], writes=[t_t])
                        sy.op("dve", lambda e: e.scalar_tensor_tensor(out=tA[:, 0:n], in0=tB[:, 0:n], scalar=-2 * PI, in1=tA[:, 0:n],
                                                                      op0=ALU.mult, op1=ALU.add), reads=[t_t], writes=[t_t])
                        sy.op("dve", lambda e: e.tensor_scalar(out=tA[:, 0:n], in0=tA[:, 0:n], scalar1=-PI, scalar2=PI, op0=ALU.max, op1=ALU.min),
                              reads=[t_t], writes=[t_t])
                        sy.op("act", lambda e: e.activation(out=dst[:, n0:n0 + n], in_=tA[:, 0:n], func=AF.Sin), reads=[t_t], writes=[t_dst])

                sin_layer(w1, 33, zT, t_z, h1, t_h1, 3)
                sin_layer(w2, 64, h1, t_h1, h2, t_h2, 4)
                for c in range(nct):
                    wn, t_wn = win_r.next()
                    sy.dma("sp", wn[:], dram[wname][:, c, :], writes=[t_wn])
                    for half in range(2):
                        pt, tp = cx.psum[2 + half]
                        sy.op("pe", lambda e: e.matmul(pt[:], h2[:, c * 128:(c + 1) * 128], w3[:, half * 512:(half + 1) * 512], start=True, stop=True),
                              reads=[t_h2, t_w], writes=[tp])
                        sy.op("dve", lambda e: e.tensor_tensor(out=hfb[:, c, half * 512:(half + 1) * 512], in0=pt[:], in1=wn[:], op=ALU.mult),
                              reads=[tp, t_wn], writes=[t_hfb])
                sy.op("dve", lambda e: e.memset(hfb[0:1, 0, 512:1024], 0.0), reads=[t_hfb], writes=[t_hfb])
                if "dbg_h" in cx.tdram and Ls == L:
                    pass
                ngrp = 8 if Ls == L else 1
                for g in range(ngrp):
                    if Ls == L:
                        Fb, t_F = F_r.next()
                        sy.dma("sp", Fb[:], dram["k_ffwd"][g], writes=[t_F])
                        Fsrc = Fb
                    else:
                        Fsrc, t_F = Fc, t_fc
                    for j in range(4):
                        pA, tpA = cx.psum[4 + (j % 2) * 2]
                        pB, tpB = cx.psum[5 + (j % 2) * 2]
                        for (pp, tpp, c0) in ((pA, tpA, 0), (pB, tpB, 512)):
                            for c in range(nct):
                                sy.op("pe", lambda e: e.matmul(pp[:], Fsrc[:, c, j * 128:(j + 1) * 128], hfb[:, c, c0:c0 + 512],
                                                               start=(c == 0), stop=(c == nct - 1)), reads=[t_F, t_hfb], writes=[tpp])
                        tm, t_tm = tmpA_r.next()
                        sy.op("act", lambda e: e.activation(out=tm[:], in_=pA[:], func=AF.Copy), reads=[tpA], writes=[t_tm])
                        if Ls == L:
                            ti = (2 * g + j) if j < 2 else (16 + 2 * g + j - 2)
                        else:
                            ti = j
                        sy.op("dve", lambda e: e.tensor_tensor(out=Kdst[:, ti, :], in0=tm[:], in1=pB[:], op=(ALU.add if j < 2 else ALU.subtract)),
                              reads=[t_tm, tpB], writes=[t_K])
            sy.barrier()
        utm = sy.sb("utm3", [128, NT, 512], BF16, ph)
        t_utm = T()
        Y = sy.sb("Y", [128, 32, 512], BF16, ph)
        Yc = sy.sb("Yc", [128, 4, 512], BF16, ph)
        t_Y, t_Yc = T(), T()
        Fi_r = Ring(sy, ph, "Fib", 2, [128, 8, 512], BF16)
        us_r = Ring(sy, ph, "us", 4, [128, 512], F32)
        tt_r = Ring(sy, ph, "ttm", 4, [128, 512], F32)
        uw_r = Ring(sy, ph, "uw", 4, [128, 2, 512], BF16)
        yb_r = Ring(sy, ph, "yb", 2, [128, 512], BF16)
        tmp_r = Ring(sy, ph, "tmp3", 2, [128, 512], F32)

        def spec_mul(pre, tpre, pim, tpim, Kt, ire, iim, Yt, t_Yt):
            ure, t_ure = us_r.next(); uim, t_uim = us_r.next()
            sy.op("act", lambda e: e.activation(out=ure[:], in_=pre[:], func=AF.Copy), reads=[tpre], writes=[t_ure])
            sy.op("act", lambda e: e.activation(out=uim[:], in_=pim[:], func=AF.Copy), reads=[tpim], writes=[t_uim])
            t1, t_t1 = tt_r.next(); t2, t_t2 = tt_r.next(); t3, t_t3 = tt_r.next(); t4, t_t4 = tt_r.next()
            sy.op("dve", lambda e: e.tensor_tensor(out=t1[:], in0=ure[:], in1=Kt[:, ire, :], op=ALU.mult), reads=[t_ure, t_K], writes=[t_t1])
            sy.op("dve", lambda e: e.tensor_tensor(out=t2[:], in0=uim[:], in1=Kt[:, iim, :], op=ALU.mult), reads=[t_uim, t_K], writes=[t_t2])
            sy.op("dve", lambda e: e.tensor_tensor(out=Yt[:, ire, :], in0=t1[:], in1=t2[:], op=ALU.subtract), reads=[t_t1, t_t2], writes=[t_Yt])
            sy.op("pool", lambda e: e.tensor_tensor(out=t3[:], in0=ure[:], in1=Kt[:, iim, :], op=ALU.mult), reads=[t_ure, t_K], writes=[t_t3])
            sy.op("pool", lambda e: e.tensor_tensor(out=t4[:], in0=uim[:], in1=Kt[:, ire, :], op=ALU.mult), reads=[t_uim, t_K], writes=[t_t4])
            sy.op("pool", lambda e: e.tensor_tensor(out=Yt[:, iim, :], in0=t3[:], in1=t4[:], op=ALU.add), reads=[t_t3, t_t4], writes=[t_Yt])

        def combine(b, ct, pacc, tpacc, tok0, n):
            uw, t_uw = uw_r.next()
            sy.dma("sp", uw[:, 0, 0:n], dram["u_fm"][b, ct * 128:(ct + 1) * 128, tok0:tok0 + n], reads=[cx.tdram["u_fm"]], writes=[t_uw])
            sy.dma("sp", uw[:, 1, 0:n], dram["w_fm"][b, ct * 128:(ct + 1) * 128, tok0:tok0 + n], reads=[cx.tdram["w_fm"]], writes=[t_uw])
            tm, t_tm = tmp_r.next()
            sy.op("dve", lambda e: e.scalar_tensor_tensor(out=tm[:, 0:n], in0=uw[:, 0, 0:n], scalar=hbias[:, ct:ct + 1], in1=pacc[:, 0:n],
                                                          op0=ALU.mult, op1=ALU.add), reads=[t_uw, tpacc, t_fc], writes=[t_tm])
            yb, t_yb = yb_r.next()
            sy.op("pool", lambda e: e.tensor_tensor(out=yb[:, 0:n], in0=tm[:, 0:n], in1=uw[:, 1, 0:n], op=ALU.mult), reads=[t_tm, t_uw], writes=[t_yb])
            sy.dma("pool", dram["yT"][b, ct * 128:(ct + 1) * 128, tok0:tok0 + n], yb[:, 0:n], reads=[t_yb], writes=[cx.tdram["yT"]])

        for b in range(NB):
            sy.dma("sp", utm[:], dram["u_tm"][b].rearrange("(t p) c -> p t c", p=128), reads=[cx.tdram["u_tm"]], writes=[t_utm])
            for j in range(4):
                pt, tp = cx.psum[j]
                for c in range(2):
                    sy.op("pe", lambda e: e.matmul(pt[:], Fc[:, c, j * 128:(j + 1) * 128], utm[:, c, :], start=(c == 0), stop=(c == 1)),
                          reads=[t_fc, t_utm], writes=[tp])
            for jj in range(2):
                spec_mul(cx.psum[jj][0], cx.psum[jj][1], cx.psum[2 + jj][0], cx.psum[2 + jj][1], Kc, jj, 2 + jj, Yc, t_Yc)
            for ct in range(4):
                pt, tp = cx.psum[4 + ct]
                for f in range(4):
                    sy.op("pe", lambda e: e.matmul(pt[:, 0:256], Yc[:, f, ct * 128:(ct + 1) * 128], Fic[:, f, :], start=(f == 0), stop=(f == 3)),
                          reads=[t_Yc, t_fc], writes=[tp])
                combine(b, ct, pt, tp, 0, 256)
            for g in range(8):
                Fb, t_F = F_r.next()
                sy.dma("sp", Fb[:], dram["k_ffwd"][g], writes=[t_F])
                for j in range(4):
                    pt, tp = cx.psum[j]
                    for c in range(16):
                        sy.op("pe", lambda e: e.matmul(pt[:], Fb[:, c, j * 128:(j + 1) * 128], utm[:, 2 + c, :], start=(c == 0), stop=(c == 15)),
                              reads=[t_F, t_utm], writes=[tp])
                for jj in range(2):
                    spec_mul(cx.psum[jj][0], cx.psum[jj][1], cx.psum[2 + jj][0], cx.psum[2 + jj][1], Kx, 2 * g + jj, 16 + 2 * g + jj, Y, t_Y)
            for ng in range(4):
                for fq in range(4):
                    Fi, t_Fi = Fi_r.next()
                    sy.dma("sp", Fi[:], dram["k_finv"][ng, fq], writes=[t_Fi])
                    for ct in range(4):
                        pt, tp = cx.psum[4 + ct]
                        for f in range(8):
                            sy.op("pe", lambda e: e.matmul(pt[:], Y[:, fq * 8 + f, ct * 128:(ct + 1) * 128], Fi[:, f, :],
                                                           start=(fq == 0 and f == 0), stop=(fq == 3 and f == 7)), reads=[t_Y, t_Fi], writes=[tp])
                for ct in range(4):
                    combine(b, ct, cx.psum[4 + ct][0], cx.psum[4 + ct][1], 256 + ng * 512, 512)


def gate_rows(cx, l, b, gbc, t_g):
    cx.sy.dma("sp", gbc[:], cx.dram["modrows"][l, b, 2 * D:3 * D].partition_broadcast(128), reads=[cx.tdram["modrows"]], writes=[t_g])


def phase4(cx):
    sy, nc, NB, dram = cx.sy, cx.nc, cx.NB, cx.dram
    with contextlib.ExitStack() as ph:
        ow = sy.sb("ow", [128, 8, D], BF16, ph)
        t_ow = T()
        load_w_bf16(cx, ow, dram["er_out_w"], D, t_ow)
        gc = sy.sb("gc", [128, D], F32, ph)
        t_gc = T()
        gate_rows(cx, 0, NB, gc, t_gc)
        gx_r = Ring(sy, ph, "gx", 2, [128, D], F32)
        yTs_r = Ring(sy, ph, "yTs", 2, [128, 8, TOK], BF16)
        xt_r = Ring(sy, ph, "xt4", 3, [128, D], F32)
        tm_r = Ring(sy, ph, "tm4", 3, [128, D], F32)
        bank = [0]
        for b in range(NB):
            gx, t_gx = gx_r.next()
            gate_rows(cx, 0, b, gx, t_gx)
            yTs, t_y = yTs_r.next()
            for k in range(8):
                sy.dma("sp", yTs[:, k, :], dram["yT"][b, k * 128:(k + 1) * 128, :], reads=[cx.tdram["yT"]], writes=[t_y])
            for tt in range(NT):
                src, bsel, tsrc = tok_src(cx, b, tt, 0)
                xt, t_x = xt_r.next()
                sy.dma("sp", xt[:], src, writes=[t_x])
                tm, t_tm = tm_r.next()
                g_, t_g_ = (gc, t_gc) if tt < 2 else (gx, t_gx)
                for half in range(2):
                    pt, tp = cx.psum[bank[0] % 6]
                    bank[0] += 1
                    hs = slice(half * 512, (half + 1) * 512)
                    for k in range(8):
                        sy.op("pe", lambda e: e.matmul(pt[:], yTs[:, k, tt * 128:(tt + 1) * 128], ow[:, k, hs], start=(k == 0), stop=(k == 7)),
                              reads=[t_y, t_ow], writes=[tp])
                    sy.op("dve", lambda e: e.tensor_tensor(out=tm[:, hs], in0=pt[:], in1=g_[:, hs], op=ALU.mult), reads=[tp, t_g_], writes=[t_tm])
                sy.op("pool", lambda e: e.tensor_tensor(out=tm[:], in0=tm[:], in1=xt[:], op=ALU.add), reads=[t_tm, t_x], writes=[t_tm])
                sy.dma("pool", dram["x1"][b, tt * 128:(tt + 1) * 128, :], tm[:], reads=[t_tm], writes=[cx.tdram["x1"]])


def head_norm_rope(cx, src, t_src, nh, gbc, t_c, tabs, pos_tile, dst, t_dst, tmp):
    sy = cx.sy
    sq, t_sq, st, t_st, rc, rd, t_r = tmp
    W = nh * 128
    s3 = src[:, 0:W].rearrange("p (h d) -> p h d", d=128)
    sy.op("pool", lambda e: e.tensor_tensor(out=sq[:, 0:W], in0=src[:, 0:W], in1=src[:, 0:W], op=ALU.mult), reads=[t_src], writes=[t_sq])
    yield
    sy.op("dve", lambda e: e.tensor_reduce(out=st[:, 0:nh], in_=sq[:, 0:W].rearrange("p (h d) -> p h d", d=128), axis=AX.X, op=ALU.add),
          reads=[t_sq], writes=[t_st])
    yield
    sy.op("act", lambda e: e.activation(out=st[:, 8:8 + nh], in_=st[:, 0:nh], func=AF.Sqrt, scale=1.0 / 128, bias=cx.eps_c[:, 0:1]),
          reads=[t_st, cx.t_eps], writes=[t_st])
    yield
    sy.op("dve", lambda e: e.reciprocal(out=st[:, 16:16 + nh], in_=st[:, 8:8 + nh]), reads=[t_st], writes=[t_st])
    yield
    sy.op("dve", lambda e: e.tensor_tensor(out=s3, in0=s3, in1=st[:, 16:16 + nh].unsqueeze(2).to_broadcast([128, nh, 128]), op=ALU.mult),
          reads=[t_src, t_st], writes=[t_src])
    yield
    if pos_tile is None:
        sy.op("pool", lambda e: e.tensor_tensor(out=dst[:, 0:W].rearrange("p (h d) -> p h d", d=128), in0=s3,
                                                in1=gbc[:].unsqueeze(1).to_broadcast([128, nh, 128]), op=ALU.mult), reads=[t_src, t_c], writes=[t_dst])
        yield
        return
    sy.op("pool", lambda e: e.tensor_tensor(out=s3, in0=s3, in1=gbc[:].unsqueeze(1).to_broadcast([128, nh, 128]), op=ALU.mult),
          reads=[t_src, t_c], writes=[t_src])
    yield
    rcos, rsin = tabs
    s4 = src[:, 0:W].rearrange("p (h m two) -> p h m two", two=2, m=64)
    d4 = dst[:, 0:W].rearrange("p (h m two) -> p h m two", two=2, m=64)
    x0, x1 = s4[:, :, :, 0], s4[:, :, :, 1]
    cb = rcos[:, pos_tile, :].unsqueeze(1).to_broadcast([128, nh, 64])
    sb_ = rsin[:, pos_tile, :].unsqueeze(1).to_broadcast([128, nh, 64])
    ra = sq[:, 0:nh * 64].rearrange("p (h m) -> p h m", m=64)
    rb = sq[:, nh * 64:2 * nh * 64].rearrange("p (h m) -> p h m", m=64)
    sy.op("dve", lambda e: e.tensor_tensor(out=ra, in0=x0, in1=cb, op=ALU.mult), reads=[t_src, t_c, t_st], writes=[t_sq])
    sy.op("pool", lambda e: e.tensor_tensor(out=rc[:, 0:nh, :], in0=x0, in1=sb_, op=ALU.mult), reads=[t_src, t_c], writes=[t_r[0]])
    yield
    sy.op("dve", lambda e: e.tensor_tensor(out=rb, in0=x1, in1=sb_, op=ALU.mult), reads=[t_src, t_c], writes=[t_sq])
    sy.op("pool", lambda e: e.tensor_tensor(out=rd[:, 0:nh, :], in0=x1, in1=cb, op=ALU.mult), reads=[t_src, t_c], writes=[t_r[1]])
    yield
    sy.op("dve", lambda e: e.tensor_tensor(out=d4[:, :, :, 0], in0=ra, in1=rb, op=ALU.subtract), reads=[t_sq], writes=[t_dst])
    sy.op("pool", lambda e: e.tensor_tensor(out=d4[:, :, :, 1], in0=rc[:, 0:nh, :], in1=rd[:, 0:nh, :], op=ALU.add),
          reads=[t_r[0], t_r[1]], writes=[t_dst])
    yield


def layer1(cx):
    l1_prep(cx)
    cx.sy.barrier()
    l1_attn(cx)


def l1_prep(cx):
    sy, nc, NB, dram = cx.sy, cx.nc, cx.NB, cx.dram
    with contextlib.ExitStack() as ph:
        mk_eps(cx, ph)
        inw = sy.sb("ainw", [128, 8, 2560], BF16, ph)
        t_inw = T()
        load_w_bf16(cx, inw, dram["at_in_w"], 2560, t_inw)
        rcos = sy.sb("rcos", [128, 16, 64], F32, ph); rsin = sy.sb("rsin", [128, 16, 64], F32, ph)
        gq = sy.sb("gq", [128, 128], F32, ph); gk = sy.sb("gk", [128, 128], F32, ph)
        t_c = T()
        sy.dma("sp", rcos[:], dram["k_rcos"], writes=[t_c]); sy.dma("sp", rsin[:], dram["k_rsin"], writes=[t_c])
        sy.dma("sp", gq[:], dram["at_q_norm"].partition_broadcast(128), writes=[t_c])
        sy.dma("sp", gk[:], dram["at_k_norm"].partition_broadcast(128), writes=[t_c])
        xt_ring = Ring(sy, ph, "xtl", 4, [128, D], F32)
        scr = Ring(sy, ph, "scrl", 4, [128, D], BF16)
        small = Ring(sy, ph, "sml", 8, [128, 4], F32)
        hxa_r = Ring(sy, ph, "hxa", 2, [128, 8, 128], BF16)
        hxq_r = Ring(sy, ph, "hxq", 2, [128, 8, 512], BF16)
        kvf_r = Ring(sy, ph, "kvf", 4, [128, 256], F32)
        qf_r = Ring(sy, ph, "qf32", 4, [128, D], F32)
        qr_r = Ring(sy, ph, "qr", 4, [128, D], BF16)
        kr_r = Ring(sy, ph, "kr", 4, [128, 256], BF16)
        sq_r = Ring(sy, ph, "sqt", 4, [128, D], F32)
        st_r = Ring(sy, ph, "lst", 8, [128, 24], F32)
        rt_r = [Ring(sy, ph, f"rt{i}", 4, [128, 8, 64], F32) for i in range(2)]
        sqk_r = Ring(sy, ph, "sqk", 4, [128, 256], F32)
        stk_r = Ring(sy, ph, "lstk", 8, [128, 24], F32)
        rtk_r = [Ring(sy, ph, f"rtk{i}", 4, [128, 2, 64], F32) for i in range(2)]
        kTst_r = Ring(sy, ph, "kTst", 1, [128, 2, TOK], BF16)
        vst_r = Ring(sy, ph, "vst", 1, [128, NT, 256], BF16)
        qTst_r = Ring(sy, ph, "qTst", 2, [128, 8, 512], BF16)
        gst_r = Ring(sy, ph, "gst", 3, [128, 512], BF16)
        pM = [cx.psum[i] for i in range(8)]
        cnt = {"m": 0}

        def nb():
            r = pM[cnt["m"] % len(pM)]
            cnt["m"] += 1
            return r

        def tmpset():
            sq, t_sq = sq_r.next()
            st, t_st = st_r.next()
            rs = [r.next() for r in rt_r]
            return (sq, t_sq, st, t_st, rs[0][0], rs[1][0], [r[1] for r in rs])

        def tmpset_k():
            sq, t_sq = sqk_r.next()
            st, t_st = stk_r.next()
            rs = [r.next() for r in rtk_r]
            return (sq, t_sq, st, t_st, rs[0][0], rs[1][0], [r[1] for r in rs])

        gate_gens = []
        for b in range(NB):
            kTst, t_kTst = kTst_r.next()
            vst, t_vst = vst_r.next()

            def kv_tile(tt, hx, t_hx, c0):
                pt, tp = nb()
                for k in range(8):
                    sy.op("pe", lambda e: e.matmul(pt[:], hx[:, k, c0:c0 + 128], inw[:, k, 1024:1536], start=(k == 0), stop=(k == 7)),
                          reads=[t_hx, t_inw], writes=[tp])
                kvf, t_kvf = kvf_r.next()
                sy.op("act", lambda e: e.activation(out=kvf[:], in_=pt[:, 0:256], func=AF.Copy), reads=[tp], writes=[t_kvf])
                sy.op("act", lambda e: e.activation(out=vst[:, tt, :], in_=pt[:, 256:512], func=AF.Copy), reads=[tp], writes=[t_vst])
                yield
                kr, t_kr = kr_r.next()
                yield from head_norm_rope(cx, kvf, t_kvf, 2, gk, t_c, (rcos, rsin), (tt - 2) if tt >= 2 else None, kr, t_kr, tmpset_k())
                pt2, tp2 = nb()
                ptb = pt2[:].bitcast(BF16)
                for g in range(2):
                    sy.op("pe", lambda e: e.transpose(out=ptb[:, g * 128:(g + 1) * 128], in_=kr[:, g * 128:(g + 1) * 128], identity=cx.ident_bf[:]),
                          reads=[t_kr, cx.t_const], writes=[tp2])
                sy.op("act", lambda e: e.activation(out=kTst[:, :, tt * 128:(tt + 1) * 128], in_=ptb[:, 0:256].rearrange("p (g c) -> p g c", c=128),
                                                    func=AF.Copy), reads=[tp2], writes=[t_kTst])
                yield

            def ctx_tile(tt):
                src, bsel, tsrc = tok_src(cx, b, tt, 1)
                hxa, t_hxa = hxa_r.next()
                yield from norm_tile_g(cx, 1, bsel, src, xt_ring, scr, hxa, 0, t_hxa, nb(), small, tsrc)
                yield from kv_tile(tt, hxa, t_hxa, 0)

            def x_tile(qg, i, hxq, t_hxq_i, qTst, t_qTst, flags):
                tt = 2 + qg * 4 + i
                src, bsel, tsrc = tok_src(cx, b, tt, 1)
                yield from norm_tile_g(cx, 1, bsel, src, xt_ring, scr, hxq, i * 128, t_hxq_i, nb(), small, tsrc)
                flags[i] = True
                qf, t_qf = qf_r.next()
                for half in range(2):
                    pt, tp = nb()
                    for k in range(8):
                        sy.op("pe", lambda e: e.matmul(pt[:], hxq[:, k, i * 128:(i + 1) * 128], inw[:, k, half * 512:(half + 1) * 512],
                                                       start=(k == 0), stop=(k == 7)), reads=[t_hxq_i, t_inw], writes=[tp])
                    sy.op("act", lambda e: e.activation(out=qf[:, half * 512:(half + 1) * 512], in_=pt[:], func=AF.Copy), reads=[tp], writes=[t_qf])
                    yield
                qr, t_qr = qr_r.next()
                yield from head_norm_rope(cx, qf, t_qf, 8, gq, t_c, (rcos, rsin), tt - 2, qr, t_qr, tmpset())
                pt2, tp2 = nb()
                ptb = pt2[:].bitcast(BF16)
                for h in range(8):
                    sy.op("pe", lambda e: e.transpose(out=ptb[:, h * 128:(h + 1) * 128], in_=qr[:, h * 128:(h + 1) * 128], identity=cx.ident_bf[:]),
                          reads=[t_qr, cx.t_const], writes=[tp2])
                sy.op("act", lambda e: e.activation(out=qTst[:, :, i * 128:(i + 1) * 128], in_=ptb[:].rearrange("p (h c) -> p h c", c=128), func=AF.Copy),
                      reads=[tp2], writes=[t_qTst])
                yield

            def x_tile_kv(qg, i, hxq, t_hxq_i, flags):
                while not flags[i]:
                    yield
                yield from kv_tile(2 + qg * 4 + i, hxq, t_hxq_i, i * 128)

            def gate_fm(b, qg, hxq, t_hxq):
                for ft in range(8):
                    pt, tp = nb()
                    for k in range(8):
                        sy.op("pe", lambda e: e.matmul(pt[:], inw[:, k, 1536 + ft * 128:1536 + (ft + 1) * 128], hxq[:, k, :], start=(k == 0), stop=(k == 7)),
                              reads=t_hxq + [t_inw], writes=[tp])
                    gst, t_gst = gst_r.next()
                    sy.op("act", lambda e: e.activation(out=gst[:], in_=pt[:], func=AF.Silu), reads=[tp], writes=[t_gst])
                    sy.dma("pool", dram["agT"][b, ft * 128:(ft + 1) * 128, qg * 512:(qg + 1) * 512], gst[:], reads=[t_gst], writes=[cx.tdram["agT"]])
                    yield
                    yield
                    yield

            interleave([ctx_tile(0), ctx_tile(1)] + gate_gens)
            gate_gens.clear()
            for qg in range(4):
                hxq, _ = hxq_r.next()
                t_hxq = [T() for _ in range(4)]
                qTst, t_qTst = qTst_r.next()
                flags = [False] * 4
                interleave([x_tile(qg, i, hxq, t_hxq[i], qTst, t_qTst, flags) for i in range(4)]
                           + [x_tile_kv(qg, i, hxq, t_hxq[i], flags) for i in range(4)] + gate_gens)
                gate_gens.clear()
                sy.dma("pool", dram["aqT"][b].rearrange("h d t -> d h t")[:, :, qg * 512:(qg + 1) * 512], qTst[:], reads=[t_qTst], writes=[cx.tdram["aqT"]])
                gate_gens.append(gate_fm(b, qg, hxq, t_hxq))
            sy.dma("pool", dram["akT"][b].rearrange("g d t -> d g t"), kTst[:], reads=[t_kTst], writes=[cx.tdram["akT"]])
            sy.dma("pool", dram["av"][b].rearrange("(t p) c -> p t c", p=128), vst[:], reads=[t_vst], writes=[cx.tdram["av"]])
        interleave(gate_gens)


def l1_attn(cx):
    sy, nc, NB, dram = cx.sy, cx.nc, cx.NB, cx.dram
    ks = 128.0 ** -0.5
    KSPLIT = 9
    with contextlib.ExitStack() as ph:
        mk_eps(cx, ph)
        ow = sy.sb("aow", [128, 8, D], BF16, ph)
        t_ow = T()
        load_w_bf16(cx, ow, dram["at_out_w"], D, t_ow)
        fng = sy.sb("fng", [128, D], F32, ph)
        ones = sy.sb("ones", [128, 128], BF16, ph)
        t_c = T()
        sy.dma("sp", fng[:], dram["final_norm_g"].partition_broadcast(128), writes=[t_c])
        sy.op("pool", lambda e: e.memset(ones[:], 1.0), writes=[t_c])
        mhalf = sy.sb("mhalf", [128, 1], F32, ph)
        sy.op("pool", lambda e: e.memset(mhalf[:], -0.5), writes=[t_c])
        kT_r = Ring(sy, ph, "akT", 2, [128, 2, TOK], BF16)
        v_r = Ring(sy, ph, "av", 2, [128, NT, 256], BF16)
        qT_r = Ring(sy, ph, "aqT", 2, [128, 8, 512], BF16)
        sg_r = Ring(sy, ph, "asg", 2, [128, 8, 512], BF16)
        ogT_r = Ring(sy, ph, "ogT", 2, [128, 8, 512], BF16)
        PTh = [sy.sb(f"PTh{i}", [128, NT, 512], BF16, ph) for i in range(3)]
        t_PTh = [[T() for _ in range(NT)] for _ in range(3)]
        pab_r = [Ring(sy, ph, "pabA", 6, [128, 512], BF16)]
        rden_r = Ring(sy, ph, "rden", 2, [128, 512], F32)
        pacc_r = Ring(sy, ph, "pacc", 2, [128, 512], F32)
        tmo_r = Ring(sy, ph, "tmo", 2, [128, 512], F32)
        xt_r = Ring(sy, ph, "xta", 2, [128, D], F32)
        x2_r = Ring(sy, ph, "x2", 2, [128, D], F32)
        sqj_r = Ring(sy, ph, "sqj", 1, [128, D], BF16)
        small = Ring(sy, ph, "sma", 8, [128, 4], F32)
        gx_r = Ring(sy, ph, "gxl", 2, [128, D], F32)
        pS = [cx.psum[0], cx.psum[1], cx.psum[2]]
        pO = [cx.psum[3], cx.psum[4]]
        pDs = [cx.psum[5], cx.psum[6]]
        pX = [cx.psum[7]]
        LOOK = 2
        KPOOL = NT
        cnt = {"s": 0, "o": 0, "x": 0, "h": 0, "d": 0}

        def nb(lst, key):
            r = lst[cnt[key] % len(lst)]
            cnt[key] += 1
            return r

        pend = []
        bg = []

        def bg_step(n=1):
            for _ in range(n):
                if not bg:
                    return
                try:
                    next(bg[0])
                except StopIteration:
                    bg.pop(0)

        def out_tile(b, qg, i, ogT, t_ogT, gx, t_gx):
            tt = 2 + qg * 4 + i
            xt, t_x = xt_r.next()
            sy.dma("sp", xt[:], dram["x1"][b, tt * 128:(tt + 1) * 128, :], reads=[cx.tdram["x1"]], writes=[t_x])
            x2, t_x2 = x2_r.next()
            for half in range(2):
                pt, tp = nb(pX, "x")
                hs = slice(half * 512, (half + 1) * 512)
                for k in range(8):
                    sy.op("pe", lambda e: e.matmul(pt[:], ogT[:, k, i * 128:(i + 1) * 128], ow[:, k, hs], start=(k == 0), stop=(k == 7)),
                          reads=[t_ogT, t_ow], writes=[tp])
                    if k % 2 == 1:
                        yield
                sy.op("dve", lambda e: e.tensor_tensor(out=x2[:, hs], in0=pt[:], in1=gx[:, hs], op=ALU.mult), reads=[tp, t_gx], writes=[t_x2])
                yield
            sy.op("pool", lambda e: e.tensor_tensor(out=x2[:], in0=x2[:], in1=xt[:], op=ALU.add), reads=[t_x2, t_x], writes=[t_x2])
            yield
            ss, t_ss = small.next()
            sqj, t_sqj = sqj_r.next()
            for _ in range(6):
                yield
            sy.op("act", lambda e: e.activation(out=sqj[:], in_=x2[:], func=AF.Square, accum_out=ss[:, 0:1]), reads=[t_x2], writes=[t_sqj, t_ss])
            yield
            sy.op("act", lambda e: e.activation(out=ss[:, 1:2], in_=ss[:, 0:1], func=AF.Ln, scale=1.0 / D, bias=cx.eps_c[:, 0:1]),
                  reads=[t_ss, cx.t_eps], writes=[t_ss])
            sy.op("act", lambda e: e.activation(out=ss[:, 2:3], in_=ss[:, 1:2], func=AF.Exp, scale=-0.5), reads=[t_ss], writes=[t_ss])
            yield
            sy.op("dve", lambda e: e.scalar_tensor_tensor(out=x2[:], in0=x2[:], scalar=ss[:, 2:3], in1=fng[:], op0=ALU.mult, op1=ALU.mult),
                  reads=[t_x2, t_ss, t_c], writes=[t_x2])
            yield
            sy.dma("pool", cx.out[b, (tt - 2) * 128:(tt - 1) * 128, :], x2[:], reads=[t_x2], writes=[cx.tout])
            yield

        def out_group(b, qg, ogT, t_ogT, gx, t_gx):
            for i0 in (0, 1, 2, 3):
                gens = [out_tile(b, qg, i, ogT, t_ogT, gx, t_gx) for i in (i0,)]
                while gens:
                    for g in list(gens):
                        try:
                            next(g)
                        except StopIteration:
                            gens.remove(g)
                        yield

        pre_b = {}
        pre_q = {}

        def load_batch(b):
            gx, t_gx = gx_r.next()
            gate_rows(cx, 1, b, gx, t_gx)
            kT, t_kT = kT_r.next()
            v, t_v = v_r.next()
            sy.dma("sp", kT[:], dram["akT"][b].rearrange("g d t -> d g t"), reads=[cx.tdram["akT"]], writes=[t_kT])
            sy.dma("sp", v[:], dram["av"][b].rearrange("(t p) c -> p t c", p=128), reads=[cx.tdram["av"]], writes=[t_v])
            pre_b[b] = (gx, t_gx, kT, t_kT, v, t_v)

        def load_qg(b, qg):
            qT, t_qT = qT_r.next()
            sg, t_sg = sg_r.next()
            sy.dma("sp", qT[:], dram["aqT"][b].rearrange("h d t -> d h t")[:, :, qg * 512:(qg + 1) * 512], reads=[cx.tdram["aqT"]], writes=[t_qT])
            sy.dma("sp", sg[:], dram["agT"][b].rearrange("(f p) t -> p f t", p=128)[:, :, qg * 512:(qg + 1) * 512], reads=[cx.tdram["agT"]], writes=[t_sg])
            pre_q[(b, qg)] = (qT, t_qT, sg, t_sg)

        for b in range(NB):
            if b not in pre_b:
                load_batch(b)
            gx, t_gx, kT, t_kT, v, t_v = pre_b.pop(b)
            for qg in range(4):
                if (b, qg) not in pre_q:
                    load_qg(b, qg)
                qT, t_qT, sg, t_sg = pre_q.pop((b, qg))
                while len(bg) > 1:
                    bg_step()
                ogT, t_ogT = ogT_r.next()
                steps = [(h, kt) for h in range(8) for kt in range(NT)]
                hbuf = {}

                def emit_S(s):
                    h, kt = steps[s]
                    if kt == 0:
                        hbuf[h] = cnt["h"] % 3
                        cnt["h"] += 1
                    hb = hbuf[h]
                    ps_, tps = nb(pS, "s")
                    sy.op("pe", lambda e: e.matmul(ps_[:], kT[:, h // 4, kt * 128:(kt + 1) * 128], qT[:, h, :], start=True, stop=True),
                          reads=[t_kT, t_qT], writes=[tps])
                    sy.op("act", lambda e: e.activation(out=PTh[hb][:, kt, :], in_=ps_[:], func=AF.Exp, scale=ks), reads=[tps], writes=[t_PTh[hb][kt]])

                head = {}
                hst = {}
                hpab = {}
                DCH = ()
                KD = 0

                def stage_b(hh, po, tpo, pd, tpd, ogT=ogT, t_ogT=t_ogT, sg=sg, t_sg=t_sg):
                    rden, t_rden = rden_r.next()
                    sy.op("dve", lambda e: e.reciprocal(out=rden[:], in_=pd[:]), reads=[tpd], writes=[t_rden])
                    tmo, t_tmo = tmo_r.next()
                    sy.op("dve", lambda e: e.tensor_tensor(out=tmo[:], in0=po[:], in1=rden[:], op=ALU.mult), reads=[tpo, t_rden], writes=[t_tmo])
                    sy.op("pool", lambda e: e.tensor_tensor(out=ogT[:, hh, :], in0=tmo[:], in1=sg[:, hh, :], op=ALU.mult), reads=[t_tmo, t_sg], writes=[t_ogT])

                for s0 in range(LOOK):
                    emit_S(s0)
                for s, (h, kt) in enumerate(steps):
                    if s + LOOK < len(steps):
                        emit_S(s + LOOK)
                    if kt == 0:
                        head[h] = nb(pO, "o")
                        hst[h] = nb(pDs, "d")
                    if kt == 4 and pend:
                        fn, after = pend.pop(0)
                        fn()
                        if after:
                            after()
                    po, tpo = head[h]
                    pd, tpd = hst[h]
                    hb = hbuf[h]
                    g = h // 4
                    sy.op("pe", lambda e: e.matmul(po[:], v[:, kt, g * 128:(g + 1) * 128], PTh[hb][:, kt, :], start=(kt == 0), stop=(kt == NT - 1)),
                          reads=[t_v, t_PTh[hb][kt]], writes=[tpo])
                    for (k0, k1) in DCH:
                        if kt == k1 - 1:
                            pab, t_pab = pab_r[0].next()
                            with nc.allow_low_precision(reason="bf16 matmul operand; reduction itself runs in fp32"):
                                sy.op("dve", lambda e: e.tensor_reduce(out=pab[:], in_=PTh[hb][:, k0:k1, :].rearrange("p k q -> p q k"), axis=AX.X, op=ALU.add),
                                      reads=t_PTh[hb][k0:k1], writes=[t_pab])
                            hpab.setdefault(h, []).append((pab, t_pab))
                    if kt >= KD:
                        sy.op("pe", lambda e: e.matmul(pd[:], ones[:], PTh[hb][:, kt, :], start=(kt == KD), stop=(kt == NT - 1)),
                              reads=[t_PTh[hb][kt], t_c], writes=[tpd])
                    if kt == NT - 2:
                        for (pab, t_pab) in hpab.pop(h, []):
                            sy.op("pe", lambda e: e.matmul(pd[:], ones[:], pab[:], start=False, stop=False), reads=[t_pab, t_c], writes=[tpd])
                    if kt == NT - 1:
                        after = None
                        if h == 7:
                            after = (lambda b=b, qg=qg, ogT=ogT, t_ogT=t_ogT, gx=gx, t_gx=t_gx: bg.append(out_group(b, qg, ogT, t_ogT, gx, t_gx)))
                        pend.append([(lambda hh=h, po=po, tpo=tpo, pd=pd, tpd=tpd, fb=stage_b: fb(hh, po, tpo, pd, tpd)), after])
                    if s == 40:
                        if qg < 3:
                            load_qg(b, qg + 1)
                        elif b + 1 < NB:
                            load_batch(b + 1)
                            load_qg(b + 1, 0)
                    bg_step()
        while pend:
            fn, after = pend.pop(0)
            fn()
            if after:
                after()
        while bg:
            bg_step()


def make_in_maps(inp, NB, ncores):
    hc = host_consts(NB)
    f = lambda a: np.ascontiguousarray(np.asarray(a, dtype=np.float32))
    shared = {
        "norm_g": f(inp["norm_g"]), "final_norm_g": f(inp["final_norm_g"]), "ada_w": f(inp["ada_w"]), "ada_b": f(inp["ada_b"]),
        "er_in_w": f(inp["er_in_w"][0]), "er_out_w": f(inp["er_out_w"][0]),
        "convw": f(np.asarray(inp["hy_conv_w"][0]).reshape(3, 12, 128).transpose(2, 1, 0)),
        "convb": f(np.asarray(inp["hy_conv_b"][0]).reshape(12, 128).T),
        "hy_f_w1": f(inp["hy_f_w1"][0]), "hy_f_b1": f(np.asarray(inp["hy_f_b1"][0]).reshape(64, 1)),
        "hy_f_freq": f(np.asarray(inp["hy_f_freq"][0]).reshape(64, 1)), "hy_f_w2": f(inp["hy_f_w2"][0]),
        "hy_f_b2": f(np.asarray(inp["hy_f_b2"][0]).reshape(64, 1)), "hy_f_w3": f(inp["hy_f_w3"][0]),
        "hy_bias": f(np.asarray(inp["hy_bias"][0]).reshape(4, 128).T),
        "retdec": f(np.concatenate([np.asarray(inp["ret_decay_f"][0]), np.asarray(inp["ret_decay_b"][0])])),
        "ret_norm_g": f(inp["ret_norm_g"][0]),
        "at_in_w": f(inp["at_in_w"][0]), "at_out_w": f(inp["at_out_w"][0]),
        "at_q_norm": f(inp["at_q_norm"][0]), "at_k_norm": f(inp["at_k_norm"][0]),
    }
    for k, v in hc.items():
        shared["k_" + k] = v
    x = np.asarray(inp["x"], dtype=np.float32)
    ctx = np.asarray(inp["ctx"], dtype=np.float32)
    c = np.asarray(inp["c"], dtype=np.float32)
    cc = np.asarray(inp["c_ctx"], dtype=np.float32)
    maps = []
    for i in range(ncores):
        sl = slice(i * NB, (i + 1) * NB)
        cfull = np.concatenate([c[sl], cc[None, :]], axis=0)
        cT = np.ascontiguousarray(cfull.reshape(NB + 1, 8, 128).transpose(2, 1, 0))
        m = dict(shared)
        m["x"] = np.ascontiguousarray(x[sl])
        m["ctx"] = np.ascontiguousarray(ctx[sl])
        m["cT"] = cT
        maps.append(m)
    return maps


_NC_CACHE = {}


def kernel(**inputs):
    NB = inputs["x"].shape[0] // NCORES
    if NB not in _NC_CACHE:
        _NC_CACHE[NB] = build(NB)
    nc = _NC_CACHE[NB]
    maps = make_in_maps(inputs, NB, NCORES)
    res = run_bass_kernel_spmd(nc, maps, core_ids=list(range(NCORES)))
    return np.concatenate([r["out"] for r in res.results], axis=0).astype(np.float32)
```

```python
import contextlib
import math
import numpy as np
import ml_dtypes
import concourse.bass as bass
import concourse.mybir as mybir
from concourse.bass_utils import run_bass_kernel_spmd

F32 = mybir.dt.float32
BF16 = mybir.dt.bfloat16
AF = mybir.ActivationFunctionType
ALU = mybir.AluOpType
AX = mybir.AxisListType

NCORES = 8
D = 1024
L = 2048
LC = 256
TOK = L + LC
NT = TOK // 128
EPS = 1e-6
NFFT = 2 * L
NFFTC = 2 * LC
PI = math.pi


class T:
    __slots__ = ("name", "w", "r")

    def __init__(self, name="t"):
        self.name = name
        self.w = None
        self.r = {}


class Sy:
    NS = 8

    def __init__(self, nc, stack):
        self.nc = nc
        self.stack = stack
        self.E = {"pe": nc.tensor, "dve": nc.vector, "act": nc.scalar, "pool": nc.gpsimd, "sp": nc.sync}
        self.sems = {}
        self.cnt = {}
        self.seen = {e: {} for e in self.E}
        for e in ("pe", "dve", "act", "pool"):
            self.sems[("c", e)] = stack.enter_context(nc.semaphore("c_" + e))
            self.cnt[e] = 0
        self.dn = {}
        for q in ("sp", "act", "pool"):
            self.dn[q] = 0
            for i in range(self.NS):
                self.sems[("d", q, i)] = stack.enter_context(nc.semaphore(f"d_{q}_{i}"))

    _uid = 0

    def sb(self, name, shape, dt, stack=None):
        Sy._uid += 1
        return (stack or self.stack).enter_context(self.nc.sbuf_tensor(f"s{Sy._uid}_{name}", list(shape), dt))

    def ps(self, name, shape, dt, stack=None):
        Sy._uid += 1
        return (stack or self.stack).enter_context(self.nc.psum_tensor(f"p{Sy._uid}_{name}", list(shape), dt))

    def _wait(self, eng, deps):
        best = {}
        for k, v in deps:
            if best.get(k, 0) < v:
                best[k] = v
        sn = self.seen[eng]
        for k, v in best.items():
            if sn.get(k, 0) >= v:
                continue
            self.E[eng].wait_ge(self.sems[k], v)
            sn[k] = v

    def _deps(self, eng_key, reads, writes, is_dma):
        deps = []
        for t in reads:
            if t.w is not None:
                if t.w[0] == eng_key and eng_key == ("c", "pe"):
                    continue
                deps.append(t.w)
        pe = eng_key == ("c", "pe")
        for t in writes:
            if t.w is not None and not (pe and t.w[0] == eng_key):
                deps.append(t.w)
            for k, v in t.r.items():
                if not (pe and k == eng_key):
                    deps.append((k, v))
        return deps

    def _reg(self, me, reads, writes):
        k, v = me
        for t in reads:
            t.r[k] = v
        for t in writes:
            t.w = me
            t.r = {}

    def op(self, eng, fn, reads=(), writes=()):
        key = ("c", eng)
        self._wait(eng, self._deps(key, reads, writes, False))
        ins = fn(self.E[eng])
        self.cnt[eng] += 1
        ins.then_inc(self.sems[key], 1)
        self._reg((key, self.cnt[eng]), reads, writes)
        return ins

    def dma(self, q, out, in_, reads=(), writes=(), **kw):
        n = self.dn[q]
        i = n % self.NS
        tgt = 16 * (n // self.NS + 1)
        key = ("d", q, i)
        deps = self._deps(key, reads, writes, True)
        if tgt > 16:
            deps.append((key, tgt - 16))
        self._wait(q, deps)
        ins = self.E[q].dma_start(out=out, in_=in_, **kw)
        ins.then_inc(self.sems[key], 16)
        self.dn[q] = n + 1
        self._reg((key, tgt), reads, writes)
        return ins

    def barrier(self):
        allk = []
        for e in ("pe", "dve", "act", "pool"):
            if self.cnt[e] > 0:
                allk.append((("c", e), self.cnt[e]))
        for q in ("sp", "act", "pool"):
            n = self.dn[q]
            for i in range(self.NS):
                c = (n - i + self.NS - 1) // self.NS if n > i else 0
                if c > 0:
                    allk.append((("d", q, i), 16 * c))
        for e in self.E:
            self._wait(e, [kv for kv in allk if kv[0] != ("c", e)])

    def finish(self, outs):
        self._wait("sp", [t.w for t in outs if t.w is not None])


class Ring:
    def __init__(self, sy, stack, name, n, shape, dt, psum=False):
        self.items = []
        for i in range(n):
            t = (sy.ps if psum else sy.sb)(f"{name}{i}", shape, dt, stack)
            self.items.append((t, T(f"{name}{i}")))
        self.i = 0

    def next(self):
        it = self.items[self.i % len(self.items)]
        self.i += 1
        return it


_CONST = {}


def _bf(a):
    return np.ascontiguousarray(a).astype(ml_dtypes.bfloat16)


def host_consts(NB):
    if NB in _CONST:
        return _CONST[NB]
    c = {}
    c["ident_bf"] = _bf(np.eye(128))
    c["ident_f"] = np.eye(128, dtype=np.float32)
    sel = np.zeros((NB + 1, NB + 1, 128), np.float32)
    for b in range(NB + 1):
        sel[b, b, :] = 1.0
    c["sel"] = sel.reshape(NB + 1, (NB + 1) * 128)
    def dft(Ls, N):
        t = np.arange(Ls, dtype=np.float64)[:, None]
        k = np.arange(N // 2, dtype=np.float64)[None, :]
        th = 2.0 * np.pi * (k + 0.5) * t / N
        return np.cos(th), -np.sin(th)
    C, S = dft(L, NFFT)
    fw = np.zeros((8, 128, 16, 512), np.float32)
    for g in range(8):
        blk = np.concatenate([C[:, 256 * g:256 * g + 256], S[:, 256 * g:256 * g + 256]], axis=1)
        fw[g] = blk.reshape(16, 128, 512).transpose(1, 0, 2)
    c["ffwd"] = _bf(fw)
    Fi = np.concatenate([C.T, S.T], axis=0) * (2.0 / NFFT)
    fi = np.zeros((4, 4, 128, 8, 512), np.float32)
    for ng in range(4):
        for fq in range(4):
            blk = Fi[fq * 1024:(fq + 1) * 1024, ng * 512:(ng + 1) * 512]
            fi[ng, fq] = blk.reshape(8, 128, 512).transpose(1, 0, 2)
    c["finv"] = _bf(fi)
    Cc, Sc = dft(LC, NFFTC)
    blk = np.concatenate([Cc, Sc], axis=1)
    c["ffwd_c"] = _bf(blk.reshape(2, 128, 512).transpose(1, 0, 2))
    Fic = np.concatenate([Cc.T, Sc.T], axis=0) * (2.0 / NFFTC)
    c["finv_c"] = _bf(Fic.reshape(4, 128, 256).transpose(1, 0, 2))
    def zfeat(Ls):
        n = np.arange(Ls, dtype=np.float32)
        t = n / np.float32(max(Ls - 1, 1))
        bands = np.linspace(1e-4, 15, 16, dtype=np.float32)
        ang = np.float32(2.0 * math.pi / Ls) * n[:, None] * bands[None, :]
        z = np.concatenate([t[:, None], np.cos(ang), -np.sin(ang)], axis=-1).astype(np.float32)
        maxd = math.log(1e-2) / 0.3
        mind = math.log(1e-2) / 1.5
        deltas = np.abs(np.linspace(mind, maxd, 512, dtype=np.float32))
        win = np.exp(-t[:, None] * deltas[None, :]).astype(np.float32)
        return z, win
    z, win = zfeat(L)
    c["zT"] = np.ascontiguousarray(z.T)
    c["win"] = np.ascontiguousarray(win.reshape(16, 128, 512).transpose(1, 0, 2))
    zc, winc = zfeat(LC)
    c["zcT"] = np.ascontiguousarray(zc.T)
    c["winc"] = np.ascontiguousarray(winc.reshape(2, 128, 512).transpose(1, 0, 2))
    rows = L // 64
    r = np.repeat(np.arange(rows, dtype=np.float32), 64)
    col = np.tile(np.arange(64, dtype=np.float32), rows)
    inv = (10000.0 ** (-np.arange(0, 64, 2, dtype=np.float32) / 64)).astype(np.float32)
    ang = np.concatenate([r[:, None] * inv, col[:, None] * inv], axis=-1)
    c["rcos"] = np.ascontiguousarray(np.cos(ang).astype(np.float32).reshape(16, 128, 64).transpose(1, 0, 2))
    c["rsin"] = np.ascontiguousarray(np.sin(ang).astype(np.float32).reshape(16, 128, 64).transpose(1, 0, 2))
    j = np.arange(128, dtype=np.float32)[:, None]
    i = np.arange(128, dtype=np.float32)[None, :]
    d = i - j
    rt = np.stack([np.maximum(d, 0), np.maximum(-d, 0), (d >= 0).astype(np.float32), (d <= 0).astype(np.float32),
                   np.broadcast_to(i + 1, (128, 128)), np.broadcast_to(128 - i, (128, 128))], axis=1)
    c["rtab"] = np.ascontiguousarray(rt.astype(np.float32))
    c["rcol"] = np.ascontiguousarray(np.stack([127 - j[:, 0], j[:, 0]], axis=1).astype(np.float32))
    _CONST[NB] = c
    return c


class Cx:
    pass


def build(NB, phases=("ada", "p1", "p2", "p3", "p4", "l1"), debug=()):
    nc = bass.Bass("TRN2", target_bir_lowering=False)
    hc = host_consts(NB)
    cx = Cx()
    cx.nc = nc
    cx.NB = NB
    NB1 = NB + 1
    dram = {}

    def din(name, shape, dt=F32):
        dram[name] = nc.dram_tensor(name, list(shape), dt, kind="ExternalInput").ap()
        return dram[name]

    def dscr(name, shape, dt):
        kind = "ExternalOutput" if name in debug else "Internal"
        dram[name] = nc.dram_tensor(name, list(shape), dt, kind=kind).ap()
        return dram[name]

    din("x", [NB, L, D]); din("ctx", [NB, LC, D]); din("cT", [128, 8, NB1])
    din("norm_g", [2, D]); din("final_norm_g", [D]); din("ada_w", [2, D, 3 * D]); din("ada_b", [2, 3 * D])
    din("er_in_w", [D, 4096]); din("er_out_w", [D, D])
    din("convw", [128, 12, 3]); din("convb", [128, 12])
    din("hy_f_w1", [33, 64]); din("hy_f_b1", [64, 1]); din("hy_f_freq", [64, 1]); din("hy_f_w2", [64, 64])
    din("hy_f_b2", [64, 1]); din("hy_f_w3", [64, 1024]); din("hy_bias", [128, 4])
    din("retdec", [8]); din("ret_norm_g", [512])
    din("at_in_w", [D, 2560]); din("at_out_w", [D, D]); din("at_q_norm", [128]); din("at_k_norm", [128])
    for k, v in hc.items():
        din("k_" + k, v.shape, BF16 if v.dtype == ml_dtypes.bfloat16 else F32)
    out = nc.dram_tensor("out", [NB, L, D], F32, kind="ExternalOutput").ap()
    dscr("u_tm", [NB, TOK, 512], BF16); dscr("u_fm", [NB, 512, TOK], BF16); dscr("w_fm", [NB, 512, TOK], BF16)
    dscr("qT", [NB, 512, TOK], BF16); dscr("kT", [NB, 512, TOK], BF16)
    dscr("k_tm", [NB, TOK, 512], BF16); dscr("v_tm", [NB, TOK, 512], BF16); dscr("g_tm", [NB, TOK, 512], BF16)
    dscr("yT", [NB, D, TOK], BF16)
    dscr("aqT", [NB, 8, 128, L], BF16); dscr("akT", [NB, 2, 128, TOK], BF16)
    dscr("av", [NB, TOK, 256], BF16); dscr("agT", [NB, D, L], BF16)
    dscr("x1", [NB, TOK, D], F32)
    dscr("dbg_mod", [2, NB1, 3 * D], F32)
    dscr("modrows", [2, NB1, 3 * D], F32)
    dscr("dbg_h", [L, 1024], F32)
    cx.dram = dram
    cx.out = out
    cx.tdram = {k: T("dram_" + k) for k in dram}
    cx.tout = T("out")

    with contextlib.ExitStack() as st:
        sy = Sy(nc, st)
        cx.sy = sy
        cx.ident_bf = sy.sb("ident_bf", [128, 128], BF16)
        cx.ident_f = sy.sb("ident_f", [128, 128], F32)
        cx.modFM = [sy.sb(f"modFM{l}", [128, 24, NB1], F32) for l in range(2)]
        cx.t_const = T("const")
        cx.t_mod = [T("mod0"), T("mod1")]
        sy.dma("sp", cx.ident_bf[:], dram["k_ident_bf"], writes=[cx.t_const])
        sy.dma("sp", cx.ident_f[:], dram["k_ident_f"], writes=[cx.t_const])
        cx.psum = []
        for i in range(8):
            cx.psum.append((sy.ps(f"pb{i}", [128, 512], F32), T(f"pb{i}")))

        full = all(p in phases for p in ("ada", "p1", "p2", "p3", "p4", "l1"))
        cx.pre_inw = None
        cx.pre_ainw = None
        if full:
            phase_ada(cx)
            sy.barrier()
            phase1(cx)
            sy.barrier()
            phase2(cx)
            sy.barrier()
            phase3(cx)
            sy.barrier()
            with contextlib.ExitStack() as w2:
                ainw = sy.sb("ainw_pre", [128, 8, 2560], BF16, w2)
                t_ainw = T()
                load_w_bf16(cx, ainw, dram["at_in_w"], 2560, t_ainw)
                cx.pre_ainw = (ainw, t_ainw)
                phase4(cx)
                sy.barrier()
                l1_prep(cx)
                sy.barrier()
            l1_attn(cx)
        else:
            if "ada" in phases:
                phase_ada(cx)
                sy.barrier()
            if "p1" in phases:
                phase1(cx)
                sy.barrier()
            if "p2" in phases:
                phase2(cx)
                sy.barrier()
            if "p3" in phases:
                phase3(cx)
                sy.barrier()
            if "p4" in phases:
                phase4(cx)
                sy.barrier()
            if "l1" in phases:
                layer1(cx)
            if "l1p" in phases:
                l1_prep(cx)
            if "l1a" in phases:
                l1_attn(cx)
        outs = [cx.tout] + [cx.tdram[k] for k in debug]
        sy.barrier()
        sy.finish(outs)
    return nc


def phase_ada(cx):
    sy, nc, NB, dram = cx.sy, cx.nc, cx.NB, cx.dram
    NB1 = NB + 1
    with contextlib.ExitStack() as ph:
        scT = sy.sb("scT", [128, 8, NB1], F32, ph)
        t_sc = T()
        sy.dma("sp", scT[:], dram["cT"], writes=[t_sc])
        sy.op("act", lambda e: e.activation(out=scT[:], in_=scT[:], func=AF.Silu), reads=[t_sc], writes=[t_sc])
        wsb = sy.sb("adaw", [128, 8, 3 * D], F32, ph)
        bb = sy.sb("adab", [NB1, 3 * D], F32, ph)
        ngb = sy.sb("ngb", [NB1, D], F32, ph)
        mTM = [sy.sb(f"modTM{l}", [NB1, 3 * D], F32, ph) for l in range(2)]
        for l in range(2):
            t_w, t_b = T(), T()
            for k in range(8):
                sy.dma("sp", wsb[:, k, :], dram["ada_w"][l, k * 128:(k + 1) * 128, :], writes=[t_w])
            sy.dma("sp", bb[:], dram["ada_b"][l].partition_broadcast(NB1), writes=[t_b])
            sy.dma("sp", ngb[:], dram["norm_g"][l].partition_broadcast(NB1), writes=[t_b])
            m = mTM[l]
            tm = cx.t_mod[l]
            for n in range(6):
                pt, tp = cx.psum[n % 2]
                for k in range(8):
                    sy.op("pe", lambda e, k=k: e.matmul(pt[0:NB1, :], scT[:, k, :], wsb[:, k, n * 512:(n + 1) * 512],
                                                        start=(k == 0), stop=(k == 7)), reads=[t_sc, t_w], writes=[tp])
                sy.op("dve", lambda e: e.tensor_tensor(out=m[:, n * 512:(n + 1) * 512], in0=pt[0:NB1, :],
                                                       in1=bb[:, n * 512:(n + 1) * 512], op=ALU.add),
                      reads=[tp, t_b], writes=[tm])
            sy.op("dve", lambda e: e.scalar_tensor_tensor(out=m[:, D:2 * D], in0=m[:, D:2 * D], scalar=1.0, in1=ngb[:],
                                                          op0=ALU.add, op1=ALU.mult), reads=[tm, t_b], writes=[tm])
            for j in range(24):
                pt, tp = cx.psum[2 + j % 2]
                sy.op("pe", lambda e: e.transpose(out=pt[:, 0:NB1], in_=m[:, j * 128:(j + 1) * 128],
                                                  identity=cx.ident_f[0:NB1, 0:NB1]), reads=[tm, cx.t_const], writes=[tp])
                sy.op("dve", lambda e: e.tensor_copy(out=cx.modFM[l][:, j, :], in_=pt[:, 0:NB1]), reads=[tp], writes=[tm])
            sy.dma("sp", dram["modrows"][l], m[:], reads=[tm], writes=[cx.tdram["modrows"]])


def load_w_bf16(cx, dst, src, ncols, t_w):
    for k in range(8):
        for c0 in range(0, ncols, 2048):
            c1 = min(ncols, c0 + 2048)
            cx.sy.dma("pool", dst[:, k, c0:c1], src[k * 128:(k + 1) * 128, c0:c1], writes=[t_w])


def interleave(gens):
    gens = list(gens)
    while gens:
        for g in list(gens):
            try:
                next(g)
            except StopIteration:
                gens.remove(g)


def norm_tile_g(cx, l, bsel, src_ap, xt_ring, scr, hxT, tok0, t_h, pbank, small, t_src=None, keep=None):
    sy = cx.sy
    xt, t_x = xt_ring.next()
    if keep is not None:
        keep.append((xt, t_x))
    sy.dma("sp", xt[:], src_ap, reads=[t_src] if t_src else [], writes=[t_x])
    ss, t_ss = small.next()
    sq, t_sq = scr.next()
    sy.op("act", lambda e: e.activation(out=sq[:], in_=xt[:], func=AF.Square, accum_out=ss[:, 0:1]),
          reads=[t_x], writes=[t_sq, t_ss])
    yield
    sy.op("act", lambda e: e.activation(out=ss[:, 1:2], in_=ss[:, 0:1], func=AF.Sqrt, scale=1.0 / D, bias=cx.eps_c[:, 0:1]),
          reads=[t_ss, cx.t_eps], writes=[t_ss])
    yield
    sy.op("dve", lambda e: e.reciprocal(out=ss[:, 2:3], in_=ss[:, 1:2]), reads=[t_ss], writes=[t_ss])
    yield
    sy.op("dve", lambda e: e.tensor_scalar(out=sq[:], in0=xt[:], scalar1=ss[:, 2:3], scalar2=None, op0=ALU.mult),
          reads=[t_x, t_ss], writes=[t_sq])
    yield
    pt, tp = pbank
    ptb = pt[:].bitcast(BF16)
    for k in range(8):
        sy.op("pe", lambda e: e.transpose(out=ptb[:, k * 128:(k + 1) * 128], in_=sq[:, k * 128:(k + 1) * 128],
                                          identity=cx.ident_bf[:]), reads=[t_sq, cx.t_const], writes=[tp])
    mf = cx.modFM[l]
    for k in range(8):
        if k % 2 == 0:
            sy.op("act", lambda e: e.activation(out=hxT[:, k, tok0:tok0 + 128], in_=ptb[:, k * 128:(k + 1) * 128],
                                                func=AF.Identity, scale=mf[:, 8 + k, bsel:bsel + 1], bias=mf[:, k, bsel:bsel + 1]),
                  reads=[tp, cx.t_mod[l]], writes=[t_h])
        else:
            sy.op("dve", lambda e: e.tensor_scalar(out=hxT[:, k, tok0:tok0 + 128], in0=ptb[:, k * 128:(k + 1) * 128],
                                                   scalar1=mf[:, 8 + k, bsel:bsel + 1], scalar2=mf[:, k, bsel:bsel + 1],
                                                   op0=ALU.mult, op1=ALU.add), reads=[tp, cx.t_mod[l]], writes=[t_h])
    yield


def norm_tile(cx, l, bsel, src_ap, xt_ring, scr, hxT, tok0, t_h, pbank, small, t_src=None):
    for _ in norm_tile_g(cx, l, bsel, src_ap, xt_ring, scr, hxT, tok0, t_h, pbank, small, t_src):
        pass


def mk_eps(cx, ph):
    cx.eps_c = cx.sy.sb("eps_c", [128, 1], F32, ph)
    cx.t_eps = T()
    cx.sy.op("dve", lambda e: e.memset(cx.eps_c[:], EPS), writes=[cx.t_eps])


def tok_src(cx, b, tt, l):
    d = cx.dram
    if l == 0:
        if tt < 2:
            return d["ctx"][b, tt * 128:(tt + 1) * 128, :], cx.NB, None
        return d["x"][b, (tt - 2) * 128:(tt - 1) * 128, :], b, None
    return d["x1"][b, tt * 128:(tt + 1) * 128, :], (cx.NB if tt < 2 else b), cx.tdram["x1"]


def phase1(cx):
    sy, nc, NB, dram = cx.sy, cx.nc, cx.NB, cx.dram
    with contextlib.ExitStack() as ph:
        mk_eps(cx, ph)
        if getattr(cx, "pre_inw", None) is not None:
            inw, t_inw = cx.pre_inw
        else:
            inw = sy.sb("inw", [128, 8, 4096], BF16, ph)
            t_inw = T()
            load_w_bf16(cx, inw, dram["er_in_w"], 4096, t_inw)
        convw = sy.sb("convw", [128, 12, 3], F32, ph)
        convb = sy.sb("convb", [128, 12], F32, ph)
        t_cw = T()
        sy.dma("sp", convw[:], dram["convw"], writes=[t_cw])
        sy.dma("sp", convb[:], dram["convb"], writes=[t_cw])
        hxT = sy.sb("hxT", [128, 8, TOK], BF16, ph)
        t_h = [T() for _ in range(NT)]
        xt_ring = Ring(sy, ph, "xt", 4, [128, D], F32)
        scr = Ring(sy, ph, "scr", 4, [128, D], BF16)
        small = Ring(sy, ph, "sm", 8, [128, 4], F32)
        rowp_ring = Ring(sy, ph, "rowp", 2, [128, 2307], F32)
        for rp, t_rp in rowp_ring.items:
            for pos in (0, 257, 2306):
                sy.op("dve", lambda e: e.memset(rp[:, pos:pos + 1], 0.0), writes=[t_rp])
        acc_ring = Ring(sy, ph, "acc", 2, [128, 2305], F32)
        sg = sy.sb("sg", [128, 2305], F32, ph)
        t_sg = T()
        sy.op("dve", lambda e: e.memset(sg[:, 256:257], 0.0), writes=[t_sg])
        urow_ring = Ring(sy, ph, "urow", 1, [128, 2305], BF16)
        wrow_ring = Ring(sy, ph, "wrow", 1, [128, 2305], BF16)
        utm_ring = Ring(sy, ph, "utm", 2, [128, NT, 128], BF16)
        row_ring = Ring(sy, ph, "qkrow", 2, [128, TOK], BF16)
        st_ring = Ring(sy, ph, "stg", 4, [128, 512], BF16)
        pacc = [cx.psum[i] for i in range(7)]
        pi = [0]

        def next_bank():
            r = pacc[pi[0] % 7]
            pi[0] += 1
            return r

        tgs = [(0, 256, 1)] + [(256 + g * 512, 512, 258 + g * 512) for g in range(4)]
        for b in range(NB):
            def ntile(tt):
                src, bsel, tsrc = tok_src(cx, b, tt, 0)
                yield from norm_tile_g(cx, 0, bsel, src, xt_ring, scr, hxT, tt * 128, t_h[tt], next_bank(), small, tsrc)
            for t0 in range(0, NT, 4):
                interleave([ntile(tt) for tt in range(t0, min(NT, t0 + 4))])
            deferred = []
            for ct in range(4):
                accs = {}
                for ft, role in ((4 + ct, "x1"), (8 + ct, "v"), (ct, "x0"), (12 + ct, "hg")):
                    if role != "hg":
                        rp, t_rp = rowp_ring.next()
                    for (tok0, n, rc0) in tgs:
                        pt, tp = next_bank()
                        hts = t_h[tok0 // 128:(tok0 + n) // 128]
                        for k in range(8):
                            sy.op("pe", lambda e: e.matmul(pt[:, 0:n], inw[:, k, ft * 128:(ft + 1) * 128], hxT[:, k, tok0:tok0 + n],
                                                           start=(k == 0), stop=(k == 7)), reads=[t_inw] + hts, writes=[tp])
                        if role == "hg":
                            sy.op("act", lambda e: e.activation(out=sg[:, rc0 - 1:rc0 - 1 + n], in_=pt[:, 0:n], func=AF.Silu),
                                  reads=[tp], writes=[t_sg])
                        else:
                            sy.op("act", lambda e: e.activation(out=rp[:, rc0:rc0 + n], in_=pt[:, 0:n], func=AF.Copy),
                                  reads=[tp], writes=[t_rp])
                    if role == "hg":
                        continue
                    ac, t_ac = acc_ring.next()
                    cw = ft
                    sy.op("dve", lambda e: e.tensor_scalar(out=ac[:], in0=rp[:, 1:2306], scalar1=convw[:, cw, 1:2], scalar2=convb[:, cw:cw + 1],
                                                           op0=ALU.mult, op1=ALU.add), reads=[t_rp, t_cw], writes=[t_ac])
                    sy.op("dve", lambda e: e.scalar_tensor_tensor(out=ac[:], in0=rp[:, 0:2305], scalar=convw[:, cw, 0:1], in1=ac[:],
                                                                  op0=ALU.mult, op1=ALU.add), reads=[t_rp, t_cw, t_ac], writes=[t_ac])
                    sy.op("dve", lambda e: e.scalar_tensor_tensor(out=ac[:], in0=rp[:, 2:2307], scalar=convw[:, cw, 2:3], in1=ac[:],
                                                                  op0=ALU.mult, op1=ALU.add), reads=[t_rp, t_cw, t_ac], writes=[t_ac])
                    accs[role] = (ac, t_ac)
                    if role == "v":
                        ur, t_ur = urow_ring.next()
                        a1, t1 = accs["x1"]
                        sy.op("pool", lambda e: e.tensor_tensor(out=ur[:], in0=a1[:], in1=ac[:], op=ALU.mult),
                              reads=[t1, t_ac], writes=[t_ur])
                        for (d0, n, s0) in ((0, 256, 0), (256, 2048, 257)):
                            sy.dma("pool", dram["u_fm"][b, ct * 128:(ct + 1) * 128, d0:d0 + n], ur[:, s0:s0 + n],
                                   reads=[t_ur], writes=[cx.tdram["u_fm"]])
                        def do_utm(ur=ur, t_ur=t_ur, ct=ct):
                            um, t_um = utm_ring.next()
                            pt, tp = cx.psum[7]
                            ptb = pt[:].bitcast(BF16)
                            for t0 in range(0, NT, 8):
                                nn = min(8, NT - t0)
                                for ti in range(nn):
                                    tt = t0 + ti
                                    c0 = tt * 128 if tt < 2 else 257 + (tt - 2) * 128
                                    sy.op("pe", lambda e: e.transpose(out=ptb[:, ti * 128:(ti + 1) * 128], in_=ur[:, c0:c0 + 128],
                                                                      identity=cx.ident_bf[:]), reads=[t_ur, cx.t_const], writes=[tp])
                                sy.op("act", lambda e: e.activation(out=um[:, t0:t0 + nn, :], in_=ptb[:, 0:nn * 128].rearrange("p (a c) -> p a c", c=128),
                                                                    func=AF.Copy), reads=[tp], writes=[t_um])
                            sy.dma("pool", dram["u_tm"][b].rearrange("(t p) c -> p t c", p=128)[:, :, ct * 128:(ct + 1) * 128], um[:],
                                   reads=[t_um], writes=[cx.tdram["u_tm"]])
                        deferred.append(do_utm)
                while deferred:
                    deferred.pop(0)()
                wr, t_wr = wrow_ring.next()
                a0, t0_ = accs["x0"]
                sy.op("pool", lambda e: e.tensor_tensor(out=wr[:], in0=a0[:], in1=sg[:], op=ALU.mult), reads=[t0_, t_sg], writes=[t_wr])
                for (d0, n, s0) in ((0, 256, 0), (256, 2048, 257)):
                    sy.dma("pool", dram["w_fm"][b, ct * 128:(ct + 1) * 128, d0:d0 + n], wr[:, s0:s0 + n],
                           reads=[t_wr], writes=[cx.tdram["w_fm"]])
            for ft in range(8):
                row, t_row = row_ring.next()
                for (tok0, n, _) in tgs:
                    pt, tp = next_bank()
                    hts = t_h[tok0 // 128:(tok0 + n) // 128]
                    for k in range(8):
                        sy.op("pe", lambda e: e.matmul(pt[:, 0:n], inw[:, k, 2048 + ft * 128:2048 + (ft + 1) * 128], hxT[:, k, tok0:tok0 + n],
                                                       start=(k == 0), stop=(k == 7)), reads=[t_inw] + hts, writes=[tp])
                    if (tok0 // 256) % 2 == 0:
                        sy.op("act", lambda e: e.activation(out=row[:, tok0:tok0 + n], in_=pt[:, 0:n], func=AF.Copy), reads=[tp], writes=[t_row])
                    else:
                        sy.op("dve", lambda e: e.tensor_copy(out=row[:, tok0:tok0 + n], in_=pt[:, 0:n]), reads=[tp], writes=[t_row])
                dst = dram["qT"] if ft < 4 else dram["kT"]
                tdst = cx.tdram["qT"] if ft < 4 else cx.tdram["kT"]
                h = ft % 4
                sy.dma("pool", dst[b, h * 128:(h + 1) * 128, :], row[:], reads=[t_row], writes=[tdst])
            for tt in range(NT):
                for g, nm in enumerate(("k_tm", "v_tm", "g_tm")):
                    pt, tp = next_bank()
                    c0 = 2560 + g * 512
                    for k in range(8):
                        sy.op("pe", lambda e: e.matmul(pt[:], hxT[:, k, tt * 128:(tt + 1) * 128], inw[:, k, c0:c0 + 512],
                                                       start=(k == 0), stop=(k == 7)), reads=[t_inw, t_h[tt]], writes=[tp])
                    sg_, t_st = st_ring.next()
                    if nm == "g_tm":
                        sy.op("act", lambda e: e.activation(out=sg_[:], in_=pt[:], func=AF.Silu), reads=[tp], writes=[t_st])
                    elif nm == "k_tm":
                        sy.op("dve", lambda e: e.tensor_copy(out=sg_[:], in_=pt[:]), reads=[tp], writes=[t_st])
                    else:
                        sy.op("act", lambda e: e.activation(out=sg_[:], in_=pt[:], func=AF.Copy), reads=[tp], writes=[t_st])
                    sy.dma("pool", dram[nm][b, tt * 128:(tt + 1) * 128, :], sg_[:], reads=[t_st], writes=[cx.tdram[nm]])


def phase2(cx):
    sy, nc, NB, dram = cx.sy, cx.nc, cx.NB, cx.dram
    NC_ = NT
    order_f = list(range(NC_))
    order_b = [1, 0] + list(range(NC_ - 1, 1, -1))
    with contextlib.ExitStack() as ph:
        mk_eps(cx, ph)
        rtab = sy.sb("rtab", [128, 6, 128], F32, ph)
        rcol = sy.sb("rcol", [128, 2], F32, ph)
        dec = sy.sb("dec", [128, 8], F32, ph)
        lg = sy.sb("lg", [128, 8], F32, ph)
        rng = sy.sb("rng", [128, 512], F32, ph)
        t_c = T()
        sy.dma("sp", rtab[:], dram["k_rtab"], writes=[t_c])
        sy.dma("sp", rcol[:], dram["k_rcol"], writes=[t_c])
        sy.dma("sp", dec[:], dram["retdec"].partition_broadcast(128), writes=[t_c])
        sy.dma("sp", rng[:], dram["ret_norm_g"].partition_broadcast(128), writes=[t_c])
        sy.op("act", lambda e: e.activation(out=lg[:], in_=dec[:], func=AF.Exp), reads=[t_c], writes=[t_c])
        sy.op("dve", lambda e: e.tensor_scalar(out=lg[:], in0=lg[:], scalar1=-1.0, scalar2=None, op0=ALU.mult), reads=[t_c], writes=[t_c])
        maskT = sy.sb("maskT", [128, 4, 128], F32, ph)
        qd = sy.sb("qd", [128, 8, 128], F32, ph)
        kd = sy.sb("kd", [128, 8], F32, ph)
        cd = sy.sb("cd", [128, 8], F32, ph)
        e1 = sy.sb("e1", [128, 128], F32, ph)
        e2 = sy.sb("e2", [128, 128], F32, ph)
        ks = 128.0 ** -0.5
        for h in range(4):
            sy.op("act", lambda e: e.activation(out=e1[:], in_=rtab[:, 0, :], func=AF.Exp, scale=lg[:, h:h + 1]), reads=[t_c], writes=[t_c])
            sy.op("act", lambda e: e.activation(out=e2[:], in_=rtab[:, 1, :], func=AF.Exp, scale=lg[:, 4 + h:5 + h]), reads=[t_c], writes=[t_c])
            sy.op("dve", lambda e: e.scalar_tensor_tensor(out=e1[:], in0=e1[:], scalar=ks, in1=rtab[:, 2, :], op0=ALU.mult, op1=ALU.mult), reads=[t_c], writes=[t_c])
            sy.op("dve", lambda e: e.scalar_tensor_tensor(out=e2[:], in0=e2[:], scalar=ks, in1=rtab[:, 3, :], op0=ALU.mult, op1=ALU.mult), reads=[t_c], writes=[t_c])
            sy.op("dve", lambda e: e.tensor_tensor(out=maskT[:, h, :], in0=e1[:], in1=e2[:], op=ALU.add), reads=[t_c], writes=[t_c])
            sy.op("act", lambda e: e.activation(out=qd[:, h, :], in_=rtab[:, 4, :], func=AF.Exp, scale=lg[:, h:h + 1]), reads=[t_c], writes=[t_c])
            sy.op("act", lambda e: e.activation(out=qd[:, 4 + h, :], in_=rtab[:, 5, :], func=AF.Exp, scale=lg[:, 4 + h:5 + h]), reads=[t_c], writes=[t_c])
            sy.op("act", lambda e: e.activation(out=kd[:, h:h + 1], in_=rcol[:, 0:1], func=AF.Exp, scale=lg[:, h:h + 1]), reads=[t_c], writes=[t_c])
            sy.op("act", lambda e: e.activation(out=kd[:, 4 + h:5 + h], in_=rcol[:, 1:2], func=AF.Exp, scale=lg[:, 4 + h:5 + h]), reads=[t_c], writes=[t_c])
        sy.op("act", lambda e: e.activation(out=cd[:], in_=lg[:], func=AF.Exp, scale=128.0), reads=[t_c], writes=[t_c])
        sy.op("dve", lambda e: e.tensor_scalar(out=kd[:], in0=kd[:], scalar1=ks, scalar2=None, op0=ALU.mult), reads=[t_c], writes=[t_c])

        qT_r = Ring(sy, ph, "qTh", 2, [128, TOK], BF16)
        kT_r = Ring(sy, ph, "kTh", 2, [128, TOK], BF16)
        k_r = Ring(sy, ph, "kh", 2, [128, NT, 128], BF16)
        v_r = Ring(sy, ph, "vh", 2, [128, NT, 128], BF16)
        g_r = Ring(sy, ph, "gh", 2, [128, NT, 128], BF16)
        qf_r = Ring(sy, ph, "qf", 2, [128, NT, 128], BF16)
        qb_r = Ring(sy, ph, "qb", 2, [128, NT, 128], BF16)
        kf_r = Ring(sy, ph, "kf", 2, [128, NT, 128], BF16)
        kb_r = Ring(sy, ph, "kb", 2, [128, NT, 128], BF16)
        gg_r = Ring(sy, ph, "gg", 2, [128, NT, 128], F32)
        S32_r = [Ring(sy, ph, f"S32_{i}", 2, [128, NT, 128], F32) for i in range(2)]
        Sbf_r = [Ring(sy, ph, f"Sbf_{i}", 2, [128, NT, 128], BF16) for i in range(2)]
        PT_r = Ring(sy, ph, "PT", 2, [128, 4, 128], BF16)
        oc_r = Ring(sy, ph, "oc", 2, [128, 4, 128], F32)
        sq_r = Ring(sy, ph, "sq", 2, [128, 4, 128], F32)
        y_r = Ring(sy, ph, "yy", 2, [128, 4, 128], BF16)
        st_r = Ring(sy, ph, "hst", 4, [128, 16], F32)
        yT_r = Ring(sy, ph, "yTh", 2, [128, TOK], BF16)
        pST = [cx.psum[0], cx.psum[1]]
        pO = [cx.psum[2], cx.psum[3]]
        pTr = [cx.psum[4], cx.psum[5]]
        pD = [cx.psum[6], cx.psum[7]]
        cnt = {"st": 0, "o": 0, "d": 0, "tr": 0}

        def nb(lst, key):
            r = lst[cnt[key] % len(lst)]
            cnt[key] += 1
            return r

        def head_front(b, h, C):
            qT, t_q = qT_r.next(); kT, t_k = kT_r.next(); kh, t_kh = k_r.next(); vh, t_vh = v_r.next(); gh, t_gh = g_r.next()
            hs = slice(h * 128, (h + 1) * 128)
            sy.dma("sp", qT[:], dram["qT"][b, hs, :], reads=[cx.tdram["qT"]], writes=[t_q])
            sy.dma("sp", kT[:], dram["kT"][b, hs, :], reads=[cx.tdram["kT"]], writes=[t_k])
            for (dst, t_d, nm) in ((kh, t_kh, "k_tm"), (vh, t_vh, "v_tm"), (gh, t_gh, "g_tm")):
                sy.dma("sp", dst[:], dram[nm][b].rearrange("(t p) c -> p t c", p=128)[:, :, hs], reads=[cx.tdram[nm]], writes=[t_d])
            qf, t_qf = qf_r.next(); qb, t_qb = qb_r.next(); kf, t_kf = kf_r.next(); kb, t_kb = kb_r.next(); gg, t_gg = gg_r.next()
            q3 = qT[:].rearrange("p (t c) -> p t c", c=128)
            sy.op("dve", lambda e: e.tensor_scalar(out=kf[:], in0=kh[:], scalar1=kd[:, h:h + 1], scalar2=None, op0=ALU.mult),
                  reads=[t_kh, t_c], writes=[t_kf])
            sy.op("dve", lambda e: e.tensor_scalar(out=kb[:], in0=kh[:], scalar1=kd[:, 4 + h:5 + h], scalar2=None, op0=ALU.mult),
                  reads=[t_kh, t_c], writes=[t_kb])
            yield
            sy.op("pool", lambda e: e.tensor_tensor(out=qf[:], in0=q3, in1=qd[:, h, :].unsqueeze(1).to_broadcast([128, NT, 128]), op=ALU.mult),
                  reads=[t_q, t_c], writes=[t_qf])
            yield
            sy.op("pool", lambda e: e.tensor_tensor(out=qb[:], in0=q3, in1=qd[:, 4 + h, :].unsqueeze(1).to_broadcast([128, NT, 128]), op=ALU.mult),
                  reads=[t_q, t_c], writes=[t_qb])
            yield
            sy.op("pool", lambda e: e.tensor_tensor(out=gg[:], in0=gh[:], in1=rng[:, hs].unsqueeze(1).to_broadcast([128, NT, 128]), op=ALU.mult),
                  reads=[t_gh, t_c], writes=[t_gg])
            yield
            S32 = [S32_r[0].next(), S32_r[1].next()]
            Sbf = [Sbf_r[0].next(), Sbf_r[1].next()]
            orders = (order_f, order_b)
            kxs = ((kf, t_kf), (kb, t_kb))
            for di in range(2):
                c0 = orders[di][0]
                sy.op("pool", lambda e: e.memset(S32[di][0][:, c0, :], 0.0), writes=[S32[di][1]])
            for idx in range(NC_ - 1):
                pt, tp = nb(pD, "d")
                for di in range(2):
                    c = orders[di][idx]
                    sy.op("pe", lambda e: e.matmul(pt[:, di * 128:(di + 1) * 128], kxs[di][0][:, c, :], vh[:, c, :], start=True, stop=True),
                          reads=[kxs[di][1], t_vh], writes=[tp])
                for di in range(2):
                    c = orders[di][idx]
                    cn = orders[di][idx + 1]
                    sy.op("dve", lambda e: e.scalar_tensor_tensor(out=S32[di][0][:, cn, :], in0=S32[di][0][:, c, :], scalar=cd[:, 4 * di + h:4 * di + h + 1],
                                                                  in1=pt[:, di * 128:(di + 1) * 128], op0=ALU.mult, op1=ALU.add),
                          reads=[tp, S32[di][1], t_c], writes=[S32[di][1]])
                yield
            for di in range(2):
                sy.op("act", lambda e: e.activation(out=Sbf[di][0][:], in_=S32[di][0][:], func=AF.Copy), reads=[S32[di][1]], writes=[Sbf[di][1]])
            C.update(b=b, h=h, qT=qT, t_q=t_q, kT=kT, t_k=t_k, vh=vh, t_vh=t_vh, qf=qf, t_qf=t_qf, qb=qb, t_qb=t_qb, gg=gg, t_gg=t_gg, Sbf=Sbf)
            yield

        def head_back(C):
            b, h = C["b"], C["h"]
            qT, t_q, kT, t_k, vh, t_vh = C["qT"], C["t_q"], C["kT"], C["t_k"], C["vh"], C["t_vh"]
            qf, t_qf, qb, t_qb, gg, t_gg, Sbf = C["qf"], C["t_qf"], C["qb"], C["t_qb"], C["gg"], C["t_gg"], C["Sbf"]
            yT, t_yT = yT_r.next()

            def group(g0):
                n = min(4, NC_ - g0)
                pst, tpst = nb(pST, "st")
                for ci in range(n):
                    c = g0 + ci
                    sy.op("pe", lambda e: e.matmul(pst[:, ci * 128:(ci + 1) * 128], kT[:, c * 128:(c + 1) * 128], qT[:, c * 128:(c + 1) * 128],
                                                   start=True, stop=True), reads=[t_k, t_q], writes=[tpst])
                yield
                PT, t_PT = PT_r.next()
                sy.op("dve", lambda e: e.tensor_tensor(out=PT[:, 0:n, :], in0=pst[:, 0:n * 128].rearrange("p (a c) -> p a c", c=128),
                                                       in1=maskT[:, h, :].unsqueeze(1).to_broadcast([128, n, 128]), op=ALU.mult),
                      reads=[tpst, t_c], writes=[t_PT])
                yield
                po, tpo = nb(pO, "o")
                for ci in range(n):
                    c = g0 + ci
                    osl = po[:, ci * 128:(ci + 1) * 128]
                    sy.op("pe", lambda e: e.matmul(osl, PT[:, ci, :], vh[:, c, :], start=True, stop=False), reads=[t_PT, t_vh], writes=[tpo])
                    sy.op("pe", lambda e: e.matmul(osl, qf[:, c, :], Sbf[0][0][:, c, :], start=False, stop=False), reads=[t_qf, Sbf[0][1]], writes=[tpo])
                    sy.op("pe", lambda e: e.matmul(osl, qb[:, c, :], Sbf[1][0][:, c, :], start=False, stop=True), reads=[t_qb, Sbf[1][1]], writes=[tpo])
                yield
                o3 = po[:, 0:n * 128].rearrange("p (a c) -> p a c", c=128)
                stt, t_st = st_r.next()
                oc, t_oc = oc_r.next(); sq, t_sq = sq_r.next(); yy, t_yy = y_r.next()
                sy.op("dve", lambda e: e.tensor_reduce(out=stt[:, 0:n], in_=o3, axis=AX.X, op=ALU.add), reads=[tpo], writes=[t_st])
                yield
                sy.op("dve", lambda e: e.tensor_scalar(out=stt[:, 4:4 + n], in0=stt[:, 0:n], scalar1=-1.0 / 128, scalar2=None, op0=ALU.mult),
                      reads=[t_st], writes=[t_st])
                yield
                sy.op("dve", lambda e: e.tensor_tensor(out=oc[:, 0:n, :], in0=o3, in1=stt[:, 4:4 + n].unsqueeze(2).to_broadcast([128, n, 128]), op=ALU.add),
                      reads=[tpo, t_st], writes=[t_oc])
                yield
                sy.op("pool", lambda e: e.tensor_tensor(out=sq[:, 0:n, :], in0=oc[:, 0:n, :], in1=oc[:, 0:n, :], op=ALU.mult), reads=[t_oc], writes=[t_sq])
                yield
                sy.op("dve", lambda e: e.tensor_reduce(out=stt[:, 8:8 + n], in_=sq[:, 0:n, :], axis=AX.X, op=ALU.add), reads=[t_sq], writes=[t_st])
                yield
                sy.op("act", lambda e: e.activation(out=stt[:, 8:8 + n], in_=stt[:, 8:8 + n], func=AF.Sqrt, scale=1.0 / 128, bias=cx.eps_c[:, 0:1]),
                      reads=[t_st, cx.t_eps], writes=[t_st])
                yield
                sy.op("dve", lambda e: e.reciprocal(out=stt[:, 12:12 + n], in_=stt[:, 8:8 + n]), reads=[t_st], writes=[t_st])
                yield
                sy.op("dve", lambda e: e.tensor_tensor(out=oc[:, 0:n, :], in0=oc[:, 0:n, :], in1=stt[:, 12:12 + n].unsqueeze(2).to_broadcast([128, n, 128]), op=ALU.mult),
                      reads=[t_oc, t_st], writes=[t_oc])
                yield
                sy.op("pool", lambda e: e.tensor_tensor(out=yy[:, 0:n, :], in0=oc[:, 0:n, :], in1=gg[:, g0:g0 + n, :], op=ALU.mult),
                      reads=[t_oc, t_gg], writes=[t_yy])
                yield
                ptr, tptr = nb(pTr, "tr")
                ptb = ptr[:].bitcast(BF16)
                for ci in range(n):
                    sy.op("pe", lambda e: e.transpose(out=ptb[:, ci * 128:(ci + 1) * 128], in_=yy[:, ci, :], identity=cx.ident_bf[:]),
                          reads=[t_yy, cx.t_const], writes=[tptr])
                yield
                sy.op("act", lambda e: e.activation(out=yT[:, g0 * 128:(g0 + n) * 128], in_=ptb[:, 0:n * 128], func=AF.Copy),
                      reads=[tptr], writes=[t_yT])
                yield

            for pair in ((0, 4), (8, 12), (16,)):
                gens = [group(g0) for g0 in pair]
                while gens:
                    for g in list(gens):
                        try:
                            next(g)
                        except StopIteration:
                            gens.remove(g)
                    yield
            sy.dma("pool", dram["yT"][b, 512 + h * 128:512 + (h + 1) * 128, :], yT[:], reads=[t_yT], writes=[cx.tdram["yT"]])
            yield

        hl = [(b, h) for b in range(NB) for h in range(4)]
        Cs = [dict() for _ in hl]
        interleave([head_front(hl[0][0], hl[0][1], Cs[0])])
        for i in range(len(hl)):
            gens = [head_back(Cs[i])]
            if i + 1 < len(hl):
                gens.append(head_front(hl[i + 1][0], hl[i + 1][1], Cs[i + 1]))
            interleave(gens)


def phase3(cx):
    sy, nc, NB, dram = cx.sy, cx.nc, cx.NB, cx.dram
    MAGIC = 12582912.0
    with contextlib.ExitStack() as ph:
        Kx = sy.sb("Kx", [128, 32, 512], BF16, ph)
        Kc = sy.sb("Kc", [128, 4, 512], BF16, ph)
        t_K = T()
        Fc = sy.sb("Fc", [128, 2, 512], BF16, ph)
        Fic = sy.sb("Fic", [128, 4, 256], BF16, ph)
        hbias = sy.sb("hbias", [128, 4], F32, ph)
        t_fc = T()
        sy.dma("sp", Fc[:], dram["k_ffwd_c"], writes=[t_fc])
        sy.dma("sp", Fic[:], dram["k_finv_c"], writes=[t_fc])
        sy.dma("sp", hbias[:], dram["hy_bias"], writes=[t_fc])
        F_r = Ring(sy, ph, "Fb", 2, [128, 16, 512], BF16)
        with contextlib.ExitStack() as fs:
            w1 = sy.sb("w1", [33, 64], F32, fs); w2 = sy.sb("w2", [64, 64], F32, fs); w3 = sy.sb("w3", [64, 1024], F32, fs)
            fb = sy.sb("fb", [64, 6], F32, fs)
            t_w = T()
            sy.dma("sp", w1[:], dram["hy_f_w1"], writes=[t_w]); sy.dma("sp", w2[:], dram["hy_f_w2"], writes=[t_w])
            sy.dma("sp", w3[:], dram["hy_f_w3"], writes=[t_w])
            sy.dma("sp", fb[:, 0:1], dram["hy_f_b1"], writes=[t_w]); sy.dma("sp", fb[:, 1:2], dram["hy_f_freq"], writes=[t_w])
            sy.dma("sp", fb[:, 2:3], dram["hy_f_b2"], writes=[t_w])
            sy.op("dve", lambda e: e.tensor_tensor(out=fb[:, 3:4], in0=fb[:, 0:1], in1=fb[:, 1:2], op=ALU.mult), reads=[t_w], writes=[t_w])
            sy.op("dve", lambda e: e.tensor_tensor(out=fb[:, 4:5], in0=fb[:, 2:3], in1=fb[:, 1:2], op=ALU.mult), reads=[t_w], writes=[t_w])
            zT = sy.sb("zT", [33, L], F32, fs)
            h1 = sy.sb("h1", [64, L], F32, fs)
            h2 = sy.sb("h2", [64, L], F32, fs)
            hfb = sy.sb("hfb", [128, 16, 1024], BF16, fs)
            tA = sy.sb("tA", [64, 512], F32, fs)
            tB = sy.sb("tB", [64, 512], F32, fs)
            win_r = Ring(sy, fs, "win", 2, [128, 512], F32)
            tmpA_r = Ring(sy, fs, "tmpA", 2, [128, 512], F32)
            for (Ls, zname, wname, Kdst, nct) in ((L, "k_zT", "k_win", Kx, 16), (LC, "k_zcT", "k_winc", Kc, 2)):
                t_z, t_h1, t_h2, t_hfb, t_t = T(), T(), T(), T(), T()
                sy.dma("sp", zT[:, 0:Ls], dram[zname], writes=[t_z])

                def sin_layer(wm, kdim, src, t_src, dst, t_dst, bcol):
                    for n0 in range(0, Ls, 512):
                        n = min(512, Ls - n0)
                        pt, tp = cx.psum[(n0 // 512) % 2]
                        sy.op("pe", lambda e: e.matmul(pt[0:64, 0:n], wm[0:kdim, :], src[0:kdim, n0:n0 + n], start=True, stop=True),
                              reads=[t_w, t_src], writes=[tp])
                        sy.op("dve", lambda e: e.tensor_scalar(out=tA[:, 0:n], in0=pt[0:64, 0:n], scalar1=fb[:, 1:2], scalar2=fb[:, bcol:bcol + 1],
                                                               op0=ALU.mult, op1=ALU.add), reads=[tp, t_w], writes=[t_t])
                        sy.op("dve", lambda e: e.tensor_scalar(out=tB[:, 0:n], in0=tA[:, 0:n], scalar1=1.0 / (2 * PI), scalar2=MAGIC,
                                                               op0=ALU.mult, op1=ALU.add), reads=[t_t], writes=[t_t])
                        sy.op("dve", lambda e: e.tensor_scalar(out=tB[:, 0:n], in0=tB[:, 0:n], scalar1=-MAGIC, scalar2=None, op0=ALU.add),
                              reads=[t_t], writes=[t_t])
                        sy.op("dve", lambda e: e.scalar_tensor_tensor(out=tA[:, 0:n], in0=tB[:, 0:n], scalar=-2 * PI, in1=tA[:, 0:n],
                                                                      op0=ALU.mult, op1=ALU.add), reads=[t_t], writes=[t_t])
                        sy.op("dve", lambda e: e.tensor_scalar(out=tA[:, 0:n], in0=tA[:, 0:n], scalar1=-PI, scalar2=PI, op0=ALU.max, op1=ALU.min),
                              reads=[t_t], writes=[t_t])
                        sy.op("act", lambda e: e.activation(out=dst[:, n0:n0 + n], in_=tA[:, 0:n], func=AF.Sin), reads=[t_t], writes=[t_dst])

                sin_layer(w1, 33, zT, t_z, h1, t_h1, 3)
                sin_layer(w2, 64, h1, t_h1, h2, t_h2, 4)
                for c in range(nct):
                    wn, t_wn = win_r.next()
                    sy.dma("sp", wn[:], dram[wname][:, c, :], writes=[t_wn])
                    for half in range(2):
                        pt, tp = cx.psum[2 + half]
                        sy.op("pe", lambda e: e.matmul(pt[:], h2[:, c * 128:(c + 1) * 128], w3[:, half * 512:(half + 1) * 512], start=True, stop=True),
                              reads=[t_h2, t_w], writes=[tp])
                        sy.op("dve", lambda e: e.tensor_tensor(out=hfb[:, c, half * 512:(half + 1) * 512], in0=pt[:], in1=wn[:], op=ALU.mult),
                              reads=[tp, t_wn], writes=[t_hfb])
                sy.op("dve", lambda e: e.memset(hfb[0:1, 0, 512:1024], 0.0), reads=[t_hfb], writes=[t_hfb])
                if "dbg_h" in cx.tdram and Ls == L:
                    pass
                ngrp = 8 if Ls == L else 1
                for g in range(ngrp):
                    if Ls == L:
                        Fb, t_F = F_r.next()
                        sy.dma("sp", Fb[:], dram["k_ffwd"][g], writes=[t_F])
                        Fsrc = Fb
                    else:
                        Fsrc, t_F = Fc, t_fc
                    for j in range(4):
                        pA, tpA = cx.psum[4 + (j % 2) * 2]
                        pB, tpB = cx.psum[5 + (j % 2) * 2]
                        for (pp, tpp, c0) in ((pA, tpA, 0), (pB, tpB, 512)):
                            for c in range(nct):
                                sy.op("pe", lambda e: e.matmul(pp[:], Fsrc[:, c, j * 128:(j + 1) * 128], hfb[:, c, c0:c0 + 512],
                                                               start=(c == 0), stop=(c == nct - 1)), reads=[t_F, t_hfb], writes=[tpp])
                        tm, t_tm = tmpA_r.next()
                        sy.op("act", lambda e: e.activation(out=tm[:], in_=pA[:], func=AF.Copy), reads=[tpA], writes=[t_tm])
                        if Ls == L:
                            ti = (2 * g + j) if j < 2 else (16 + 2 * g + j - 2)
                        else:
                            ti = j
                        sy.op("dve", lambda e: e.tensor_tensor(out=Kdst[:, ti, :], in0=tm[:], in1=pB[:], op=(ALU.add if j < 2 else ALU.subtract)),
                              reads=[t_tm, tpB], writes=[t_K])
            sy.barrier()
        utm = sy.sb("utm3", [128, NT, 512], BF16, ph)
        t_utm = T()
        Y = sy.sb("Y", [128, 32, 512], BF16, ph)
        Yc = sy.sb("Yc", [128, 4, 512], BF16, ph)
        t_Y, t_Yc = T(), T()
        Fi_r = Ring(sy, ph, "Fib", 2, [128, 8, 512], BF16)
        us_r = Ring(sy, ph, "us", 4, [128, 512], F32)
        tt_r = Ring(sy, ph, "ttm", 4, [128, 512], F32)
        uw_r = Ring(sy, ph, "uw", 4, [128, 2, 512], BF16)
        yb_r = Ring(sy, ph, "yb", 2, [128, 512], BF16)
        tmp_r = Ring(sy, ph, "tmp3", 2, [128, 512], F32)

        def spec_mul(pre, tpre, pim, tpim, Kt, ire, iim, Yt, t_Yt):
            ure, t_ure = us_r.next(); uim, t_uim = us_r.next()
            sy.op("act", lambda e: e.activation(out=ure[:], in_=pre[:], func=AF.Copy), reads=[tpre], writes=[t_ure])
            sy.op("act", lambda e: e.activation(out=uim[:], in_=pim[:], func=AF.Copy), reads=[tpim], writes=[t_uim])
            t1, t_t1 = tt_r.next(); t2, t_t2 = tt_r.next(); t3, t_t3 = tt_r.next(); t4, t_t4 = tt_r.next()
            sy.op("dve", lambda e: e.tensor_tensor(out=t1[:], in0=ure[:], in1=Kt[:, ire, :], op=ALU.mult), reads=[t_ure, t_K], writes=[t_t1])
            sy.op("dve", lambda e: e.tensor_tensor(out=t2[:], in0=uim[:], in1=Kt[:, iim, :], op=ALU.mult), reads=[t_uim, t_K], writes=[t_t2])
            sy.op("dve", lambda e: e.tensor_tensor(out=Yt[:, ire, :], in0=t1[:], in1=t2[:], op=ALU.subtract), reads=[t_t1, t_t2], writes=[t_Yt])
            sy.op("pool", lambda e: e.tensor_tensor(out=t3[:], in0=ure[:], in1=Kt[:, iim, :], op=ALU.mult), reads=[t_ure, t_K], writes=[t_t3])
            sy.op("pool", lambda e: e.tensor_tensor(out=t4[:], in0=uim[:], in1=Kt[:, ire, :], op=ALU.mult), reads=[t_uim, t_K], writes=[t_t4])
            sy.op("pool", lambda e: e.tensor_tensor(out=Yt[:, iim, :], in0=t3[:], in1=t4[:], op=ALU.add), reads=[t_t3, t_t4], writes=[t_Yt])

        def combine(b, ct, pacc, tpacc, tok0, n):
            uw, t_uw = uw_r.next()
            sy.dma("sp", uw[:, 0, 0:n], dram["u_fm"][b, ct * 128:(ct + 1) * 128, tok0:tok0 + n], reads=[cx.tdram["u_fm"]], writes=[t_uw])
            sy.dma("sp", uw[:, 1, 0:n], dram["w_fm"][b, ct * 128:(ct + 1) * 128, tok0:tok0 + n], reads=[cx.tdram["w_fm"]], writes=[t_uw])
            tm, t_tm = tmp_r.next()
            sy.op("dve", lambda e: e.scalar_tensor_tensor(out=tm[:, 0:n], in0=uw[:, 0, 0:n], scalar=hbias[:, ct:ct + 1], in1=pacc[:, 0:n],
                                                          op0=ALU.mult, op1=ALU.add), reads=[t_uw, tpacc, t_fc], writes=[t_tm])
            yb, t_yb = yb_r.next()
            sy.op("pool", lambda e: e.tensor_tensor(out=yb[:, 0:n], in0=tm[:, 0:n], in1=uw[:, 1, 0:n], op=ALU.mult), reads=[t_tm, t_uw], writes=[t_yb])
            sy.dma("pool", dram["yT"][b, ct * 128:(ct + 1) * 128, tok0:tok0 + n], yb[:, 0:n], reads=[t_yb], writes=[cx.tdram["yT"]])

        for b in range(NB):
            sy.dma("sp", utm[:], dram["u_tm"][b].rearrange("(t p) c -> p t c", p=128), reads=[cx.tdram["u_tm"]], writes=[t_utm])
            for j in range(4):
                pt, tp = cx.psum[j]
                for c in range(2):
                    sy.op("pe", lambda e: e.matmul(pt[:], Fc[:, c, j * 128:(j + 1) * 128], utm[:, c, :], start=(c == 0), stop=(c == 1)),
                          reads=[t_fc, t_utm], writes=[tp])
            for jj in range(2):
                spec_mul(cx.psum[jj][0], cx.psum[jj][1], cx.psum[2 + jj][0], cx.psum[2 + jj][1], Kc, jj, 2 + jj, Yc, t_Yc)
            for ct in range(4):
                pt, tp = cx.psum[4 + ct]
                for f in range(4):
                    sy.op("pe", lambda e: e.matmul(pt[:, 0:256], Yc[:, f, ct * 128:(ct + 1) * 128], Fic[:, f, :], start=(f == 0), stop=(f == 3)),
                          reads=[t_Yc, t_fc], writes=[tp])
                combine(b, ct, pt, tp, 0, 256)
            for g in range(8):
                Fb, t_F = F_r.next()
                sy.dma("sp", Fb[:], dram["k_ffwd"][g], writes=[t_F])
                for j in range(4):
                    pt, tp = cx.psum[j]
                    for c in range(16):
                        sy.op("pe", lambda e: e.matmul(pt[:], Fb[:, c, j * 128:(j + 1) * 128], utm[:, 2 + c, :], start=(c == 0), stop=(c == 15)),
                              reads=[t_F, t_utm], writes=[tp])
                for jj in range(2):
                    spec_mul(cx.psum[jj][0], cx.psum[jj][1], cx.psum[2 + jj][0], cx.psum[2 + jj][1], Kx, 2 * g + jj, 16 + 2 * g + jj, Y, t_Y)
            for ng in range(4):
                for fq in range(4):
                    Fi, t_Fi = Fi_r.next()
                    sy.dma("sp", Fi[:], dram["k_finv"][ng, fq], writes=[t_Fi])
                    for ct in range(4):
                        pt, tp = cx.psum[4 + ct]
                        for f in range(8):
                            sy.op("pe", lambda e: e.matmul(pt[:], Y[:, fq * 8 + f, ct * 128:(ct + 1) * 128], Fi[:, f, :],
                                                           start=(fq == 0 and f == 0), stop=(fq == 3 and f == 7)), reads=[t_Y, t_Fi], writes=[tp])
                for ct in range(4):
                    combine(b, ct, cx.psum[4 + ct][0], cx.psum[4 + ct][1], 256 + ng * 512, 512)


def gate_rows(cx, l, b, gbc, t_g):
    cx.sy.dma("sp", gbc[:], cx.dram["modrows"][l, b, 2 * D:3 * D].partition_broadcast(128), reads=[cx.tdram["modrows"]], writes=[t_g])


def phase4(cx):
    sy, nc, NB, dram = cx.sy, cx.nc, cx.NB, cx.dram
    with contextlib.ExitStack() as ph:
        ow = sy.sb("ow", [128, 8, D], BF16, ph)
        t_ow = T()
        load_w_bf16(cx, ow, dram["er_out_w"], D, t_ow)
        gc = sy.sb("gc", [128, D], F32, ph)
        t_gc = T()
        gate_rows(cx, 0, NB, gc, t_gc)
        gx_r = Ring(sy, ph, "gx", 2, [128, D], F32)
        yTs_r = Ring(sy, ph, "yTs", 2, [128, 8, TOK], BF16)
        xt_r = Ring(sy, ph, "xt4", 3, [128, D], F32)
        tm_r = Ring(sy, ph, "tm4", 3, [128, D], F32)
        bank = [0]
        for b in range(NB):
            gx, t_gx = gx_r.next()
            gate_rows(cx, 0, b, gx, t_gx)
            yTs, t_y = yTs_r.next()
            for k in range(8):
                sy.dma("sp", yTs[:, k, :], dram["yT"][b, k * 128:(k + 1) * 128, :], reads=[cx.tdram["yT"]], writes=[t_y])
            for tt in range(NT):
                src, bsel, tsrc = tok_src(cx, b, tt, 0)
                xt, t_x = xt_r.next()
                sy.dma("sp", xt[:], src, writes=[t_x])
                tm, t_tm = tm_r.next()
                g_, t_g_ = (gc, t_gc) if tt < 2 else (gx, t_gx)
                for half in range(2):
                    pt, tp = cx.psum[bank[0] % 6]
                    bank[0] += 1
                    hs = slice(half * 512, (half + 1) * 512)
                    for k in range(8):
                        sy.op("pe", lambda e: e.matmul(pt[:], yTs[:, k, tt * 128:(tt + 1) * 128], ow[:, k, hs], start=(k == 0), stop=(k == 7)),
                              reads=[t_y, t_ow], writes=[tp])
                    sy.op("dve", lambda e: e.tensor_tensor(out=tm[:, hs], in0=pt[:], in1=g_[:, hs], op=ALU.mult), reads=[tp, t_g_], writes=[t_tm])
                sy.op("pool", lambda e: e.tensor_tensor(out=tm[:], in0=tm[:], in1=xt[:], op=ALU.add), reads=[t_tm, t_x], writes=[t_tm])
                sy.dma("pool", dram["x1"][b, tt * 128:(tt + 1) * 128, :], tm[:], reads=[t_tm], writes=[cx.tdram["x1"]])


def head_norm_rope(cx, src, t_src, nh, gbc, t_c, tabs, pos_tile, dst, t_dst, tmp):
    sy = cx.sy
    sq, t_sq, st, t_st, rc, rd, t_r = tmp
    W = nh * 128
    s3 = src[:, 0:W].rearrange("p (h d) -> p h d", d=128)
    sy.op("pool", lambda e: e.tensor_tensor(out=sq[:, 0:W], in0=src[:, 0:W], in1=src[:, 0:W], op=ALU.mult), reads=[t_src], writes=[t_sq])
    yield
    sy.op("dve", lambda e: e.tensor_reduce(out=st[:, 0:nh], in_=sq[:, 0:W].rearrange("p (h d) -> p h d", d=128), axis=AX.X, op=ALU.add),
          reads=[t_sq], writes=[t_st])
    yield
    sy.op("act", lambda e: e.activation(out=st[:, 8:8 + nh], in_=st[:, 0:nh], func=AF.Sqrt, scale=1.0 / 128, bias=cx.eps_c[:, 0:1]),
          reads=[t_st, cx.t_eps], writes=[t_st])
    yield
    sy.op("dve", lambda e: e.reciprocal(out=st[:, 16:16 + nh], in_=st[:, 8:8 + nh]), reads=[t_st], writes=[t_st])
    yield
    sy.op("dve", lambda e: e.tensor_tensor(out=s3, in0=s3, in1=st[:, 16:16 + nh].unsqueeze(2).to_broadcast([128, nh, 128]), op=ALU.mult),
          reads=[t_src, t_st], writes=[t_src])
    yield
    if pos_tile is None:
        sy.op("pool", lambda e: e.tensor_tensor(out=dst[:, 0:W].rearrange("p (h d) -> p h d", d=128), in0=s3,
                                                in1=gbc[:].unsqueeze(1).to_broadcast([128, nh, 128]), op=ALU.mult), reads=[t_src, t_c], writes=[t_dst])
        yield
        return
    sy.op("pool", lambda e: e.tensor_tensor(out=s3, in0=s3, in1=gbc[:].unsqueeze(1).to_broadcast([128, nh, 128]), op=ALU.mult),
          reads=[t_src, t_c], writes=[t_src])
    yield
    rcos, rsin = tabs
    s4 = src[:, 0:W].rearrange("p (h m two) -> p h m two", two=2, m=64)
    d4 = dst[:, 0:W].rearrange("p (h m two) -> p h m two", two=2, m=64)
    x0, x1 = s4[:, :, :, 0], s4[:, :, :, 1]
    cb = rcos[:, pos_tile, :].unsqueeze(1).to_broadcast([128, nh, 64])
    sb_ = rsin[:, pos_tile, :].unsqueeze(1).to_broadcast([128, nh, 64])
    ra = sq[:, 0:nh * 64].rearrange("p (h m) -> p h m", m=64)
    rb = sq[:, nh * 64:2 * nh * 64].rearrange("p (h m) -> p h m", m=64)
    sy.op("dve", lambda e: e.tensor_tensor(out=ra, in0=x0, in1=cb, op=ALU.mult), reads=[t_src, t_c, t_st], writes=[t_sq])
    sy.op("pool", lambda e: e.tensor_tensor(out=rc[:, 0:nh, :], in0=x0, in1=sb_, op=ALU.mult), reads=[t_src, t_c], writes=[t_r[0]])
    yield
    sy.op("dve", lambda e: e.tensor_tensor(out=rb, in0=x1, in1=sb_, op=ALU.mult), reads=[t_src, t_c], writes=[t_sq])
    sy.op("pool", lambda e: e.tensor_tensor(out=rd[:, 0:nh, :], in0=x1, in1=cb, op=ALU.mult), reads=[t_src, t_c], writes=[t_r[1]])
    yield
    sy.op("dve", lambda e: e.tensor_tensor(out=d4[:, :, :, 0], in0=ra, in1=rb, op=ALU.subtract), reads=[t_sq], writes=[t_dst])
    sy.op("pool", lambda e: e.tensor_tensor(out=d4[:, :, :, 1], in0=rc[:, 0:nh, :], in1=rd[:, 0:nh, :], op=ALU.add),
          reads=[t_r[0], t_r[1]], writes=[t_dst])
    yield


def layer1(cx):
    l1_prep(cx)
    cx.sy.barrier()
    l1_attn(cx)


def l1_prep(cx):
    sy, nc, NB, dram = cx.sy, cx.nc, cx.NB, cx.dram
    with contextlib.ExitStack() as ph:
        mk_eps(cx, ph)
        if getattr(cx, "pre_ainw", None) is not None:
            inw, t_inw = cx.pre_ainw
        else:
            inw = sy.sb("ainw", [128, 8, 2560], BF16, ph)
            t_inw = T()
            load_w_bf16(cx, inw, dram["at_in_w"], 2560, t_inw)
        rcos = sy.sb("rcos", [128, 16, 64], F32, ph); rsin = sy.sb("rsin", [128, 16, 64], F32, ph)
        gq = sy.sb("gq", [128, 128], F32, ph); gk = sy.sb("gk", [128, 128], F32, ph)
        t_c = T()
        sy.dma("sp", rcos[:], dram["k_rcos"], writes=[t_c]); sy.dma("sp", rsin[:], dram["k_rsin"], writes=[t_c])
        sy.dma("sp", gq[:], dram["at_q_norm"].partition_broadcast(128), writes=[t_c])
        sy.dma("sp", gk[:], dram["at_k_norm"].partition_broadcast(128), writes=[t_c])
        xt_ring = Ring(sy, ph, "xtl", 4, [128, D], F32)
        scr = Ring(sy, ph, "scrl", 4, [128, D], BF16)
        small = Ring(sy, ph, "sml", 8, [128, 4], F32)
        hxa_r = Ring(sy, ph, "hxa", 2, [128, 8, 128], BF16)
        hxq_r = Ring(sy, ph, "hxq", 2, [128, 8, 512], BF16)
        kvf_r = Ring(sy, ph, "kvf", 4, [128, 256], F32)
        qf_r = Ring(sy, ph, "qf32", 4, [128, D], F32)
        qr_r = Ring(sy, ph, "qr", 4, [128, D], BF16)
        kr_r = Ring(sy, ph, "kr", 4, [128, 256], BF16)
        sq_r = Ring(sy, ph, "sqt", 4, [128, D], F32)
        st_r = Ring(sy, ph, "lst", 8, [128, 24], F32)
        rt_r = [Ring(sy, ph, f"rt{i}", 4, [128, 8, 64], F32) for i in range(2)]
        sqk_r = Ring(sy, ph, "sqk", 4, [128, 256], F32)
        stk_r = Ring(sy, ph, "lstk", 8, [128, 24], F32)
        rtk_r = [Ring(sy, ph, f"rtk{i}", 4, [128, 2, 64], F32) for i in range(2)]
        kTst_r = Ring(sy, ph, "kTst", 1, [128, 2, TOK], BF16)
        vst_r = Ring(sy, ph, "vst", 1, [128, NT, 256], BF16)
        qTst_r = Ring(sy, ph, "qTst", 2, [128, 8, 512], BF16)
        gst_r = Ring(sy, ph, "gst", 3, [128, 512], BF16)
        pM = [cx.psum[i] for i in range(8)]
        cnt = {"m": 0}

        def nb():
            r = pM[cnt["m"] % len(pM)]
            cnt["m"] += 1
            return r

        def tmpset():
            sq, t_sq = sq_r.next()
            st, t_st = st_r.next()
            rs = [r.next() for r in rt_r]
            return (sq, t_sq, st, t_st, rs[0][0], rs[1][0], [r[1] for r in rs])

        def tmpset_k():
            sq, t_sq = sqk_r.next()
            st, t_st = stk_r.next()
            rs = [r.next() for r in rtk_r]
            return (sq, t_sq, st, t_st, rs[0][0], rs[1][0], [r[1] for r in rs])

        gate_gens = []
        for b in range(NB):
            kTst, t_kTst = kTst_r.next()
            vst, t_vst = vst_r.next()

            def kv_tile(tt, hx, t_hx, c0):
                pt, tp = nb()
                for k in range(8):
                    sy.op("pe", lambda e: e.matmul(pt[:], hx[:, k, c0:c0 + 128], inw[:, k, 1024:1536], start=(k == 0), stop=(k == 7)),
                          reads=[t_hx, t_inw], writes=[tp])
                kvf, t_kvf = kvf_r.next()
                sy.op("act", lambda e: e.activation(out=kvf[:], in_=pt[:, 0:256], func=AF.Copy), reads=[tp], writes=[t_kvf])
                sy.op("act", lambda e: e.activation(out=vst[:, tt, :], in_=pt[:, 256:512], func=AF.Copy), reads=[tp], writes=[t_vst])
                yield
                kr, t_kr = kr_r.next()
                yield from head_norm_rope(cx, kvf, t_kvf, 2, gk, t_c, (rcos, rsin), (tt - 2) if tt >= 2 else None, kr, t_kr, tmpset_k())
                pt2, tp2 = nb()
                ptb = pt2[:].bitcast(BF16)
                for g in range(2):
                    sy.op("pe", lambda e: e.transpose(out=ptb[:, g * 128:(g + 1) * 128], in_=kr[:, g * 128:(g + 1) * 128], identity=cx.ident_bf[:]),
                          reads=[t_kr, cx.t_const], writes=[tp2])
                sy.op("act", lambda e: e.activation(out=kTst[:, :, tt * 128:(tt + 1) * 128], in_=ptb[:, 0:256].rearrange("p (g c) -> p g c", c=128),
                                                    func=AF.Copy), reads=[tp2], writes=[t_kTst])
                yield

            def ctx_tile(tt):
                src, bsel, tsrc = tok_src(cx, b, tt, 1)
                hxa, t_hxa = hxa_r.next()
                yield from norm_tile_g(cx, 1, bsel, src, xt_ring, scr, hxa, 0, t_hxa, nb(), small, tsrc)
                yield from kv_tile(tt, hxa, t_hxa, 0)

            def x_tile(qg, i, hxq, t_hxq_i, qTst, t_qTst, flags):
                tt = 2 + qg * 4 + i
                src, bsel, tsrc = tok_src(cx, b, tt, 1)
                yield from norm_tile_g(cx, 1, bsel, src, xt_ring, scr, hxq, i * 128, t_hxq_i, nb(), small, tsrc)
                flags[i] = True
                qf, t_qf = qf_r.next()
                for half in range(2):
                    pt, tp = nb()
                    for k in range(8):
                        sy.op("pe", lambda e: e.matmul(pt[:], hxq[:, k, i * 128:(i + 1) * 128], inw[:, k, half * 512:(half + 1) * 512],
                                                       start=(k == 0), stop=(k == 7)), reads=[t_hxq_i, t_inw], writes=[tp])
                    sy.op("act", lambda e: e.activation(out=qf[:, half * 512:(half + 1) * 512], in_=pt[:], func=AF.Copy), reads=[tp], writes=[t_qf])
                    yield
                qr, t_qr = qr_r.next()
                yield from head_norm_rope(cx, qf, t_qf, 8, gq, t_c, (rcos, rsin), tt - 2, qr, t_qr, tmpset())
                pt2, tp2 = nb()
                ptb = pt2[:].bitcast(BF16)
                for h in range(8):
                    sy.op("pe", lambda e: e.transpose(out=ptb[:, h * 128:(h + 1) * 128], in_=qr[:, h * 128:(h + 1) * 128], identity=cx.ident_bf[:]),
                          reads=[t_qr, cx.t_const], writes=[tp2])
                sy.op("act", lambda e: e.activation(out=qTst[:, :, i * 128:(i + 1) * 128], in_=ptb[:].rearrange("p (h c) -> p h c", c=128), func=AF.Copy),
                      reads=[tp2], writes=[t_qTst])
                yield

            def x_tile_kv(qg, i, hxq, t_hxq_i, flags):
                while not flags[i]:
                    yield
                yield from kv_tile(2 + qg * 4 + i, hxq, t_hxq_i, i * 128)

            def gate_fm(b, qg, hxq, t_hxq):
                for ft in range(8):
                    pt, tp = nb()
                    for k in range(8):
                        sy.op("pe", lambda e: e.matmul(pt[:], inw[:, k, 1536 + ft * 128:1536 + (ft + 1) * 128], hxq[:, k, :], start=(k == 0), stop=(k == 7)),
                              reads=t_hxq + [t_inw], writes=[tp])
                    gst, t_gst = gst_r.next()
                    sy.op("act", lambda e: e.activation(out=gst[:], in_=pt[:], func=AF.Silu), reads=[tp], writes=[t_gst])
                    sy.dma("pool", dram["agT"][b, ft * 128:(ft + 1) * 128, qg * 512:(qg + 1) * 512], gst[:], reads=[t_gst], writes=[cx.tdram["agT"]])
                    yield
                    yield
                    yield

            interleave([ctx_tile(0), ctx_tile(1)] + gate_gens)
            gate_gens.clear()
            for qg in range(4):
                hxq, _ = hxq_r.next()
                t_hxq = [T() for _ in range(4)]
                qTst, t_qTst = qTst_r.next()
                flags = [False] * 4
                interleave([x_tile(qg, i, hxq, t_hxq[i], qTst, t_qTst, flags) for i in range(4)]
                           + [x_tile_kv(qg, i, hxq, t_hxq[i], flags) for i in range(4)] + gate_gens)
                gate_gens.clear()
                sy.dma("pool", dram["aqT"][b].rearrange("h d t -> d h t")[:, :, qg * 512:(qg + 1) * 512], qTst[:], reads=[t_qTst], writes=[cx.tdram["aqT"]])
                gate_gens.append(gate_fm(b, qg, hxq, t_hxq))
            sy.dma("pool", dram["akT"][b].rearrange("g d t -> d g t"), kTst[:], reads=[t_kTst], writes=[cx.tdram["akT"]])
            sy.dma("pool", dram["av"][b].rearrange("(t p) c -> p t c", p=128), vst[:], reads=[t_vst], writes=[cx.tdram["av"]])
        interleave(gate_gens)


def l1_attn(cx):
    sy, nc, NB, dram = cx.sy, cx.nc, cx.NB, cx.dram
    ks = 128.0 ** -0.5
    KSPLIT = 9
    with contextlib.ExitStack() as ph:
        mk_eps(cx, ph)
        ow = sy.sb("aow", [128, 8, D], BF16, ph)
        t_ow = T()
        load_w_bf16(cx, ow, dram["at_out_w"], D, t_ow)
        fng = sy.sb("fng", [128, D], F32, ph)
        ones = sy.sb("ones", [128, 128], BF16, ph)
        t_c = T()
        sy.dma("sp", fng[:], dram["final_norm_g"].partition_broadcast(128), writes=[t_c])
        sy.op("pool", lambda e: e.memset(ones[:], 1.0), writes=[t_c])
        mhalf = sy.sb("mhalf", [128, 1], F32, ph)
        sy.op("pool", lambda e: e.memset(mhalf[:], -0.5), writes=[t_c])
        kT_r = Ring(sy, ph, "akT", 2, [128, 2, TOK], BF16)
        v_r = Ring(sy, ph, "av", 2, [128, NT, 256], BF16)
        qT_r = Ring(sy, ph, "aqT", 2, [128, 8, 512], BF16)
        sg_r = Ring(sy, ph, "asg", 2, [128, 8, 512], BF16)
        ogT_r = Ring(sy, ph, "ogT", 2, [128, 8, 512], BF16)
        PTh = [sy.sb(f"PTh{i}", [128, NT, 512], BF16, ph) for i in range(3)]
        t_PTh = [[T() for _ in range(NT)] for _ in range(3)]
        pab_r = [Ring(sy, ph, "pabA", 6, [128, 512], BF16)]
        rden_r = Ring(sy, ph, "rden", 2, [128, 512], F32)
        pacc_r = Ring(sy, ph, "pacc", 2, [128, 512], F32)
        tmo_r = Ring(sy, ph, "tmo", 2, [128, 512], F32)
        xt_r = Ring(sy, ph, "xta", 2, [128, D], F32)
        x2_r = Ring(sy, ph, "x2", 2, [128, D], F32)
        sqj_r = Ring(sy, ph, "sqj", 1, [128, D], BF16)
        small = Ring(sy, ph, "sma", 8, [128, 4], F32)
        gx_r = Ring(sy, ph, "gxl", 2, [128, D], F32)
        pS = [cx.psum[0], cx.psum[1], cx.psum[2]]
        pO = [cx.psum[3], cx.psum[4]]
        pDs = [cx.psum[5], cx.psum[6]]
        pX = [cx.psum[7]]
        LOOK = 2
        KPOOL = NT
        cnt = {"s": 0, "o": 0, "x": 0, "h": 0, "d": 0}

        def nb(lst, key):
            r = lst[cnt[key] % len(lst)]
            cnt[key] += 1
            return r

        pend = []
        bg = []

        def bg_step(n=1):
            for _ in range(n):
                if not bg:
                    return
                try:
                    next(bg[0])
                except StopIteration:
                    bg.pop(0)

        def out_tile(b, qg, i, ogT, t_ogT, gx, t_gx):
            tt = 2 + qg * 4 + i
            xt, t_x = xt_r.next()
            sy.dma("sp", xt[:], dram["x1"][b, tt * 128:(tt + 1) * 128, :], reads=[cx.tdram["x1"]], writes=[t_x])
            x2, t_x2 = x2_r.next()
            for half in range(2):
                pt, tp = nb(pX, "x")
                hs = slice(half * 512, (half + 1) * 512)
                for k in range(8):
                    sy.op("pe", lambda e: e.matmul(pt[:], ogT[:, k, i * 128:(i + 1) * 128], ow[:, k, hs], start=(k == 0), stop=(k == 7)),
                          reads=[t_ogT, t_ow], writes=[tp])
                    if k % 2 == 1:
                        yield
                sy.op("dve", lambda e: e.tensor_tensor(out=x2[:, hs], in0=pt[:], in1=gx[:, hs], op=ALU.mult), reads=[tp, t_gx], writes=[t_x2])
                yield
            sy.op("pool", lambda e: e.tensor_tensor(out=x2[:], in0=x2[:], in1=xt[:], op=ALU.add), reads=[t_x2, t_x], writes=[t_x2])
            yield
            ss, t_ss = small.next()
            sqj, t_sqj = sqj_r.next()
            for _ in range(6):
                yield
            sy.op("act", lambda e: e.activation(out=sqj[:], in_=x2[:], func=AF.Square, accum_out=ss[:, 0:1]), reads=[t_x2], writes=[t_sqj, t_ss])
            yield
            sy.op("act", lambda e: e.activation(out=ss[:, 1:2], in_=ss[:, 0:1], func=AF.Ln, scale=1.0 / D, bias=cx.eps_c[:, 0:1]),
                  reads=[t_ss, cx.t_eps], writes=[t_ss])
            sy.op("act", lambda e: e.activation(out=ss[:, 2:3], in_=ss[:, 1:2], func=AF.Exp, scale=-0.5), reads=[t_ss], writes=[t_ss])
            yield
            sy.op("dve", lambda e: e.scalar_tensor_tensor(out=x2[:], in0=x2[:], scalar=ss[:, 2:3], in1=fng[:], op0=ALU.mult, op1=ALU.mult),
                  reads=[t_x2, t_ss, t_c], writes=[t_x2])
            yield
            sy.dma("pool", cx.out[b, (tt - 2) * 128:(tt - 1) * 128, :], x2[:], reads=[t_x2], writes=[cx.tout])
            yield

        def out_group(b, qg, ogT, t_ogT, gx, t_gx):
            for i0 in (0, 1, 2, 3):
                gens = [out_tile(b, qg, i, ogT, t_ogT, gx, t_gx) for i in (i0,)]
                while gens:
                    for g in list(gens):
                        try:
                            next(g)
                        except StopIteration:
                            gens.remove(g)
                        yield

        pre_b = {}
        pre_q = {}

        def load_batch(b):
            gx, t_gx = gx_r.next()
            gate_rows(cx, 1, b, gx, t_gx)
            kT, t_kT = kT_r.next()
            v, t_v = v_r.next()
            sy.dma("sp", kT[:], dram["akT"][b].rearrange("g d t -> d g t"), reads=[cx.tdram["akT"]], writes=[t_kT])
            sy.dma("sp", v[:], dram["av"][b].rearrange("(t p) c -> p t c", p=128), reads=[cx.tdram["av"]], writes=[t_v])
            pre_b[b] = (gx, t_gx, kT, t_kT, v, t_v)

        def load_qg(b, qg):
            qT, t_qT = qT_r.next()
            sg, t_sg = sg_r.next()
            sy.dma("sp", qT[:], dram["aqT"][b].rearrange("h d t -> d h t")[:, :, qg * 512:(qg + 1) * 512], reads=[cx.tdram["aqT"]], writes=[t_qT])
            sy.dma("sp", sg[:], dram["agT"][b].rearrange("(f p) t -> p f t", p=128)[:, :, qg * 512:(qg + 1) * 512], reads=[cx.tdram["agT"]], writes=[t_sg])
            pre_q[(b, qg)] = (qT, t_qT, sg, t_sg)

        for b in range(NB):
            if b not in pre_b:
                load_batch(b)
            gx, t_gx, kT, t_kT, v, t_v = pre_b.pop(b)
            for qg in range(4):
                if (b, qg) not in pre_q:
                    load_qg(b, qg)
                qT, t_qT, sg, t_sg = pre_q.pop((b, qg))
                while len(bg) > 1:
                    bg_step()
                ogT, t_ogT = ogT_r.next()
                steps = [(h, kt) for h in range(8) for kt in range(NT)]
                hbuf = {}

                def emit_S(s):
                    h, kt = steps[s]
                    if kt == 0:
                        hbuf[h] = cnt["h"] % 3
                        cnt["h"] += 1
                    hb = hbuf[h]
                    ps_, tps = nb(pS, "s")
                    sy.op("pe", lambda e: e.matmul(ps_[:], kT[:, h // 4, kt * 128:(kt + 1) * 128], qT[:, h, :], start=True, stop=True),
                          reads=[t_kT, t_qT], writes=[tps])
                    sy.op("act", lambda e: e.activation(out=PTh[hb][:, kt, :], in_=ps_[:], func=AF.Exp, scale=ks), reads=[tps], writes=[t_PTh[hb][kt]])

                head = {}
                hst = {}
                hpab = {}
                DCH = ()
                KD = 0

                def stage_b(hh, po, tpo, pd, tpd, ogT=ogT, t_ogT=t_ogT, sg=sg, t_sg=t_sg):
                    rden, t_rden = rden_r.next()
                    sy.op("dve", lambda e: e.reciprocal(out=rden[:], in_=pd[:]), reads=[tpd], writes=[t_rden])
                    tmo, t_tmo = tmo_r.next()
                    sy.op("dve", lambda e: e.tensor_tensor(out=tmo[:], in0=po[:], in1=rden[:], op=ALU.mult), reads=[tpo, t_rden], writes=[t_tmo])
                    sy.op("pool", lambda e: e.tensor_tensor(out=ogT[:, hh, :], in0=tmo[:], in1=sg[:, hh, :], op=ALU.mult), reads=[t_tmo, t_sg], writes=[t_ogT])

                for s0 in range(LOOK):
                    emit_S(s0)
                for s, (h, kt) in enumerate(steps):
                    if s + LOOK < len(steps):
                        emit_S(s + LOOK)
                    if kt == 0:
                        head[h] = nb(pO, "o")
                        hst[h] = nb(pDs, "d")
                    if kt == 4 and pend:
                        fn, after = pend.pop(0)
                        fn()
                        if after:
                            after()
                    po, tpo = head[h]
                    pd, tpd = hst[h]
                    hb = hbuf[h]
                    g = h // 4
                    sy.op("pe", lambda e: e.matmul(po[:], v[:, kt, g * 128:(g + 1) * 128], PTh[hb][:, kt, :], start=(kt == 0), stop=(kt == NT - 1)),
                          reads=[t_v, t_PTh[hb][kt]], writes=[tpo])
                    for (k0, k1) in DCH:
                        if kt == k1 - 1:
                            pab, t_pab = pab_r[0].next()
                            with nc.allow_low_precision(reason="bf16 matmul operand; reduction itself runs in fp32"):
                                sy.op("dve", lambda e: e.tensor_reduce(out=pab[:], in_=PTh[hb][:, k0:k1, :].rearrange("p k q -> p q k"), axis=AX.X, op=ALU.add),
                                      reads=t_PTh[hb][k0:k1], writes=[t_pab])
                            hpab.setdefault(h, []).append((pab, t_pab))
                    if kt >= KD:
                        sy.op("pe", lambda e: e.matmul(pd[:], ones[:], PTh[hb][:, kt, :], start=(kt == KD), stop=(kt == NT - 1)),
                              reads=[t_PTh[hb][kt], t_c], writes=[tpd])
                    if kt == NT - 2:
                        for (pab, t_pab) in hpab.pop(h, []):
                            sy.op("pe", lambda e: e.matmul(pd[:], ones[:], pab[:], start=False, stop=False), reads=[t_pab, t_c], writes=[tpd])
                    if kt == NT - 1:
                        after = None
                        if h == 7:
                            after = (lambda b=b, qg=qg, ogT=ogT, t_ogT=t_ogT, gx=gx, t_gx=t_gx: bg.append(out_group(b, qg, ogT, t_ogT, gx, t_gx)))
                        pend.append([(lambda hh=h, po=po, tpo=tpo, pd=pd, tpd=tpd, fb=stage_b: fb(hh, po, tpo, pd, tpd)), after])
                    if s == 40:
                        if qg < 3:
                            load_qg(b, qg + 1)
                        elif b + 1 < NB:
                            load_batch(b + 1)
                            load_qg(b + 1, 0)
                    bg_step()
        while pend:
            fn, after = pend.pop(0)
            fn()
            if after:
                after()
        while bg:
            bg_step()


def make_in_maps(inp, NB, ncores):
    hc = host_consts(NB)
    f = lambda a: np.ascontiguousarray(np.asarray(a, dtype=np.float32))
    shared = {
        "norm_g": f(inp["norm_g"]), "final_norm_g": f(inp["final_norm_g"]), "ada_w": f(inp["ada_w"]), "ada_b": f(inp["ada_b"]),
        "er_in_w": f(inp["er_in_w"][0]), "er_out_w": f(inp["er_out_w"][0]),
        "convw": f(np.asarray(inp["hy_conv_w"][0]).reshape(3, 12, 128).transpose(2, 1, 0)),
        "convb": f(np.asarray(inp["hy_conv_b"][0]).reshape(12, 128).T),
        "hy_f_w1": f(inp["hy_f_w1"][0]), "hy_f_b1": f(np.asarray(inp["hy_f_b1"][0]).reshape(64, 1)),
        "hy_f_freq": f(np.asarray(inp["hy_f_freq"][0]).reshape(64, 1)), "hy_f_w2": f(inp["hy_f_w2"][0]),
        "hy_f_b2": f(np.asarray(inp["hy_f_b2"][0]).reshape(64, 1)), "hy_f_w3": f(inp["hy_f_w3"][0]),
        "hy_bias": f(np.asarray(inp["hy_bias"][0]).reshape(4, 128).T),
        "retdec": f(np.concatenate([np.asarray(inp["ret_decay_f"][0]), np.asarray(inp["ret_decay_b"][0])])),
        "ret_norm_g": f(inp["ret_norm_g"][0]),
        "at_in_w": f(inp["at_in_w"][0]), "at_out_w": f(inp["at_out_w"][0]),
        "at_q_norm": f(inp["at_q_norm"][0]), "at_k_norm": f(inp["at_k_norm"][0]),
    }
    for k, v in hc.items():
        shared["k_" + k] = v
    x = np.asarray(inp["x"], dtype=np.float32)
    ctx = np.asarray(inp["ctx"], dtype=np.float32)
    c = np.asarray(inp["c"], dtype=np.float32)
    cc = np.asarray(inp["c_ctx"], dtype=np.float32)
    maps = []
    for i in range(ncores):
        sl = slice(i * NB, (i + 1) * NB)
        cfull = np.concatenate([c[sl], cc[None, :]], axis=0)
        cT = np.ascontiguousarray(cfull.reshape(NB + 1, 8, 128).transpose(2, 1, 0))
        m = dict(shared)
        m["x"] = np.ascontiguousarray(x[sl])
        m["ctx"] = np.ascontiguousarray(ctx[sl])
        m["cT"] = cT
        maps.append(m)
    return maps


_NC_CACHE = {}


def kernel(**inputs):
    NB = inputs["x"].shape[0] // NCORES
    if NB not in _NC_CACHE:
        _NC_CACHE[NB] = build(NB)
    nc = _NC_CACHE[NB]
    maps = make_in_maps(inputs, NB, NCORES)
    res = run_bass_kernel_spmd(nc, maps, core_ids=list(range(NCORES)))
    return np.concatenate([r["out"] for r in res.results], axis=0).astype(np.float32)
```
